# Optimizing a Trainium2 kernel written in Bass

```python
import math
import jax
import jax.numpy as jnp
from jax import lax
import numpy as np

D_MODEL = 2048
BATCH = 8
SEQ = 4096
DEPTH = 4

HEAD_DIM = 128
MIX_WIDTH = D_MODEL
N_HEADS_TOTAL = MIX_WIDTH // HEAD_DIM
N_HEADS_SB = N_HEADS_TOTAL // 2
N_HEADS_DIL = N_HEADS_TOTAL - N_HEADS_SB
SB_WIDTH = N_HEADS_SB * HEAD_DIM
DIL_WIDTH = N_HEADS_DIL * HEAD_DIM
IN_WIDTH = 3 * (SB_WIDTH + DIL_WIDTH)
IN_SPLITS = (SB_WIDTH, 2 * SB_WIDTH, 3 * SB_WIDTH, 3 * SB_WIDTH + DIL_WIDTH, 3 * SB_WIDTH + 2 * DIL_WIDTH)
D_FF = -(-8 * D_MODEL // (3 * 256)) * 256
PLI_DIM = 256
BLOCK = 128
DIL_GROUPS = ((128, 1), (512, 4), (2048, 16))
NUM_BUCKETS = 32
MAX_DISTANCE = 2048
RMS_EPS = 1e-6

kernel_name = "hybrid_stickbreak_dilated_sandwich_trunk"


def rms_norm(x, gain):
    xf = x.astype(jnp.float32)
    y = xf * lax.rsqrt(jnp.mean(xf * xf, axis=-1, keepdims=True) + RMS_EPS)
    return (y * gain.astype(jnp.float32)).astype(x.dtype)


def t5_bucket(dist):
    max_exact = NUM_BUCKETS // 2
    d = jnp.maximum(dist, 1).astype(jnp.float32)
    large = max_exact + (jnp.log(d / max_exact) / math.log(MAX_DISTANCE / max_exact)
                         * (NUM_BUCKETS - max_exact)).astype(jnp.int32)
    large = jnp.minimum(large, NUM_BUCKETS - 1)
    return jnp.where(dist < max_exact, dist, large)


def stick_breaking_attention(q, k, v):
    b, s, h, e = q.shape
    nblk = s // BLOCK
    scale = 1.0 / math.sqrt(e)
    kt = k.transpose(0, 2, 1, 3)
    vt = v.transpose(0, 2, 1, 3).astype(jnp.float32)
    qb = q.reshape(b, nblk, BLOCK, h, e).transpose(1, 0, 3, 2, 4)
    starts = jnp.arange(nblk, dtype=jnp.int32) * BLOCK
    key_pos = jnp.arange(s, dtype=jnp.int32)

    def one_block(args):
        qi, t0 = args
        z = jnp.einsum('bhqe,bhke->bhqk', qi, kt).astype(jnp.float32) * scale
        q_pos = t0 + jnp.arange(BLOCK, dtype=jnp.int32)
        earlier = key_pos[None, :] < q_pos[:, None]
        log_keep = jnp.where(earlier, jax.nn.log_sigmoid(-z), 0.0)
        later = lax.cumsum(log_keep, axis=3, reverse=True) - log_keep
        a = jnp.where(earlier, jnp.exp(jax.nn.log_sigmoid(z) + later), 0.0)
        return jnp.einsum('bhqk,bhke->bhqe', a, vt)

    o = lax.map(one_block, (qb, starts))
    return o.transpose(1, 0, 3, 2, 4).reshape(b, s, h, e).astype(q.dtype)


def dilated_branch(q, k, v, rel_bias, window, dilation):
    b, s, h, e = q.shape
    span = dilation * BLOCK
    s_pad = -(-s // span) * span
    nb = s_pad // span
    n_back = window // dilation
    scale = 1.0 / math.sqrt(e)

    def to_sub(t):
        t = jnp.pad(t, ((0, 0), (0, s_pad - s), (0, 0), (0, 0)))
        return t.reshape(b, nb, BLOCK, dilation, h, e).transpose(0, 3, 4, 1, 2, 5)

    def with_prev(t):
        prev = jnp.pad(t[:, :, :, :-1], ((0, 0), (0, 0), (0, 0), (1, 0), (0, 0), (0, 0)))
        return jnp.concatenate([prev, t], axis=4)

    qs = to_sub(q)
    kk = with_prev(to_sub(k))
    vv = with_prev(to_sub(v)).astype(jnp.float32)
    logits = jnp.einsum('brhnqe,brhnke->brhnqk', qs, kk).astype(jnp.float32) * scale

    qi = jnp.arange(BLOCK, dtype=jnp.int32)[:, None]
    ki = jnp.arange(2 * BLOCK, dtype=jnp.int32)[None, :]
    rel = BLOCK + qi - ki
    band = (rel >= 0) & (rel <= n_back)
    bias = rel_bias.astype(jnp.float32)[t5_bucket(jnp.maximum(rel, 0) * dilation)]
    bias = bias.transpose(2, 0, 1)[:, None]
    has_prev = (jnp.arange(nb)[:, None, None] > 0) | (ki[None] >= BLOCK)
    valid = band[None] & has_prev

    logits = jnp.where(valid, logits + bias, -jnp.inf)
    m = jnp.max(logits, axis=-1, keepdims=True)
    p = jnp.exp(logits - m)
    denom = jnp.sum(p, axis=-1, keepdims=True)
    o = jnp.einsum('brhnqk,brhnke->brhnqe', p, vv) / denom
    lse = m + jnp.log(denom)

    def from_sub(t):
        return t.transpose(0, 3, 4, 1, 2, 5).reshape(b, s_pad, h, t.shape[-1])[:, :s]

    return from_sub(o), from_sub(lse)[..., 0]


def dilated_mixture(q, k, v, rel_bias):
    outs, lses = [], []
    for window, dilation in DIL_GROUPS:
        o, l = dilated_branch(q, k, v, rel_bias, window, dilation)
        outs.append(o)
        lses.append(l)
    w = jax.nn.softmax(jnp.stack(lses, axis=0), axis=0)
    o = jnp.sum(w[..., None] * jnp.stack(outs, axis=0), axis=0)
    return o.astype(q.dtype)


def setup_inputs(seed: int = 0) -> dict:
    key = jax.random.key(seed)
    ks = jax.random.split(key, 15)

    def normal(k, shape, scale):
        return jax.random.normal(k, shape, jnp.float32) * scale

    def gain(k, shape):
        return 1.0 + normal(k, shape, 0.01)

    return {
        "x": normal(ks[0], (BATCH, SEQ, D_MODEL), 1.0),
        "p": normal(ks[1], (DEPTH, BATCH, SEQ, PLI_DIM), 1.0),
        "ln_mix_pre": gain(ks[2], (DEPTH, D_MODEL)),
        "w_in": normal(ks[3], (DEPTH, D_MODEL, IN_WIDTH), D_MODEL ** -0.5),
        "ln_head": gain(ks[4], (DEPTH, MIX_WIDTH)),
        "w_out": normal(ks[5], (DEPTH, MIX_WIDTH, D_MODEL), MIX_WIDTH ** -0.5),
        "ln_mix_post": gain(ks[6], (DEPTH, D_MODEL)),
        "rel_bias": normal(ks[7], (NUM_BUCKETS, N_HEADS_DIL), 0.5),
        "ln_ffn_pre": gain(ks[8], (DEPTH, D_MODEL)),
        "w_gate_up": normal(ks[9], (DEPTH, D_MODEL, 2 * D_FF), D_MODEL ** -0.5),
        "w_down": normal(ks[10], (DEPTH, D_FF, D_MODEL), D_FF ** -0.5),
        "ln_ffn_post": gain(ks[11], (DEPTH, D_MODEL)),
        "ln_pli": gain(ks[12], (DEPTH, D_MODEL)),
        "w_pli_gate": normal(ks[13], (DEPTH, D_MODEL, D_MODEL), D_MODEL ** -0.5),
        "w_pli_proj": normal(ks[14], (DEPTH, PLI_DIM, D_MODEL), PLI_DIM ** -0.5),
    }


def reference(x, p, ln_mix_pre, w_in, ln_head, w_out, ln_mix_post, rel_bias,
              ln_ffn_pre, w_gate_up, w_down, ln_ffn_post, ln_pli, w_pli_gate, w_pli_proj):
    b, s, _ = x.shape
    for i in range(DEPTH):
        h = rms_norm(x, ln_mix_pre[i])
        proj = h @ w_in[i]
        q_sb, k_sb, v_sb, q_dl, k_dl, v_dl = jnp.split(proj, list(IN_SPLITS), axis=-1)
        sb_shape = (b, s, N_HEADS_SB, HEAD_DIM)
        dl_shape = (b, s, N_HEADS_DIL, HEAD_DIM)
        o_sb = stick_breaking_attention(q_sb.reshape(sb_shape), k_sb.reshape(sb_shape), v_sb.reshape(sb_shape))
        o_dl = dilated_mixture(q_dl.reshape(dl_shape), k_dl.reshape(dl_shape), v_dl.reshape(dl_shape), rel_bias)
        o = jnp.concatenate([o_sb, o_dl], axis=2)
        o = rms_norm(o, ln_head[i].reshape(N_HEADS_TOTAL, HEAD_DIM)).reshape(b, s, MIX_WIDTH)
        x = x + rms_norm(o @ w_out[i], ln_mix_post[i])
        h = rms_norm(x, ln_ffn_pre[i])
        g, u = jnp.split(h @ w_gate_up[i], 2, axis=-1)
        f = (jax.nn.silu(g) * u) @ w_down[i]
        x = x + rms_norm(f, ln_ffn_post[i])
        gate = jax.nn.sigmoid(rms_norm(x, ln_pli[i]) @ w_pli_gate[i])
        x = x + gate * (p[i] @ w_pli_proj[i])
    return x
```

```python
import math
from contextlib import ExitStack

import numpy as np
import concourse.bass as bass
import concourse.mybir as mybir
from concourse.bass_utils import run_bass_kernel_spmd

F32 = mybir.dt.float32
BF16 = mybir.dt.bfloat16
AF = mybir.ActivationFunctionType
ALU = mybir.AluOpType

RMS_EPS = 1e-6
NEG_BIG = -30000.0


class Res:
    __slots__ = ("name", "writer", "readers", "dsem", "dcount", "dbase")

    def __init__(self, name):
        self.name = name
        self.writer = None
        self.readers = []
        self.dsem = None
        self.dcount = 0
        self.dbase = 0


class _Eng:
    def __init__(self, name, sem):
        self.name = name
        self.sem = sem
        self.count = 0
        self.waited = {}
        self.ops = []


class Sched:
    ENGS = ("pe", "act", "dve", "pool", "sp")

    def __init__(self, nc, pool, tag):
        self.nc = nc
        self.pool = pool
        self.tag = tag
        self.eng = {}
        for e in self.ENGS:
            eo = _Eng(e, pool.eng_sems[e])
            eo.count = pool.eng_counts[e]
            self.eng[e] = eo
        self.dma_res = []

    def _deps(self, ename, reads, writes):
        deps = []
        for r in reads:
            if r.writer is not None:
                deps.append(r.writer)
        for w in writes:
            if w.writer is not None:
                deps.append(w.writer)
            deps.extend(w.readers)
        e = self.eng[ename]
        best = {}
        for (sem, val, src) in deps:
            if src == "pe" and ename == "pe":
                continue
            k = id(sem)
            if e.waited.get(k, 0) >= val:
                continue
            if k not in best or best[k][1] < val:
                best[k] = (sem, val)
        for k, (sem, val) in best.items():
            e.waited[k] = val
        return list(best.values())

    def _record(self, tok, reads, writes):
        for r in reads:
            r.readers.append(tok)
        for w in writes:
            w.writer = tok
            w.readers = []

    def op(self, ename, fn, reads=(), writes=(), signal=True):
        e = self.eng[ename]
        waits = self._deps(ename, reads, writes)
        tok = (e.sem, e.count + 1, ename)
        if signal:
            e.count += 1
        e.ops.append((waits, fn, (e.sem, 1) if signal else None))
        self._record(tok, reads, writes)
        return tok

    def dma(self, qname, out_ap, in_ap, reads=(), writes=(), sres=None):
        e = self.eng[qname]
        waits = self._deps(qname, reads, writes)
        if sres is None:
            sres = writes[0] if writes else reads[0]
        if sres.dsem is None:
            sres.dsem, sres.dbase = self.pool.get()
            self.dma_res.append(sres)
        sres.dcount += 1
        tok = (sres.dsem, sres.dbase + 16 * sres.dcount, "dma")

        def fn(eng, out_ap=out_ap, in_ap=in_ap):
            return eng.dma_start(out=out_ap, in_=in_ap)

        e.ops.append((waits, fn, (sres.dsem, 16)))
        self._record(tok, reads, writes)
        return tok

    def mm(self, out, lhsT, rhs, start, stop, reads, writes, signal=None):
        if signal is None:
            signal = stop
        return self.op("pe", lambda e: e.matmul(out, lhsT, rhs, start=start, stop=stop),
                       reads=reads, writes=writes, signal=signal)

    def act(self, out, in_, func, reads, writes, scale=1.0, bias=0.0):
        return self.op("act", lambda e: e.activation(out=out, in_=in_, func=func, bias=bias, scale=scale),
                       reads=reads, writes=writes)

    def tt(self, eng, out, in0, in1, op, reads, writes):
        return self.op(eng, lambda e: e.tensor_tensor(out, in0, in1, op), reads=reads, writes=writes)

    def copy(self, eng, out, in_, reads, writes, scale=None):
        if eng == "act":
            return self.act(out, in_, AF.Copy, reads, writes, scale=(1.0 if scale is None else scale))
        if scale is None:
            return self.op(eng, lambda e: e.tensor_copy(out, in_), reads=reads, writes=writes)
        return self.op(eng, lambda e: e.tensor_scalar_mul(out, in_, scale), reads=reads, writes=writes)

    def emit(self):
        nc = self.nc
        finals = [(r.dsem, r.dbase + 16 * r.dcount) for r in self.dma_res]
        engs = self.eng

        def run(engobj, e):
            for waits, fn, inc in e.ops:
                for sem, val in waits:
                    engobj.wait_ge(sem, val)
                ins = fn(engobj)
                if inc is not None:
                    ins.then_inc(inc[0], inc[1])

        with nc.Block() as block:
            @block.tensor
            def _(t):
                run(t, engs["pe"])

            @block.scalar
            def _(s):
                run(s, engs["act"])

            @block.vector
            def _(v):
                run(v, engs["dve"])

            @block.gpsimd
            def _(g):
                run(g, engs["pool"])

            @block.sync
            def _(s):
                run(s, engs["sp"])
                for sem, val in finals:
                    s.wait_ge(sem, val)

        for e in self.ENGS:
            self.pool.eng_counts[e] = engs[e].count
        for r in self.dma_res:
            self.pool.put(r.dsem, r.dbase + 16 * r.dcount)


class SemPool:
    def __init__(self, nc, stack):
        self.nc = nc
        self.stack = stack
        self.eng_sems = {e: stack.enter_context(nc.semaphore(f"eng_{e}")) for e in Sched.ENGS}
        self.eng_counts = {e: 0 for e in Sched.ENGS}
        self.free = []
        self.n = 0

    def get(self):
        if self.free:
            return self.free.pop()
        sem = self.stack.enter_context(self.nc.semaphore(f"dsem_{self.n}"))
        self.n += 1
        return sem, 0

    def put(self, sem, value):
        self.free.append((sem, value))


class Cfg:
    def __init__(self, D=2048, S=4096, DEPTH=4, HS=8, HD=8, DFF=5632, PLI=256, debug=False,
                 stages=None):
        self.D, self.S, self.DEPTH, self.HS, self.HD, self.DFF, self.PLI = D, S, DEPTH, HS, HD, DFF, PLI
        self.T = 512
        self.KC = D // 128
        self.NH = HS + HD
        self.MIXW = self.NH * 128
        assert self.MIXW == D
        self.INW = 3 * self.MIXW
        self.FC = DFF // 128
        self.NT = S // self.T
        self.NBLK = S // 128
        self.NBI = self.INW // 512
        self.NBO = D // 512
        self.NBG = self.FC // 2
        self.KG = 11
        self.NKG = self.FC // self.KG
        assert self.NKG * self.KG == self.FC
        self.NCG = D // 512
        self.NBS = HS * 128 // 512
        self.NBD = HD * 128 // 512
        self.PC = PLI // 128
        self.NG = 5 * self.KC + self.NH
        self.debug = debug
        self.overlap_cvt = False
        self.stages = stages
        self.SCALE = 1.0 / math.sqrt(128.0)
        self.DIL = ((128, 1), (512, 4), (2048, 16))


def t5_bucket_np(dist):
    num_buckets, max_distance = 32, 2048
    max_exact = num_buckets // 2
    dist = np.asarray(dist, dtype=np.int32)
    d = np.maximum(dist, 1).astype(np.float32)
    large = max_exact + (np.log(d / np.float32(max_exact)) / np.float32(math.log(max_distance / max_exact))
                         * np.float32(num_buckets - max_exact)).astype(np.int32)
    large = np.minimum(large, num_buckets - 1)
    return np.where(dist < max_exact, dist, large)


def make_consts(cfg):
    jp = np.arange(128)[:, None]
    j = np.arange(128)[None, :]
    tri_neg = np.where(jp >= j, -1.0, 0.0).astype(np.float32)
    masks = np.zeros((128, 4, 512), np.float32)
    t = np.arange(512)[None, :]
    for r in range(4):
        masks[:, r, :] = ((r * 128 + np.arange(128)[:, None]) < t).astype(np.float32)
    J = np.zeros((128, 128), np.float32)
    J[np.arange(128), 127 - np.arange(128)] = 1.0
    cst = np.concatenate([tri_neg, masks.reshape(128, 2048), J], axis=1)
    oh = np.zeros((33, 3, 384), np.float32)
    for di, (window, dil) in enumerate(cfg.DIL):
        n_back = window // dil
        for u in range(384):
            delta = u - 127
            if 0 <= delta <= n_back:
                b = int(t5_bucket_np(np.array([delta * dil]))[0])
                oh[b, di, u] = 1.0
            else:
                oh[32, di, u] = 1.0
    return cst, oh.reshape(33, 3 * 384)


def build_program(cfg):
    nc = bass.Bass("TRN2", target_bir_lowering=False)
    c = cfg
    D, S, T, KC, NH, HS, HD, FC = c.D, c.S, c.T, c.KC, c.NH, c.HS, c.HD, c.FC
    DEPTH = c.DEPTH

    def din(name, shape, dt=F32):
        return nc.dram_tensor(name, list(shape), dt, kind="ExternalInput").ap()

    def dtmp(name, shape, dt):
        kind = "ExternalOutput" if c.debug else "Internal"
        return nc.dram_tensor(name, list(shape), dt, kind=kind).ap()

    xT_in = din("xT", [D, S])
    pT_in = din("pT", [DEPTH, c.PLI, S])
    w_in = din("w_in", [DEPTH, D, c.INW])
    w_out = din("w_out", [DEPTH, c.MIXW, D])
    w_gu = din("w_gate_up", [DEPTH, D, 2 * c.DFF])
    w_dn = din("w_down", [DEPTH, c.DFF, D])
    w_pg = din("w_pli_gate", [DEPTH, D, D])
    w_pp = din("w_pli_proj", [DEPTH, c.PLI, D])
    gains_in = din("gains", [128, DEPTH * c.NG])
    relb_in = din("rel_bias", [32, HD])
    cst_in = din("cst", [128, 128 + 2048 + 128])
    oh_in = din("oh", [33, 3 * 384])
    yT_out = nc.dram_tensor("yT", [D, S], F32, kind="ExternalOutput").ap()

    Wb_in = dtmp("Wb_in", [DEPTH, c.NBI, 128, KC, 512], BF16)
    Wb_out = dtmp("Wb_out", [DEPTH, c.NBO, 128, NH, 512], BF16)
    Wb_gu = dtmp("Wb_gu", [DEPTH, c.NBG, 128, KC, 512], BF16)
    Wb_dn = dtmp("Wb_dn", [DEPTH, c.NCG, c.NKG, 128, c.KG, 512], BF16)
    Wb_pg = dtmp("Wb_pg", [DEPTH, c.NBO, 128, KC, 512], BF16)
    Wb_pp = dtmp("Wb_pp", [DEPTH, c.NBO, 128, c.PC, 512], BF16)
    xr = dtmp("xr", [D, S], F32)
    qT_sb = dtmp("qT_sb", [HS, 128, S], BF16)
    kT_sb = dtmp("kT_sb", [HS, 128, S], BF16)
    v_sb = dtmp("v_sb", [S, HS * 128], BF16)
    qT_dl = dtmp("qT_dl", [HD, 128, S], BF16)
    kT_dl = dtmp("kT_dl", [HD, 128, S], BF16)
    v_dl = dtmp("v_dl", [S, HD * 128], BF16)
    oT = dtmp("oT", [c.MIXW, S], BF16)
    gp = dtmp("gp", [HD, 3, 384], F32)
    Gmat = dtmp("Gmat", [HD, 128, 3, 256], F32)

    uid = [0]

    def want(stage):
        return c.stages is None or stage in c.stages

    def emit_convert(Sx, l, dres, throttle=0):
        dummies = [Res(f"cvthr{i}") for i in range(max(throttle, 1))]
        cnt = [0]

        def cdma(out_ap, in_ap, sres):
            w = [dummies[cnt[0] % len(dummies)]] if throttle else []
            cnt[0] += 1
            Sx.dma("pool", out_ap, in_ap, writes=w, sres=sres)

        for b in range(c.NBI):
            cdma(Wb_in[l, b],
                   w_in[l][:, b * 512:(b + 1) * 512].rearrange("(kc p) n -> p kc n", p=128), sres=dres)
        for b in range(c.NBO):
            cdma(Wb_out[l, b],
                   w_out[l][:, b * 512:(b + 1) * 512].rearrange("(kc p) n -> p kc n", p=128), sres=dres)
        for b in range(c.NBG):
            cdma(Wb_gu[l, b][:, :, 0:256],
                   w_gu[l][:, b * 256:(b + 1) * 256].rearrange("(kc p) n -> p kc n", p=128), sres=dres)
            cdma(Wb_gu[l, b][:, :, 256:512],
                   w_gu[l][:, c.DFF + b * 256:c.DFF + (b + 1) * 256].rearrange("(kc p) n -> p kc n", p=128),
                   sres=dres)
        for cg in range(c.NCG):
            for kg in range(c.NKG):
                cdma(Wb_dn[l, cg, kg],
                       w_dn[l][kg * c.KG * 128:(kg + 1) * c.KG * 128, cg * 512:(cg + 1) * 512]
                       .rearrange("(kc p) n -> p kc n", p=128), sres=dres)
        for b in range(c.NBO):
            cdma(Wb_pg[l, b],
                   w_pg[l][:, b * 512:(b + 1) * 512].rearrange("(kc p) n -> p kc n", p=128), sres=dres)
            cdma(Wb_pp[l, b],
                   w_pp[l][:, b * 512:(b + 1) * 512].rearrange("(kc p) n -> p kc n", p=128), sres=dres)

    def stage_convert(layers):
        with ExitStack() as st:
            Sx = Sched(nc, pool, f"cv{layers[0]}")
            dres = Res("cvt")
            for l in layers:
                emit_convert(Sx, l, dres)
            Sx.emit()

    class Ctx:
        pass

    def new_stage(tag):
        st = ExitStack()
        Sx = Sched(nc, pool, tag)
        uid[0] += 1
        u = uid[0]

        def sb(name, shape, dt):
            t = st.enter_context(nc.sbuf_tensor(f"{name}_{u}", list(shape), dt))
            return t, Res(name)

        def ps(name, shape, dt=F32):
            t = st.enter_context(nc.psum_tensor(f"{name}_{u}", list(shape), dt))
            return t, Res(name)

        return st, Sx, sb, ps

    def load_consts(Sx, sb):
        k = Ctx()
        k.ones, k.r_ones = sb("ones", [128, 128], BF16)
        k.gains, k.r_gains = sb("gains", [128, DEPTH * c.NG], F32)
        Sx.op("pool", lambda e: e.memset(k.ones[:], 1.0), writes=[k.r_ones])
        Sx.dma("sp", k.gains[:], gains_in, writes=[k.r_gains])
        return k

    def emit_rstd(Sx, k, src3, nchunks, n_feat, sq, r_sq, src_res, psn, r_psn, rstd, r_rstd, width):
        for g0 in range(0, nchunks, 4):
            g1 = min(nchunks, g0 + 4)
            Sx.act(sq[:, g0:g1, 0:width], src3[:, g0:g1, :], AF.Square, reads=[src_res], writes=[r_sq])
        for kc in range(nchunks):
            Sx.mm(psn[:, 0:width], k.ones[:], sq[:, kc, 0:width], start=(kc == 0), stop=(kc == nchunks - 1),
                  reads=[k.r_ones, r_sq], writes=[r_psn])
        Sx.act(rstd[:, 0:width], psn[:, 0:width], AF.Ln, reads=[r_psn], writes=[r_rstd],
               scale=1.0 / n_feat, bias=RMS_EPS)
        Sx.act(rstd[:, 0:width], rstd[:, 0:width], AF.Exp, reads=[r_rstd], writes=[r_rstd], scale=-0.5)

    def scale_chunk(Sx, eng, out, in_, gain_ap, rstd_ap, tmp, r_tmp, reads, writes):
        if eng == "dve":
            Sx.op("dve", lambda e: e.scalar_tensor_tensor(out, in_, gain_ap, rstd_ap, ALU.mult, ALU.mult),
                  reads=reads, writes=writes)
        else:
            Sx.tt("pool", tmp, in_, rstd_ap, ALU.mult, reads=reads, writes=[r_tmp])
            Sx.op("pool", lambda e: e.tensor_scalar_mul(out, tmp, gain_ap), reads=[r_tmp] + list(reads), writes=writes)

    def stage_bias():
        st, Sx, sb, ps = new_stage("gb")
        with st:
            rb, r_rb = sb("rb", [33, HD], F32)
            oh, r_oh = sb("oh", [33, 3 * 384], F32)
            gsb, r_gsb = sb("gsb", [HD, 3 * 384], F32)
            pg = [ps(f"pg{i}", [HD, 512]) for i in range(3)]
            Sx.op("pool", lambda e: e.memset(rb[32:33, :], NEG_BIG), writes=[r_rb])
            Sx.dma("sp", rb[0:32, :], relb_in, writes=[r_rb])
            Sx.dma("sp", oh[:], oh_in, writes=[r_oh])
            for di in range(3):
                Sx.mm(pg[di][0][:, 0:384], rb[:], oh[:, di * 384:(di + 1) * 384], True, True,
                      reads=[r_rb, r_oh], writes=[pg[di][1]])
                Sx.copy("dve", gsb[:, di * 384:(di + 1) * 384], pg[di][0][:, 0:384], reads=[pg[di][1]], writes=[r_gsb])
            r_gp = Res("gp_dram")
            Sx.dma("sp", gp.rearrange("h d u -> h (d u)"), gsb[:], reads=[r_gsb], writes=[r_gp], sres=r_gsb)
            Jm, r_J = sb("Jm", [128, 128], F32)
            Sx.dma("sp", Jm[:], cst_in[:, 128 + 2048:128 + 2048 + 128], writes=[r_J])
            brev = [sb(f"brev{i}", [128, 256], F32) for i in range(2)]
            gout = [sb(f"gout{i}", [128, 3, 256], F32) for i in range(2)]
            pt = [ps(f"pt{i}", [128, 512]) for i in range(2)]
            n_ = 0
            for h in range(HD):
                go_t, r_go = gout[h % 2]
                for di in range(3):
                    b_t, r_b = brev[n_ % 2]
                    p_t, r_p = pt[n_ % 2]
                    n_ += 1
                    src = bass.AP(gp.tensor, (h * 3 + di) * 384, [[1, 128], [1, 256]])
                    Sx.dma("sp", b_t[:], src, reads=[r_gp], writes=[r_b])
                    Sx.mm(p_t[:, 0:256], Jm[:], b_t[:], True, True, reads=[r_J, r_b], writes=[r_p])
                    Sx.act(go_t[:, di, :], p_t[:, 0:256], AF.Exp, reads=[r_p], writes=[r_go])
                Sx.dma("sp", Gmat[h], go_t[:], reads=[r_go])
            Sx.emit()

    class WStream:
        def __init__(self, Sx, slots, seq, queue="sp"):
            self.Sx, self.slots, self.seq, self.queue = Sx, slots, seq, queue
            self.issued = 0
            self.cur = 0

        def _issue_upto(self, j):
            n = len(self.slots)
            while self.issued <= min(j, len(self.seq) - 1):
                i = self.issued
                t, r = self.slots[i % n]
                src, nk = self.seq[i]
                self.Sx.dma(self.queue, t[:, 0:nk, :], src, writes=[r])
                self.issued += 1

        def next(self):
            i = self.cur
            self._issue_upto(i + len(self.slots) - 1)
            self.cur += 1
            return self.slots[i % len(self.slots)]

    def stage_qkv(l):
        st, Sx, sb, ps = new_stage(f"a{l}")
        with st:
            k = load_consts(Sx, sb)
            xsrc = xT_in if l == 0 else xr
            xt = [sb(f"xt{i}", [128, KC, T], F32) for i in range(2)]
            sqs = [sb(f"sq{i}", [128, KC, T], BF16) for i in range(2)]
            hTs = [sb(f"hT{i}", [128, KC, T], BF16) for i in range(2)]
            rstds = [sb(f"rstd{i}", [128, T], F32) for i in range(2)]
            wsl = [sb(f"w{i}", [128, KC, 512], BF16) for i in range(3)]
            ev = [sb(f"ev{i}", [128, 4, 512], BF16) for i in range(4)]
            psn, r_psn = ps("psn", [128, T])
            pacc = [ps(f"pa{i}", [128, 512]) for i in range(6)]
            goff = l * c.NG
            ws = WStream(Sx, wsl, [(Wb_in[l, b], KC) for tt in range(c.NT) for b in range(c.NBI)])
            evi = 0
            pi = 0

            def load_x(tt):
                x_t, r_x = xt[tt % 2]
                Sx.dma("sp", x_t[:], xsrc[:, tt * T:(tt + 1) * T].rearrange("(kc p) t -> p kc t", p=128),
                       writes=[r_x])

            def norm(tt):
                x_t, r_x = xt[tt % 2]
                sq, r_sq = sqs[tt % 2]
                hT, r_hT = hTs[tt % 2]
                rstd, r_rstd = rstds[tt % 2]
                emit_rstd(Sx, k, x_t[:], KC, D, sq, r_sq, r_x, psn, r_psn, rstd, r_rstd, T)
                for kc in range(KC):
                    scale_chunk(Sx, "dve", hT[:, kc, :], x_t[:, kc, :], k.gains[:, goff + kc:goff + kc + 1], rstd[:],
                                None, None, [r_x, k.r_gains, r_rstd], [r_hT])

            if c.overlap_cvt and l + 1 < DEPTH and want("cvt"):
                emit_convert(Sx, l + 1, Res("cvt_next"), throttle=4)
            load_x(0)
            if c.NT > 1:
                load_x(1)
            norm(0)
            for tt in range(c.NT):
                t0 = tt * T
                hT, r_hT = hTs[tt % 2]
                for b in range(c.NBI):
                    w_t, r_w = ws.next()
                    if b == 2 and tt + 1 < c.NT:
                        norm(tt + 1)
                        if tt + 2 < c.NT:
                            load_x(tt + 2)
                    if b < 3 * c.NBS:
                        role, rbk = ("q", "k", "v")[b // c.NBS], b % c.NBS
                        dq, dk, dv = qT_sb, kT_sb, v_sb
                    else:
                        bb = b - 3 * c.NBS
                        role, rbk = ("q", "k", "v")[bb // c.NBD], bb % c.NBD
                        dq, dk, dv = qT_dl, kT_dl, v_dl
                    e_t, r_e = ev[evi % 4]
                    evi += 1
                    for j in range(4):
                        p_t, r_p = pacc[pi % 6]
                        pi += 1
                        for kc in range(KC):
                            if role == "v":
                                Sx.mm(p_t[:], hT[:, kc, j * 128:(j + 1) * 128], w_t[:, kc, :],
                                      kc == 0, kc == KC - 1, reads=[r_hT, r_w], writes=[r_p])
                            else:
                                Sx.mm(p_t[:], w_t[:, kc, j * 128:(j + 1) * 128], hT[:, kc, :],
                                      kc == 0, kc == KC - 1, reads=[r_hT, r_w], writes=[r_p])
                        eng = "act" if j % 2 == 0 else "dve"
                        Sx.copy(eng, e_t[:, j, :], p_t[:], reads=[r_p], writes=[r_e],
                                scale=(c.SCALE if role == "q" else None))
                    if role == "v":
                        Sx.dma("sp", dv[t0:t0 + T, rbk * 512:(rbk + 1) * 512].rearrange("(tb p) n -> p tb n", p=128),
                               e_t[:], reads=[r_e])
                    else:
                        dst = dq if role == "q" else dk
                        Sx.dma("sp", dst[rbk * 4:(rbk + 1) * 4, :, t0:t0 + T].rearrange("h p t -> p h t"),
                               e_t[:], reads=[r_e])
            Sx.emit()

    def head_epilogue(Sx, k, l, hidx, o_ap, r_o, q0, W, sqh, r_sqh, psn, r_psn, rstd, r_rstd, on, r_on):
        emit_rstd(Sx, k, o_ap.unsqueeze(1), 1, 128, sqh, r_sqh, r_o, psn, r_psn, rstd, r_rstd, W)
        gcol = l * c.NG + 5 * KC + hidx
        Sx.op("dve", lambda e: e.scalar_tensor_tensor(on[:, 0:W], o_ap, k.gains[:, gcol:gcol + 1], rstd[:, 0:W],
                                                      ALU.mult, ALU.mult),
              reads=[r_o, k.r_gains, r_rstd], writes=[r_on])
        Sx.dma("sp", oT[hidx * 128:(hidx + 1) * 128, q0:q0 + W], on[:, 0:W], reads=[r_on])

    def stage_sb(l):
        st, Sx, sb, ps = new_stage(f"s{l}")
        with st:
            k = load_consts(Sx, sb)
            negones, r_negones = sb("negones", [128, 128], BF16)
            trineg, r_trineg = sb("trineg", [128, 128], BF16)
            masks, r_masks = sb("masks", [128, 4, 512], BF16)
            Sx.op("pool", lambda e: e.memset(negones[:], -1.0), writes=[r_negones])
            Sx.dma("pool", trineg[:], cst_in[:, 0:128], writes=[r_trineg])
            Sx.dma("pool", masks[:], cst_in[:, 128:128 + 2048].rearrange("p (r t) -> p r t", r=4), writes=[r_masks])
            NB = c.NBLK
            qk = [(sb(f"q{i}", [128, S], BF16), sb(f"k{i}", [128, S], BF16), sb(f"v{i}", [128, NB, 128], BF16))
                  for i in range(2)]
            NR = 5
            e32 = [sb(f"e32_{i}", [128, 512], F32) for i in range(2)]
            spb = [sb(f"sp{i}", [128, 512], BF16) for i in range(NR)]
            aT = [sb(f"aT{i}", [128, 512], BF16) for i in range(NR)]
            Rb = [sb(f"Rb{i}", [128, 512], F32) for i in range(2)]
            Rb16 = [sb(f"Rb16_{i}", [128, 512], BF16) for i in range(2)]
            negones32, r_negones32 = sb("negones32", [128, 128], F32)
            Sx.op("pool", lambda e: e.memset(negones32[:], -1.0), writes=[r_negones32])
            osb, r_osb = sb("osb", [128, 512], F32)
            sqh, r_sqh = sb("sqh", [128, 1, 512], BF16)
            rstd, r_rstd = sb("rstd", [128, 512], F32)
            on = [sb(f"on{i}", [128, 512], BF16) for i in range(2)]
            pA = [ps(f"pA{i}", [128, 512]) for i in range(NR)]
            pO = [ps(f"pO{i}", [128, 512]) for i in range(2)]
            psn, r_psn = ps("psn", [128, 512])
            oni = 0
            gq = 0
            def load_head(h):
                (q_t, r_q), (k_t, r_k), (v_t, r_v) = qk[h % 2]
                Sx.dma("sp", q_t[:], qT_sb[h], writes=[r_q])
                Sx.dma("sp", k_t[:], kT_sb[h], writes=[r_k])
                Sx.dma("sp", v_t[:], v_sb[:, h * 128:(h + 1) * 128].rearrange("(b p) e -> p b e", p=128),
                       writes=[r_v])

            load_head(0)
            for h in range(HS):
                (q_t, r_q), (k_t, r_k), (v_t, r_v) = qk[h % 2]
                if h + 1 < HS:
                    load_head(h + 1)
                items = []
                for Q in range(c.NT):
                    kbs = list(range(4 * Q + 3, -1, -1))
                    for i, kb in enumerate(kbs):
                        items.append((Q, i, kb, len(kbs)))
                G = len(items)

                def Zs(g):
                    Q, i, kb, n = items[g]
                    A, r_A = pA[g % NR]
                    Sx.mm(A[:], k_t[:, kb * 128:(kb + 1) * 128], q_t[:, Q * T:(Q + 1) * T], True, False,
                          reads=[r_q, r_k], writes=[r_A], signal=True)

                def Es_exp(g):
                    Q, i, kb, n = items[g]
                    A, r_A = pA[g % NR]
                    e_t, r_e = e32[g % 2]
                    Sx.act(e_t[:], A[:], AF.Exp, reads=[r_A], writes=[r_e])

                def Es(g):
                    Q, i, kb, n = items[g]
                    e_t, r_e = e32[g % 2]
                    s_t, r_s = spb[g % NR]
                    Sx.act(s_t[:], e_t[:], AF.Ln, reads=[r_e], writes=[r_s], bias=1.0)
                    if kb >= 4 * Q:
                        r = kb - 4 * Q
                        Sx.tt("pool", s_t[:], s_t[:], masks[:, r, :], ALU.mult, reads=[r_s, r_masks], writes=[r_s])

                def Cs(g):
                    Q, i, kb, n = items[g]
                    A, r_A = pA[g % NR]
                    s_t, r_s = spb[g % NR]
                    last = (i == 0)
                    Sx.mm(A[:], trineg[:], s_t[:], False, last, reads=[r_trineg, r_s], writes=[r_A], signal=last)
                    if i > 0:
                        R_t, r_Rt = Rb[g % 2]
                        sp_prev, r_sp_prev = spb[(g - 1) % NR]
                        if i == 1:
                            Sx.copy("dve", R_t[:], sp_prev[:], reads=[r_sp_prev], writes=[r_Rt])
                            Sx.mm(A[:], negones[:], sp_prev[:], False, True, reads=[r_negones, r_sp_prev],
                                  writes=[r_A], signal=True)
                        else:
                            R_p, r_Rp = Rb[(g - 1) % 2]
                            R16, r_R16 = Rb16[g % 2]
                            Sx.tt("dve", R_t[:], R_p[:], sp_prev[:], ALU.add, reads=[r_Rp, r_sp_prev], writes=[r_Rt])
                            Sx.copy("dve", R16[:], R_t[:], reads=[r_Rt], writes=[r_R16])
                            Sx.mm(A[:], negones[:], R16[:], False, True, reads=[r_negones, r_R16], writes=[r_A],
                                  signal=True)

                def Xs(g):
                    Q, i, kb, n = items[g]
                    A, r_A = pA[g % NR]
                    a_t, r_a = aT[g % NR]
                    Sx.act(a_t[:], A[:], AF.Exp, reads=[r_A], writes=[r_a])
                    if kb >= 4 * Q:
                        r = kb - 4 * Q
                        Sx.tt("pool", a_t[:], a_t[:], masks[:, r, :], ALU.mult, reads=[r_a, r_masks], writes=[r_a])

                def Vs(g):
                    nonlocal oni
                    Q, i, kb, n = items[g]
                    a_t, r_a = aT[g % NR]
                    po_t, r_po = pO[Q % 2]
                    Sx.mm(po_t[:], v_t[:, kb, :], a_t[:], i == 0, i == n - 1, reads=[r_v, r_a], writes=[r_po],
                          signal=(i == n - 1))
                    if i == n - 1:
                        Sx.copy("dve", osb[:], po_t[:], reads=[r_po], writes=[r_osb])
                        on_t, r_on = on[oni % 2]
                        oni += 1
                        head_epilogue(Sx, k, l, h, osb[:], r_osb, Q * T, T, sqh, r_sqh, psn, r_psn, rstd, r_rstd,
                                      on_t, r_on)

                for step in range(G + 3):
                    if step < G:
                        Zs(step)
                        Es_exp(step)
                    if 0 <= step - 2 < G:
                        Cs(step - 2)
                        Xs(step - 2)
                    if step < G:
                        Es(step)
                    if 0 <= step - 3 < G:
                        Vs(step - 3)
            Sx.emit()

    def stage_dl(l):
        st, Sx, sb, ps = new_stage(f"d{l}")
        with st:
            k = load_consts(Sx, sb)
            NB = c.NBLK
            qk = [(sb(f"q{i}", [128, S], BF16), sb(f"k{i}", [128, S], BF16),
                   [sb(f"v{i}_{di}", [128, NB, 128], BF16) for di in range(3)],
                   sb(f"G{i}", [128, 3, 256], F32))
                  for i in range(2)]
            acc, r_acc = sb("acc", [128, 2, S], F32)
            W32 = [sb(f"W32_{i}", [128, 4, 128], F32) for i in range(3)]
            Wb = [sb(f"Wb{i}", [128, 4, 128], BF16) for i in range(3)]
            osb, r_osb = sb("osb", [128, 512], F32)
            rden, r_rden = sb("rden", [128, 512], F32)
            sqh, r_sqh = sb("sqh", [128, 1, 512], BF16)
            rstd, r_rstd = sb("rstd", [128, 512], F32)
            on = [sb(f"on{i}", [128, 512], BF16) for i in range(2)]
            pS = [ps(f"pS{i}", [128, 4, 128]) for i in range(3)]
            pN = [ps(f"pN{i}", [128, 4, 128]) for i in range(3)]
            psn, r_psn = ps("psn", [128, 512])
            ui = 0
            oni = 0
            def load_head(h):
                (q_t, r_q), (k_t, r_k), vds, (G_t, r_G) = qk[h % 2]
                Sx.dma("sp", q_t[:], qT_dl[h], writes=[r_q])
                Sx.dma("sp", k_t[:], kT_dl[h], writes=[r_k])
                Sx.dma("sp", G_t[:], Gmat[h], writes=[r_G])
                for di, (window, d) in enumerate(c.DIL):
                    v_t, r_v = vds[di]
                    nsub = S // (128 * d)
                    if d == 1:
                        Sx.dma("sp", v_t[:], v_dl[:, h * 128:(h + 1) * 128].rearrange("(n i) e -> i n e", i=128),
                               writes=[r_v])
                    else:
                        for n_ in range(nsub):
                            Sx.dma("sp", v_t[:, n_ * d:(n_ + 1) * d, :],
                                   v_dl[n_ * 128 * d:(n_ + 1) * 128 * d, h * 128:(h + 1) * 128]
                                   .rearrange("(i r) e -> i r e", r=d), writes=[r_v])

            load_head(0)
            for h in range(HD):
                (q_t, r_q), (k_t, r_k), vds, (G_t, r_G) = qk[h % 2]
                if h + 1 < HD:
                    load_head(h + 1)
                units = [(di, d, r, n0) for di, (window, d) in enumerate(c.DIL)
                         for r in range(d) for n0 in range(0, S // (128 * d), 2)]
                NP = 3

                def sl(d, r, m):
                    a_ = m * 128 * d + r
                    return slice(a_, a_ + 127 * d + 1, d)

                def S_phase(u):
                    di, d, r, n0 = units[u]
                    S_t, r_S = pS[u % NP]
                    w32, r_w32 = W32[u % NP]
                    wb, r_wb = Wb[u % NP]
                    subs = []
                    for qi in range(2):
                        nq = n0 + qi
                        for part in range(2):
                            subs.append((qi, part, max(nq - part, 0), nq))
                    for si, (qi, part, m, nq) in enumerate(subs):
                        Sx.mm(S_t[:, 2 * qi + part, :], k_t[:, sl(d, r, m)], q_t[:, sl(d, r, nq)], True, True,
                              reads=[r_q, r_k], writes=[r_S], signal=(si == len(subs) - 1))
                    Sx.act(w32[:], S_t[:], AF.Exp, reads=[r_S], writes=[r_w32])
                    Sx.tt("dve", wb[:].rearrange("p (a b) n -> p a (b n)", a=2),
                          w32[:].rearrange("p (a b) n -> p a (b n)", a=2),
                          G_t[:, di, :].unsqueeze(1).broadcast_to([128, 2, 256]), ALU.mult,
                          reads=[r_w32, r_G], writes=[r_wb])

                def N_phase(u):
                    di, d, r, n0 = units[u]
                    v_t, r_v = vds[di]
                    N_t, r_N = pN[u % NP]
                    wb, r_wb = Wb[u % NP]
                    nmm = []
                    for which in range(2):
                        for qi in range(2):
                            nq = n0 + qi
                            parts = [pp for pp in range(2) if nq - pp >= 0]
                            for pi_, part in enumerate(parts):
                                nmm.append((which, qi, part, nq - part, pi_ == 0, pi_ == len(parts) - 1))
                    for ni, (which, qi, part, m, first, last) in enumerate(nmm):
                        lhsT = v_t[:, m * d + r, :] if which == 0 else k.ones[:]
                        Sx.mm(N_t[:, 2 * which + qi, :], lhsT, wb[:, 2 * qi + part, :], first, last,
                              reads=[r_v, r_wb, k.r_ones], writes=[r_N], signal=(ni == len(nmm) - 1))
                    a0 = n0 * 128 * d + r
                    acc_view = acc[:, :, a0:a0 + 255 * d + 1:d]
                    n_view = N_t[:].rearrange("p (w q) n -> p w (q n)", w=2)
                    if di == 0:
                        Sx.copy("act", acc_view, n_view, reads=[r_N], writes=[r_acc])
                    else:
                        Sx.tt("dve", acc_view, acc_view, n_view, ALU.add, reads=[r_N, r_acc], writes=[r_acc])

                nu = len(units)
                for step in range(nu + NP - 1):
                    if step < nu:
                        S_phase(step)
                    if step - (NP - 1) >= 0:
                        N_phase(step - (NP - 1))
                for Q in range(c.NT):
                    q0 = Q * T
                    Sx.act(rden[:], acc[:, 1, q0:q0 + T], AF.Ln, reads=[r_acc], writes=[r_rden])
                    Sx.act(rden[:], rden[:], AF.Exp, reads=[r_rden], writes=[r_rden], scale=-1.0)
                    Sx.tt("dve", osb[:], acc[:, 0, q0:q0 + T], rden[:], ALU.mult, reads=[r_acc, r_rden], writes=[r_osb])
                    on_t, r_on = on[oni % 2]
                    oni += 1
                    head_epilogue(Sx, k, l, HS + h, osb[:], r_osb, q0, T, sqh, r_sqh, psn, r_psn, rstd, r_rstd, on_t, r_on)
            Sx.emit()

    def stage_dense(l):
        st, Sx, sb, ps = new_stage(f"c{l}")
        with st:
            k = load_consts(Sx, sb)
            xsrc = xT_in if l == 0 else xr
            xdst = yT_out if l == DEPTH - 1 else xr
            x_t, _ = sb("xt", [128, KC, T], F32)
            yT, _ = sb("yT", [128, KC, T], F32)
            hT, _ = sb("hT", [128, KC, T], BF16)
            actb, _ = sb("actb", [128, FC, T], BF16)
            r_xc = [Res(f"x{i}") for i in range(KC)]
            r_yc = [Res(f"y{i}") for i in range(KC)]
            r_hc = [Res(f"h{i}") for i in range(KC)]
            r_ac = [Res(f"a{i}") for i in range(FC)]
            sqg = [sb(f"sqg{i}", [128, 4, T], BF16) for i in range(2)]
            rstd, r_rstd = sb("rstd", [128, T], F32)
            ptmp, r_ptmp = sb("ptmp", [128, T], F32)
            WK = max(KC, c.KG)
            wsl = [sb(f"w{i}", [128, WK, 512], BF16) for i in range(3)]
            wpp = [sb(f"wpp{i}", [128, c.PC, 512], BF16) for i in range(2)]
            sg = [sb(f"sg{i}", [128, T], F32) for i in range(2)]
            gtmp = [sb(f"gtmp{i}", [128, T], F32) for i in range(2)]
            pb, r_pb = sb("pb", [128, c.PC, T], BF16)
            psn, r_psn = ps("psn", [128, T])
            banks = [ps(f"pb{i}", [128, 512]) for i in range(7)]
            oT_t = actb[:, 0:NH, :]
            bi = [0]
            ei = [0]
            sqi = [0]
            seq = []
            for tt in range(c.NT):
                seq += [(Wb_out[l, b], NH) for b in range(c.NBO)]
                seq += [(Wb_gu[l, b], KC) for b in range(c.NBG)]
                seq += [(Wb_dn[l, cg, kg], c.KG) for cg in range(c.NCG) for kg in range(c.NKG)]
                seq += [(Wb_pg[l, b], KC) for b in range(c.NBO)]
            ws = WStream(Sx, wsl, seq)
            wps = WStream(Sx, wpp, [(Wb_pp[l, b], c.PC) for tt in range(c.NT) for b in range(c.NBO)])

            def next_bank():
                b = banks[bi[0] % 7]
                bi[0] += 1
                return b

            def evac_eng():
                ei[0] += 1
                return "act" if ei[0] % 2 == 0 else "dve"

            pending = []

            def flush_sq():
                while pending:
                    s_t, r_s, g0, g1 = pending.pop(0)
                    for kc in range(g0, g1):
                        Sx.mm(psn[:], k.ones[:], s_t[:, kc - g0, :], start=(kc == 0), stop=(kc == KC - 1),
                              reads=[k.r_ones, r_s], writes=[r_psn], signal=(kc == g1 - 1))

            def sq_group(src, r_src, g, defer=False):
                g0, g1 = 4 * g, min(KC, 4 * g + 4)
                flush_sq()
                s_t, r_s = sqg[sqi[0] % 2]
                sqi[0] += 1
                Sx.act(s_t[:, 0:g1 - g0, :], src[:, g0:g1, :], AF.Square, reads=r_src[g0:g1], writes=[r_s])
                pending.append((s_t, r_s, g0, g1))
                if not defer:
                    flush_sq()

            def rstd_finish():
                flush_sq()
                Sx.act(rstd[:], psn[:], AF.Ln, reads=[r_psn], writes=[r_rstd], scale=1.0 / D, bias=RMS_EPS)
                Sx.act(rstd[:], rstd[:], AF.Exp, reads=[r_rstd], writes=[r_rstd], scale=-0.5)

            NG4 = (KC + 3) // 4

            def post_norm_residual(which, nxt):
                goff = l * c.NG + which * KC
                goff2 = l * c.NG + nxt * KC
                rstd_finish()
                for kc in range(KC):
                    eng = "pool" if kc % 5 == 2 else "dve"
                    scale_chunk(Sx, eng, yT[:, kc, :], yT[:, kc, :], k.gains[:, goff + kc:goff + kc + 1], rstd[:],
                                ptmp[:], r_ptmp, [r_yc[kc], k.r_gains, r_rstd], [r_yc[kc]])
                    Sx.tt(eng, x_t[:, kc, :], x_t[:, kc, :], yT[:, kc, :], ALU.add, reads=[r_xc[kc], r_yc[kc]],
                          writes=[r_xc[kc]])
                for kc in range(KC):
                    Sx.act(hT[:, kc, :], x_t[:, kc, :], AF.Copy, reads=[r_xc[kc], k.r_gains], writes=[r_hc[kc]],
                           scale=k.gains[:, goff2 + kc:goff2 + kc + 1])
                    if kc % 4 == 3 or kc == KC - 1:
                        sq_group(x_t, r_xc, kc // 4, defer=True)
                rstd_finish()

            def load_x_chunk(tt, cc):
                Sx.dma("sp", x_t[:, cc, :], xsrc[cc * 128:(cc + 1) * 128, tt * T:(tt + 1) * T], writes=[r_xc[cc]])

            def load_oT(tt):
                Sx.dma("sp", oT_t, oT[:, tt * T:(tt + 1) * T].rearrange("(kc p) t -> p kc t", p=128),
                       writes=r_ac[0:NH], sres=r_ac[0])

            def load_p(tt):
                Sx.dma("pool", pb[:], pT_in[l][:, tt * T:(tt + 1) * T].rearrange("(kc p) t -> p kc t", p=128),
                       writes=[r_pb])

            load_oT(0)
            for cc in range(KC):
                load_x_chunk(0, cc)
            for tt in range(c.NT):
                t0 = tt * T
                load_p(tt)
                for b in range(c.NBO):
                    w_t, r_w = ws.next()
                    for j in range(4):
                        p_t, r_p = next_bank()
                        for kc in range(NH):
                            Sx.mm(p_t[:], w_t[:, kc, j * 128:(j + 1) * 128], oT_t[:, kc, :], kc == 0, kc == NH - 1,
                                  reads=[r_w, r_ac[kc]], writes=[r_p])
                        Sx.copy(evac_eng(), yT[:, b * 4 + j, :], p_t[:], reads=[r_p], writes=[r_yc[b * 4 + j]])
                    sq_group(yT, r_yc, b, defer=True)
                post_norm_residual(1, 2)
                for b in range(c.NBG):
                    w_t, r_w = ws.next()
                    for j in range(2):
                        f = b * 2 + j
                        pg_t, r_pg = next_bank()
                        pu_t, r_pu = next_bank()
                        for kc in range(KC):
                            Sx.mm(pg_t[:], w_t[:, kc, j * 128:(j + 1) * 128], hT[:, kc, :], kc == 0, kc == KC - 1,
                                  reads=[r_w, r_hc[kc]], writes=[r_pg])
                        for kc in range(KC):
                            Sx.mm(pu_t[:], w_t[:, kc, 256 + j * 128:256 + (j + 1) * 128], hT[:, kc, :], kc == 0,
                                  kc == KC - 1, reads=[r_w, r_hc[kc]], writes=[r_pu])
                        sg_t, r_sg = sg[f % 2]
                        gt_t, r_gt = gtmp[f % 2]
                        Sx.tt("dve", gt_t[:], pg_t[:], rstd[:], ALU.mult, reads=[r_pg, r_rstd], writes=[r_gt])
                        Sx.act(sg_t[:], gt_t[:], AF.Silu, reads=[r_gt], writes=[r_sg])
                        Sx.tt("dve", gt_t[:], pu_t[:], rstd[:], ALU.mult, reads=[r_pu, r_rstd, r_sg], writes=[r_gt])
                        Sx.tt("dve", actb[:, f, :], sg_t[:], gt_t[:], ALU.mult, reads=[r_sg, r_gt], writes=[r_ac[f]])
                for cg in range(c.NCG):
                    ybanks = [next_bank() for _ in range(4)]
                    for kg in range(c.NKG):
                        w_t, r_w = ws.next()
                        for j in range(4):
                            p_t, r_p = ybanks[j]
                            for kk in range(c.KG):
                                first = (kg == 0 and kk == 0)
                                last = (kg == c.NKG - 1 and kk == c.KG - 1)
                                Sx.mm(p_t[:], w_t[:, kk, j * 128:(j + 1) * 128], actb[:, kg * c.KG + kk, :], first, last,
                                      reads=[r_w, r_ac[kg * c.KG + kk]], writes=[r_p], signal=(kk == c.KG - 1))
                    for j in range(4):
                        p_t, r_p = ybanks[j]
                        Sx.copy(evac_eng(), yT[:, cg * 4 + j, :], p_t[:], reads=[r_p], writes=[r_yc[cg * 4 + j]])
                    sq_group(yT, r_yc, cg, defer=True)
                if tt + 1 < c.NT:
                    load_oT(tt + 1)
                post_norm_residual(3, 4)
                for b in range(c.NBO):
                    w_t, r_w = ws.next()
                    wp_t, r_wp = wps.next()
                    for j in range(4):
                        cc = b * 4 + j
                        pg_t, r_pg = next_bank()
                        pu_t, r_pu = next_bank()
                        for kc in range(KC):
                            Sx.mm(pg_t[:], w_t[:, kc, j * 128:(j + 1) * 128], hT[:, kc, :], kc == 0, kc == KC - 1,
                                  reads=[r_w, r_hc[kc]], writes=[r_pg])
                        for kc in range(c.PC):
                            Sx.mm(pu_t[:], wp_t[:, kc, j * 128:(j + 1) * 128], pb[:, kc, :], kc == 0, kc == c.PC - 1,
                                  reads=[r_wp, r_pb], writes=[r_pu])
                        sg_t, r_sg = sg[j % 2]
                        gt_t, r_gt = gtmp[j % 2]
                        Sx.tt("dve", gt_t[:], pg_t[:], rstd[:], ALU.mult, reads=[r_pg, r_rstd], writes=[r_gt])
                        Sx.act(sg_t[:], gt_t[:], AF.Sigmoid, reads=[r_gt], writes=[r_sg])
                        Sx.tt("dve", yT[:, cc, :], sg_t[:], pu_t[:], ALU.mult, reads=[r_sg, r_pu], writes=[r_yc[cc]])
                        Sx.tt("pool" if j == 3 else "dve", x_t[:, cc, :], x_t[:, cc, :], yT[:, cc, :], ALU.add,
                              reads=[r_xc[cc], r_yc[cc]], writes=[r_xc[cc]])
                        Sx.dma("sp", xdst[cc * 128:(cc + 1) * 128, t0:t0 + T], x_t[:, cc, :], reads=[r_xc[cc]])
                        if tt + 1 < c.NT:
                            load_x_chunk(tt + 1, cc)
            Sx.emit()

    gstack = ExitStack()
    pool = SemPool(nc, gstack)
    if want("cvt"):
        stage_convert([0] if c.overlap_cvt else list(range(DEPTH)))
    if want("bias"):
        stage_bias()
    for l in range(DEPTH):
        if want("qkv"):
            stage_qkv(l)
        if want("sb"):
            stage_sb(l)
        if want("dl"):
            stage_dl(l)
        if want("dense"):
            stage_dense(l)
    gstack.close()
    return nc


def pack_gains(cfg, ln_mix_pre, ln_mix_post, ln_ffn_pre, ln_ffn_post, ln_pli, ln_head):
    cols = []
    for l in range(cfg.DEPTH):
        for g in (ln_mix_pre, ln_mix_post, ln_ffn_pre, ln_ffn_post, ln_pli):
            cols.append(np.asarray(g[l], np.float32).reshape(cfg.KC, 128).T)
        cols.append(np.asarray(ln_head[l], np.float32).reshape(cfg.NH, 128).T)
    return np.ascontiguousarray(np.concatenate(cols, axis=1))


def make_in_maps(cfg, x, p, ln_mix_pre, w_in, ln_head, w_out, ln_mix_post, rel_bias,
                 ln_ffn_pre, w_gate_up, w_down, ln_ffn_post, ln_pli, w_pli_gate, w_pli_proj, n_cores):
    cst, oh = make_consts(cfg)
    gains = pack_gains(cfg, ln_mix_pre, ln_mix_post, ln_ffn_pre, ln_ffn_post, ln_pli, ln_head)
    f = lambda a: np.ascontiguousarray(np.asarray(a, np.float32))
    shared = {
        "w_in": f(w_in), "w_out": f(w_out), "w_gate_up": f(w_gate_up), "w_down": f(w_down),
        "w_pli_gate": f(w_pli_gate), "w_pli_proj": f(w_pli_proj), "gains": gains,
        "rel_bias": f(rel_bias), "cst": cst, "oh": oh,
    }
    x = np.asarray(x, np.float32)
    p = np.asarray(p, np.float32)
    maps = []
    for b in range(n_cores):
        m = dict(shared)
        m["xT"] = np.ascontiguousarray(x[b].T)
        m["pT"] = np.ascontiguousarray(p[:, b].transpose(0, 2, 1))
        maps.append(m)
    return maps


_PROGRAM_CACHE = {}


def kernel(x, p, ln_mix_pre, w_in, ln_head, w_out, ln_mix_post, rel_bias,
           ln_ffn_pre, w_gate_up, w_down, ln_ffn_post, ln_pli, w_pli_gate, w_pli_proj):
    cfg = Cfg()
    n = 8
    in_maps = make_in_maps(cfg, x, p, ln_mix_pre, w_in, ln_head, w_out, ln_mix_post, rel_bias,
                           ln_ffn_pre, w_gate_up, w_down, ln_ffn_post, ln_pli, w_pli_gate, w_pli_proj, n)
    nc = build_program(cfg)
    res = run_bass_kernel_spmd(nc, in_maps, core_ids=list(range(n)))
    out = np.stack([np.asarray(r["yT"], np.float32).T for r in res.results], axis=0)
    return np.ascontiguousarray(out)
```

```python
import math
import os
from contextlib import ExitStack

import numpy as np
import concourse.bass as bass
import concourse.mybir as mybir
from concourse.bass_utils import run_bass_kernel_spmd

F32 = mybir.dt.float32
BF16 = mybir.dt.bfloat16
AF = mybir.ActivationFunctionType
ALU = mybir.AluOpType

RMS_EPS = 1e-6
NEG_BIG = -30000.0


class Res:
    __slots__ = ("name", "writer", "readers", "dsem", "dcount", "dbase")

    def __init__(self, name):
        self.name = name
        self.writer = None
        self.readers = []
        self.dsem = None
        self.dcount = 0
        self.dbase = 0


class _Eng:
    def __init__(self, name, sem):
        self.name = name
        self.sem = sem
        self.count = 0
        self.waited = {}
        self.ops = []


class Sched:
    ENGS = ("pe", "act", "dve", "pool", "sp")

    def __init__(self, nc, pool, tag):
        self.nc = nc
        self.pool = pool
        self.tag = tag
        self.eng = {}
        for e in self.ENGS:
            eo = _Eng(e, pool.eng_sems[e])
            eo.count = pool.eng_counts[e]
            self.eng[e] = eo
        self.dma_res = []

    def _deps(self, ename, reads, writes):
        deps = []
        for r in reads:
            if r.writer is not None:
                deps.append(r.writer)
        for w in writes:
            if w.writer is not None:
                deps.append(w.writer)
            deps.extend(w.readers)
        e = self.eng[ename]
        best = {}
        for (sem, val, src) in deps:
            if src == "pe" and ename == "pe":
                continue
            k = id(sem)
            if e.waited.get(k, 0) >= val:
                continue
            if k not in best or best[k][1] < val:
                best[k] = (sem, val)
        for k, (sem, val) in best.items():
            e.waited[k] = val
        return list(best.values())

    def _record(self, tok, reads, writes):
        for r in reads:
            r.readers.append(tok)
        for w in writes:
            w.writer = tok
            w.readers = []

    def op(self, ename, fn, reads=(), writes=(), signal=True):
        e = self.eng[ename]
        waits = self._deps(ename, reads, writes)
        tok = (e.sem, e.count + 1, ename)
        if signal:
            e.count += 1
        e.ops.append((waits, fn, (e.sem, 1) if signal else None))
        self._record(tok, reads, writes)
        return tok

    def dma(self, qname, out_ap, in_ap, reads=(), writes=(), sres=None):
        e = self.eng[qname]
        waits = self._deps(qname, reads, writes)
        if sres is None:
            sres = writes[0] if writes else reads[0]
        if sres.dsem is None:
            sres.dsem, sres.dbase = self.pool.get()
            self.dma_res.append(sres)
        sres.dcount += 1
        tok = (sres.dsem, sres.dbase + 16 * sres.dcount, "dma")

        def fn(eng, out_ap=out_ap, in_ap=in_ap):
            return eng.dma_start(out=out_ap, in_=in_ap)

        e.ops.append((waits, fn, (sres.dsem, 16)))
        self._record(tok, reads, writes)
        return tok

    def mm(self, out, lhsT, rhs, start, stop, reads, writes, signal=None):
        if signal is None:
            signal = stop
        return self.op("pe", lambda e: e.matmul(out, lhsT, rhs, start=start, stop=stop),
                       reads=reads, writes=writes, signal=signal)

    def act(self, out, in_, func, reads, writes, scale=1.0, bias=0.0):
        return self.op("act", lambda e: e.activation(out=out, in_=in_, func=func, bias=bias, scale=scale),
                       reads=reads, writes=writes)

    def tt(self, eng, out, in0, in1, op, reads, writes):
        return self.op(eng, lambda e: e.tensor_tensor(out, in0, in1, op), reads=reads, writes=writes)

    def copy(self, eng, out, in_, reads, writes, scale=None):
        if eng == "act":
            return self.act(out, in_, AF.Copy, reads, writes, scale=(1.0 if scale is None else scale))
        if scale is None:
            return self.op(eng, lambda e: e.tensor_copy(out, in_), reads=reads, writes=writes)
        return self.op(eng, lambda e: e.tensor_scalar_mul(out, in_, scale), reads=reads, writes=writes)

    def emit(self):
        nc = self.nc
        finals = [(r.dsem, r.dbase + 16 * r.dcount) for r in self.dma_res]
        engs = self.eng

        def run(engobj, e):
            for waits, fn, inc in e.ops:
                for sem, val in waits:
                    engobj.wait_ge(sem, val)
                ins = fn(engobj)
                if inc is not None:
                    ins.then_inc(inc[0], inc[1])

        with nc.Block() as block:
            @block.tensor
            def _(t):
                run(t, engs["pe"])

            @block.scalar
            def _(s):
                run(s, engs["act"])

            @block.vector
            def _(v):
                run(v, engs["dve"])

            @block.gpsimd
            def _(g):
                run(g, engs["pool"])

            @block.sync
            def _(s):
                run(s, engs["sp"])
                for sem, val in finals:
                    s.wait_ge(sem, val)

        for e in self.ENGS:
            self.pool.eng_counts[e] = engs[e].count
        for r in self.dma_res:
            self.pool.put(r.dsem, r.dbase + 16 * r.dcount)


class SemPool:
    def __init__(self, nc, stack):
        self.nc = nc
        self.stack = stack
        self.eng_sems = {e: stack.enter_context(nc.semaphore(f"eng_{e}")) for e in Sched.ENGS}
        self.eng_counts = {e: 0 for e in Sched.ENGS}
        self.free = []
        self.n = 0

    def get(self):
        if self.free:
            return self.free.pop()
        sem = self.stack.enter_context(self.nc.semaphore(f"dsem_{self.n}"))
        self.n += 1
        return sem, 0

    def put(self, sem, value):
        self.free.append((sem, value))


class Cfg:
    def __init__(self, D=2048, S=4096, DEPTH=4, HS=8, HD=8, DFF=5632, PLI=256, debug=False,
                 stages=None):
        self.D, self.S, self.DEPTH, self.HS, self.HD, self.DFF, self.PLI = D, S, DEPTH, HS, HD, DFF, PLI
        self.T = 512
        self.KC = D // 128
        self.NH = HS + HD
        self.MIXW = self.NH * 128
        assert self.MIXW == D
        self.INW = 3 * self.MIXW
        self.FC = DFF // 128
        self.NT = S // self.T
        self.NBLK = S // 128
        self.NBI = self.INW // 512
        self.NBO = D // 512
        self.NBG = self.FC // 2
        self.KG = 11
        self.NKG = self.FC // self.KG
        assert self.NKG * self.KG == self.FC
        self.NCG = D // 512
        self.NBS = HS * 128 // 512
        self.NBD = HD * 128 // 512
        self.PC = PLI // 128
        self.NG = 5 * self.KC + self.NH
        self.debug = debug
        self.overlap_cvt = os.environ.get("K_OVERLAP", "sb")
        self.stages = stages
        self.SCALE = 1.0 / math.sqrt(128.0)
        self.DIL = ((128, 1), (512, 4), (2048, 16))


def t5_bucket_np(dist):
    num_buckets, max_distance = 32, 2048
    max_exact = num_buckets // 2
    dist = np.asarray(dist, dtype=np.int32)
    d = np.maximum(dist, 1).astype(np.float32)
    large = max_exact + (np.log(d / np.float32(max_exact)) / np.float32(math.log(max_distance / max_exact))
                         * np.float32(num_buckets - max_exact)).astype(np.int32)
    large = np.minimum(large, num_buckets - 1)
    return np.where(dist < max_exact, dist, large)


def make_consts(cfg):
    jp = np.arange(128)[:, None]
    j = np.arange(128)[None, :]
    tri_neg = np.where(jp >= j, -1.0, 0.0).astype(np.float32)
    masks = np.zeros((128, 4, 512), np.float32)
    t = np.arange(512)[None, :]
    for r in range(4):
        masks[:, r, :] = ((r * 128 + np.arange(128)[:, None]) < t).astype(np.float32)
    J = np.zeros((128, 128), np.float32)
    J[np.arange(128), 127 - np.arange(128)] = 1.0
    cst = np.concatenate([tri_neg, masks.reshape(128, 2048), J], axis=1)
    oh = np.zeros((33, 3, 384), np.float32)
    for di, (window, dil) in enumerate(cfg.DIL):
        n_back = window // dil
        for u in range(384):
            delta = u - 127
            if 0 <= delta <= n_back:
                b = int(t5_bucket_np(np.array([delta * dil]))[0])
                oh[b, di, u] = 1.0
            else:
                oh[32, di, u] = 1.0
    return cst, oh.reshape(33, 3 * 384)


def build_program(cfg):
    nc = bass.Bass("TRN2", target_bir_lowering=False)
    c = cfg
    D, S, T, KC, NH, HS, HD, FC = c.D, c.S, c.T, c.KC, c.NH, c.HS, c.HD, c.FC
    DEPTH = c.DEPTH

    def din(name, shape, dt=F32):
        return nc.dram_tensor(name, list(shape), dt, kind="ExternalInput").ap()

    def dtmp(name, shape, dt):
        kind = "ExternalOutput" if c.debug else "Internal"
        return nc.dram_tensor(name, list(shape), dt, kind=kind).ap()

    xT_in = din("xT", [D, S])
    pT_in = din("pT", [DEPTH, c.PLI, S])
    w_in = din("w_in", [DEPTH, D, c.INW])
    w_out = din("w_out", [DEPTH, c.MIXW, D])
    w_gu = din("w_gate_up", [DEPTH, D, 2 * c.DFF])
    w_dn = din("w_down", [DEPTH, c.DFF, D])
    w_pg = din("w_pli_gate", [DEPTH, D, D])
    w_pp = din("w_pli_proj", [DEPTH, c.PLI, D])
    gains_in = din("gains", [128, DEPTH * c.NG])
    relb_in = din("rel_bias", [32, HD])
    cst_in = din("cst", [128, 128 + 2048 + 128])
    oh_in = din("oh", [33, 3 * 384])
    yT_out = nc.dram_tensor("yT", [D, S], F32, kind="ExternalOutput").ap()

    Wb_in = dtmp("Wb_in", [DEPTH, c.NBI, 128, KC, 512], BF16)
    Wb_out = dtmp("Wb_out", [DEPTH, c.NBO, 128, NH, 512], BF16)
    Wb_gu = dtmp("Wb_gu", [DEPTH, c.NBG, 128, KC, 512], BF16)
    Wb_dn = dtmp("Wb_dn", [DEPTH, c.NCG, c.NKG, 128, c.KG, 512], BF16)
    Wb_pg = dtmp("Wb_pg", [DEPTH, c.NBO, 128, KC, 512], BF16)
    Wb_pp = dtmp("Wb_pp", [DEPTH, c.NBO, 128, c.PC, 512], BF16)
    xr = dtmp("xr", [D, S], F32)
    qT_sb = dtmp("qT_sb", [HS, 128, S], BF16)
    kT_sb = dtmp("kT_sb", [HS, 128, S], BF16)
    v_sb = dtmp("v_sb", [S, HS * 128], BF16)
    qT_dl = dtmp("qT_dl", [HD, 128, S], BF16)
    kT_dl = dtmp("kT_dl", [HD, 128, S], BF16)
    v_dl = dtmp("v_dl", [S, HD * 128], BF16)
    oT = dtmp("oT", [c.MIXW, S], BF16)
    gp = dtmp("gp", [HD, 3, 384], F32)
    Gmat = dtmp("Gmat", [HD, 128, 3, 256], F32)

    uid = [0]

    def want(stage):
        return c.stages is None or stage in c.stages

    def emit_convert(Sx, l, dres, throttle=0, lazy=None):
        dummies = [Res(f"cvthr{i}") for i in range(max(throttle, 1))]
        cnt = [0]

        def cdma(out_ap, in_ap, sres):
            def go(out_ap=out_ap, in_ap=in_ap, sres=sres):
                w = [dummies[cnt[0] % len(dummies)]] if throttle else []
                cnt[0] += 1
                Sx.dma("pool", out_ap, in_ap, writes=w, sres=sres)
            if lazy is None:
                go()
            else:
                lazy.append(go)

        for b in range(c.NBI):
            cdma(Wb_in[l, b],
                   w_in[l][:, b * 512:(b + 1) * 512].rearrange("(kc p) n -> p kc n", p=128), sres=dres)
        for b in range(c.NBO):
            cdma(Wb_out[l, b],
                   w_out[l][:, b * 512:(b + 1) * 512].rearrange("(kc p) n -> p kc n", p=128), sres=dres)
        for b in range(c.NBG):
            cdma(Wb_gu[l, b][:, :, 0:256],
                   w_gu[l][:, b * 256:(b + 1) * 256].rearrange("(kc p) n -> p kc n", p=128), sres=dres)
            cdma(Wb_gu[l, b][:, :, 256:512],
                   w_gu[l][:, c.DFF + b * 256:c.DFF + (b + 1) * 256].rearrange("(kc p) n -> p kc n", p=128),
                   sres=dres)
        for cg in range(c.NCG):
            for kg in range(c.NKG):
                cdma(Wb_dn[l, cg, kg],
                       w_dn[l][kg * c.KG * 128:(kg + 1) * c.KG * 128, cg * 512:(cg + 1) * 512]
                       .rearrange("(kc p) n -> p kc n", p=128), sres=dres)
        for b in range(c.NBO):
            cdma(Wb_pg[l, b],
                   w_pg[l][:, b * 512:(b + 1) * 512].rearrange("(kc p) n -> p kc n", p=128), sres=dres)
            cdma(Wb_pp[l, b],
                   w_pp[l][:, b * 512:(b + 1) * 512].rearrange("(kc p) n -> p kc n", p=128), sres=dres)

    def stage_convert(layers):
        with ExitStack() as st:
            Sx = Sched(nc, pool, f"cv{layers[0]}")
            dres = Res("cvt")
            for l in layers:
                emit_convert(Sx, l, dres)
            Sx.emit()

    class Ctx:
        pass

    def new_stage(tag):
        st = ExitStack()
        Sx = Sched(nc, pool, tag)
        uid[0] += 1
        u = uid[0]

        def sb(name, shape, dt):
            t = st.enter_context(nc.sbuf_tensor(f"{name}_{u}", list(shape), dt))
            return t, Res(name)

        def ps(name, shape, dt=F32):
            t = st.enter_context(nc.psum_tensor(f"{name}_{u}", list(shape), dt))
            return t, Res(name)

        return st, Sx, sb, ps

    def load_consts(Sx, sb):
        k = Ctx()
        k.ones, k.r_ones = sb("ones", [128, 128], BF16)
        k.gains, k.r_gains = sb("gains", [128, DEPTH * c.NG], F32)
        Sx.op("pool", lambda e: e.memset(k.ones[:], 1.0), writes=[k.r_ones])
        Sx.dma("sp", k.gains[:], gains_in, writes=[k.r_gains])
        return k

    def emit_rstd(Sx, k, src3, nchunks, n_feat, sq, r_sq, src_res, psn, r_psn, rstd, r_rstd, width):
        for g0 in range(0, nchunks, 4):
            g1 = min(nchunks, g0 + 4)
            Sx.act(sq[:, g0:g1, 0:width], src3[:, g0:g1, :], AF.Square, reads=[src_res], writes=[r_sq])
        for kc in range(nchunks):
            Sx.mm(psn[:, 0:width], k.ones[:], sq[:, kc, 0:width], start=(kc == 0), stop=(kc == nchunks - 1),
                  reads=[k.r_ones, r_sq], writes=[r_psn])
        Sx.act(rstd[:, 0:width], psn[:, 0:width], AF.Ln, reads=[r_psn], writes=[r_rstd],
               scale=1.0 / n_feat, bias=RMS_EPS)
        Sx.act(rstd[:, 0:width], rstd[:, 0:width], AF.Exp, reads=[r_rstd], writes=[r_rstd], scale=-0.5)

    def scale_chunk(Sx, eng, out, in_, gain_ap, rstd_ap, tmp, r_tmp, reads, writes):
        if eng == "dve":
            Sx.op("dve", lambda e: e.scalar_tensor_tensor(out, in_, gain_ap, rstd_ap, ALU.mult, ALU.mult),
                  reads=reads, writes=writes)
        else:
            Sx.tt("pool", tmp, in_, rstd_ap, ALU.mult, reads=reads, writes=[r_tmp])
            Sx.op("pool", lambda e: e.tensor_scalar_mul(out, tmp, gain_ap), reads=[r_tmp] + list(reads), writes=writes)

    def stage_bias():
        st, Sx, sb, ps = new_stage("gb")
        with st:
            rb, r_rb = sb("rb", [33, HD], F32)
            oh, r_oh = sb("oh", [33, 3 * 384], F32)
            gsb, r_gsb = sb("gsb", [HD, 3 * 384], F32)
            pg = [ps(f"pg{i}", [HD, 512]) for i in range(3)]
            Sx.op("pool", lambda e: e.memset(rb[32:33, :], NEG_BIG), writes=[r_rb])
            Sx.dma("sp", rb[0:32, :], relb_in, writes=[r_rb])
            Sx.dma("sp", oh[:], oh_in, writes=[r_oh])
            for di in range(3):
                Sx.mm(pg[di][0][:, 0:384], rb[:], oh[:, di * 384:(di + 1) * 384], True, True,
                      reads=[r_rb, r_oh], writes=[pg[di][1]])
                Sx.copy("dve", gsb[:, di * 384:(di + 1) * 384], pg[di][0][:, 0:384], reads=[pg[di][1]], writes=[r_gsb])
            r_gp = Res("gp_dram")
            Sx.dma("sp", gp.rearrange("h d u -> h (d u)"), gsb[:], reads=[r_gsb], writes=[r_gp], sres=r_gsb)
            Jm, r_J = sb("Jm", [128, 128], F32)
            Sx.dma("sp", Jm[:], cst_in[:, 128 + 2048:128 + 2048 + 128], writes=[r_J])
            brev = [sb(f"brev{i}", [128, 256], F32) for i in range(2)]
            gout = [sb(f"gout{i}", [128, 3, 256], F32) for i in range(2)]
            pt = [ps(f"pt{i}", [128, 512]) for i in range(2)]
            n_ = 0
            for h in range(HD):
                go_t, r_go = gout[h % 2]
                for di in range(3):
                    b_t, r_b = brev[n_ % 2]
                    p_t, r_p = pt[n_ % 2]
                    n_ += 1
                    src = bass.AP(gp.tensor, (h * 3 + di) * 384, [[1, 128], [1, 256]])
                    Sx.dma("sp", b_t[:], src, reads=[r_gp], writes=[r_b])
                    Sx.mm(p_t[:, 0:256], Jm[:], b_t[:], True, True, reads=[r_J, r_b], writes=[r_p])
                    Sx.act(go_t[:, di, :], p_t[:, 0:256], AF.Exp, reads=[r_p], writes=[r_go])
                Sx.dma("sp", Gmat[h], go_t[:], reads=[r_go])
            Sx.emit()

    class WStream:
        def __init__(self, Sx, slots, seq, queue="sp"):
            self.Sx, self.slots, self.seq, self.queue = Sx, slots, seq, queue
            self.issued = 0
            self.cur = 0

        def _issue_upto(self, j):
            n = len(self.slots)
            while self.issued <= min(j, len(self.seq) - 1):
                i = self.issued
                t, r = self.slots[i % n]
                src, nk = self.seq[i]
                self.Sx.dma(self.queue, t[:, 0:nk, :], src, writes=[r])
                self.issued += 1

        def next(self):
            i = self.cur
            self._issue_upto(i + len(self.slots) - 1)
            self.cur += 1
            return self.slots[i % len(self.slots)]

    def stage_qkv(l):
        st, Sx, sb, ps = new_stage(f"a{l}")
        with st:
            k = load_consts(Sx, sb)
            xsrc = xT_in if l == 0 else xr
            xt = [sb(f"xt{i}", [128, KC, T], F32) for i in range(2)]
            sqs = [sb(f"sq{i}", [128, KC, T], BF16) for i in range(2)]
            hTs = [sb(f"hT{i}", [128, KC, T], BF16) for i in range(2)]
            rstds = [sb(f"rstd{i}", [128, T], F32) for i in range(2)]
            wsl = [sb(f"w{i}", [128, KC, 512], BF16) for i in range(3)]
            ev = [sb(f"ev{i}", [128, 4, 512], BF16) for i in range(4)]
            psn, r_psn = ps("psn", [128, T])
            pacc = [ps(f"pa{i}", [128, 512]) for i in range(6)]
            goff = l * c.NG
            ws = WStream(Sx, wsl, [(Wb_in[l, b], KC) for tt in range(c.NT) for b in range(c.NBI)])
            evi = 0
            pi = 0

            def load_x(tt):
                x_t, r_x = xt[tt % 2]
                Sx.dma("sp", x_t[:], xsrc[:, tt * T:(tt + 1) * T].rearrange("(kc p) t -> p kc t", p=128),
                       writes=[r_x])

            def norm(tt):
                x_t, r_x = xt[tt % 2]
                sq, r_sq = sqs[tt % 2]
                hT, r_hT = hTs[tt % 2]
                rstd, r_rstd = rstds[tt % 2]
                emit_rstd(Sx, k, x_t[:], KC, D, sq, r_sq, r_x, psn, r_psn, rstd, r_rstd, T)
                for kc in range(KC):
                    scale_chunk(Sx, "dve", hT[:, kc, :], x_t[:, kc, :], k.gains[:, goff + kc:goff + kc + 1], rstd[:],
                                None, None, [r_x, k.r_gains, r_rstd], [r_hT])

            if c.overlap_cvt == "qkv" and l + 1 < DEPTH and want("cvt"):
                emit_convert(Sx, l + 1, Res("cvt_next"), throttle=4)
            load_x(0)
            if c.NT > 1:
                load_x(1)
            norm(0)
            for tt in range(c.NT):
                t0 = tt * T
                hT, r_hT = hTs[tt % 2]
                for b in range(c.NBI):
                    w_t, r_w = ws.next()
                    if b == 2 and tt + 1 < c.NT:
                        norm(tt + 1)
                        if tt + 2 < c.NT:
                            load_x(tt + 2)
                    if b < 3 * c.NBS:
                        role, rbk = ("q", "k", "v")[b // c.NBS], b % c.NBS
                        dq, dk, dv = qT_sb, kT_sb, v_sb
                    else:
                        bb = b - 3 * c.NBS
                        role, rbk = ("q", "k", "v")[bb // c.NBD], bb % c.NBD
                        dq, dk, dv = qT_dl, kT_dl, v_dl
                    e_t, r_e = ev[evi % 4]
                    evi += 1
                    for j in range(4):
                        p_t, r_p = pacc[pi % 6]
                        pi += 1
                        for kc in range(KC):
                            if role == "v":
                                Sx.mm(p_t[:], hT[:, kc, j * 128:(j + 1) * 128], w_t[:, kc, :],
                                      kc == 0, kc == KC - 1, reads=[r_hT, r_w], writes=[r_p])
                            else:
                                Sx.mm(p_t[:], w_t[:, kc, j * 128:(j + 1) * 128], hT[:, kc, :],
                                      kc == 0, kc == KC - 1, reads=[r_hT, r_w], writes=[r_p])
                        eng = "act" if j % 2 == 0 else "dve"
                        Sx.copy(eng, e_t[:, j, :], p_t[:], reads=[r_p], writes=[r_e],
                                scale=(c.SCALE if role == "q" else None))
                    if role == "v":
                        Sx.dma("sp", dv[t0:t0 + T, rbk * 512:(rbk + 1) * 512].rearrange("(tb p) n -> p tb n", p=128),
                               e_t[:], reads=[r_e])
                    else:
                        dst = dq if role == "q" else dk
                        Sx.dma("sp", dst[rbk * 4:(rbk + 1) * 4, :, t0:t0 + T].rearrange("h p t -> p h t"),
                               e_t[:], reads=[r_e])
            Sx.emit()

    def head_epilogue(Sx, k, l, hidx, o_ap, r_o, q0, W, sqh, r_sqh, psn, r_psn, rstd, r_rstd, on, r_on):
        emit_rstd(Sx, k, o_ap.unsqueeze(1), 1, 128, sqh, r_sqh, r_o, psn, r_psn, rstd, r_rstd, W)
        gcol = l * c.NG + 5 * KC + hidx
        Sx.op("dve", lambda e: e.scalar_tensor_tensor(on[:, 0:W], o_ap, k.gains[:, gcol:gcol + 1], rstd[:, 0:W],
                                                      ALU.mult, ALU.mult),
              reads=[r_o, k.r_gains, r_rstd], writes=[r_on])
        Sx.dma("sp", oT[hidx * 128:(hidx + 1) * 128, q0:q0 + W], on[:, 0:W], reads=[r_on])

    def stage_sb(l):
        st, Sx, sb, ps = new_stage(f"s{l}")
        with st:
            k = load_consts(Sx, sb)
            negones, r_negones = sb("negones", [128, 128], BF16)
            trineg, r_trineg = sb("trineg", [128, 128], BF16)
            masks, r_masks = sb("masks", [128, 4, 512], BF16)
            Sx.op("pool", lambda e: e.memset(negones[:], -1.0), writes=[r_negones])
            Sx.dma("pool", trineg[:], cst_in[:, 0:128], writes=[r_trineg])
            Sx.dma("pool", masks[:], cst_in[:, 128:128 + 2048].rearrange("p (r t) -> p r t", r=4), writes=[r_masks])
            NB = c.NBLK
            qk = [(sb(f"q{i}", [128, S], BF16), sb(f"k{i}", [128, S], BF16), sb(f"v{i}", [128, NB, 128], BF16))
                  for i in range(2)]
            NR = 5
            e32 = [sb(f"e32_{i}", [128, 512], F32) for i in range(2)]
            spb = [sb(f"sp{i}", [128, 512], BF16) for i in range(NR)]
            aT = [sb(f"aT{i}", [128, 512], BF16) for i in range(NR)]
            Rb = [sb(f"Rb{i}", [128, 512], F32) for i in range(2)]
            Rb16 = [sb(f"Rb16_{i}", [128, 512], BF16) for i in range(2)]
            negones32, r_negones32 = sb("negones32", [128, 128], F32)
            Sx.op("pool", lambda e: e.memset(negones32[:], -1.0), writes=[r_negones32])
            osb, r_osb = sb("osb", [128, 512], F32)
            sqh, r_sqh = sb("sqh", [128, 1, 512], BF16)
            rstd, r_rstd = sb("rstd", [128, 512], F32)
            on = [sb(f"on{i}", [128, 512], BF16) for i in range(2)]
            pA = [ps(f"pA{i}", [128, 512]) for i in range(NR)]
            pO = [ps(f"pO{i}", [128, 512]) for i in range(2)]
            psn, r_psn = ps("psn", [128, 512])
            oni = 0
            gq = 0
            def load_head(h):
                (q_t, r_q), (k_t, r_k), (v_t, r_v) = qk[h % 2]
                Sx.dma("sp", q_t[:], qT_sb[h], writes=[r_q])
                Sx.dma("sp", k_t[:], kT_sb[h], writes=[r_k])
                Sx.dma("sp", v_t[:], v_sb[:, h * 128:(h + 1) * 128].rearrange("(b p) e -> p b e", p=128),
                       writes=[r_v])

            cv_jobs = []
            if c.overlap_cvt == "sb" and l + 1 < DEPTH and want("cvt"):
                emit_convert(Sx, l + 1, Res("cvt_next"), throttle=2, lazy=cv_jobs)
            n_items_total = HS * sum(4 * Q + 4 for Q in range(c.NT))
            cv_every = max(1, n_items_total // (len(cv_jobs) + 1)) if cv_jobs else 0
            cv_step = [0]
            load_head(0)
            for h in range(HS):
                (q_t, r_q), (k_t, r_k), (v_t, r_v) = qk[h % 2]
                if h + 1 < HS:
                    load_head(h + 1)
                items = []
                for Q in range(c.NT):
                    kbs = list(range(4 * Q + 3, -1, -1))
                    for i, kb in enumerate(kbs):
                        items.append((Q, i, kb, len(kbs)))
                G = len(items)

                def Zs(g):
                    Q, i, kb, n = items[g]
                    A, r_A = pA[g % NR]
                    Sx.mm(A[:], k_t[:, kb * 128:(kb + 1) * 128], q_t[:, Q * T:(Q + 1) * T], True, False,
                          reads=[r_q, r_k], writes=[r_A], signal=True)

                def Es_exp(g):
                    Q, i, kb, n = items[g]
                    A, r_A = pA[g % NR]
                    e_t, r_e = e32[g % 2]
                    Sx.act(e_t[:], A[:], AF.Exp, reads=[r_A], writes=[r_e])

                def Es(g):
                    Q, i, kb, n = items[g]
                    e_t, r_e = e32[g % 2]
                    s_t, r_s = spb[g % NR]
                    Sx.act(s_t[:], e_t[:], AF.Ln, reads=[r_e], writes=[r_s], bias=1.0)
                    if kb >= 4 * Q:
                        r = kb - 4 * Q
                        Sx.tt("pool", s_t[:], s_t[:], masks[:, r, :], ALU.mult, reads=[r_s, r_masks], writes=[r_s])

                def Cs(g):
                    Q, i, kb, n = items[g]
                    A, r_A = pA[g % NR]
                    s_t, r_s = spb[g % NR]
                    last = (i == 0)
                    Sx.mm(A[:], trineg[:], s_t[:], False, last, reads=[r_trineg, r_s], writes=[r_A], signal=last)
                    if i > 0:
                        R_t, r_Rt = Rb[g % 2]
                        sp_prev, r_sp_prev = spb[(g - 1) % NR]
                        if i == 1:
                            Sx.copy("dve", R_t[:], sp_prev[:], reads=[r_sp_prev], writes=[r_Rt])
                            Sx.mm(A[:], negones[:], sp_prev[:], False, True, reads=[r_negones, r_sp_prev],
                                  writes=[r_A], signal=True)
                        else:
                            R_p, r_Rp = Rb[(g - 1) % 2]
                            R16, r_R16 = Rb16[g % 2]
                            Sx.tt("dve", R_t[:], R_p[:], sp_prev[:], ALU.add, reads=[r_Rp, r_sp_prev], writes=[r_Rt])
                            Sx.copy("dve", R16[:], R_t[:], reads=[r_Rt], writes=[r_R16])
                            Sx.mm(A[:], negones[:], R16[:], False, True, reads=[r_negones, r_R16], writes=[r_A],
                                  signal=True)

                def Xs(g):
                    Q, i, kb, n = items[g]
                    A, r_A = pA[g % NR]
                    a_t, r_a = aT[g % NR]
                    Sx.act(a_t[:], A[:], AF.Exp, reads=[r_A], writes=[r_a])
                    if kb >= 4 * Q:
                        r = kb - 4 * Q
                        Sx.tt("pool", a_t[:], a_t[:], masks[:, r, :], ALU.mult, reads=[r_a, r_masks], writes=[r_a])

                def Vs(g):
                    nonlocal oni
                    Q, i, kb, n = items[g]
                    a_t, r_a = aT[g % NR]
                    po_t, r_po = pO[Q % 2]
                    Sx.mm(po_t[:], v_t[:, kb, :], a_t[:], i == 0, i == n - 1, reads=[r_v, r_a], writes=[r_po],
                          signal=(i == n - 1))
                    if i == n - 1:
                        Sx.copy("dve", osb[:], po_t[:], reads=[r_po], writes=[r_osb])
                        on_t, r_on = on[oni % 2]
                        oni += 1
                        head_epilogue(Sx, k, l, h, osb[:], r_osb, Q * T, T, sqh, r_sqh, psn, r_psn, rstd, r_rstd,
                                      on_t, r_on)

                for step in range(G + 3):
                    if step < G:
                        Zs(step)
                        Es_exp(step)
                    if 0 <= step - 2 < G:
                        Cs(step - 2)
                        Xs(step - 2)
                    if step < G:
                        Es(step)
                    if 0 <= step - 3 < G:
                        Vs(step - 3)
                    if cv_jobs:
                        cv_step[0] += 1
                        if cv_step[0] % cv_every == 0:
                            cv_jobs.pop(0)()
            while cv_jobs:
                cv_jobs.pop(0)()
            Sx.emit()

    def stage_dl(l):
        st, Sx, sb, ps = new_stage(f"d{l}")
        with st:
            k = load_consts(Sx, sb)
            NB = c.NBLK
            qk = [(sb(f"q{i}", [128, S], BF16), sb(f"k{i}", [128, S], BF16),
                   [sb(f"v{i}_{di}", [128, NB, 128], BF16) for di in range(3)],
                   sb(f"G{i}", [128, 3, 256], F32))
                  for i in range(2)]
            acc, r_acc = sb("acc", [128, 2, S], F32)
            W32 = [sb(f"W32_{i}", [128, 4, 128], F32) for i in range(3)]
            Wb = [sb(f"Wb{i}", [128, 4, 128], BF16) for i in range(3)]
            osb, r_osb = sb("osb", [128, 512], F32)
            rden, r_rden = sb("rden", [128, 512], F32)
            sqh, r_sqh = sb("sqh", [128, 1, 512], BF16)
            rstd, r_rstd = sb("rstd", [128, 512], F32)
            on = [sb(f"on{i}", [128, 512], BF16) for i in range(2)]
            pS = [ps(f"pS{i}", [128, 4, 128]) for i in range(3)]
            pN = [ps(f"pN{i}", [128, 4, 128]) for i in range(3)]
            psn, r_psn = ps("psn", [128, 512])
            ui = 0
            oni = 0
            def load_head(h):
                (q_t, r_q), (k_t, r_k), vds, (G_t, r_G) = qk[h % 2]
                Sx.dma("sp", q_t[:], qT_dl[h], writes=[r_q])
                Sx.dma("sp", k_t[:], kT_dl[h], writes=[r_k])
                Sx.dma("sp", G_t[:], Gmat[h], writes=[r_G])
                for di, (window, d) in enumerate(c.DIL):
                    v_t, r_v = vds[di]
                    nsub = S // (128 * d)
                    if d == 1:
                        Sx.dma("sp", v_t[:], v_dl[:, h * 128:(h + 1) * 128].rearrange("(n i) e -> i n e", i=128),
                               writes=[r_v])
                    else:
                        for n_ in range(nsub):
                            Sx.dma("sp", v_t[:, n_ * d:(n_ + 1) * d, :],
                                   v_dl[n_ * 128 * d:(n_ + 1) * 128 * d, h * 128:(h + 1) * 128]
                                   .rearrange("(i r) e -> i r e", r=d), writes=[r_v])

            load_head(0)
            for h in range(HD):
                (q_t, r_q), (k_t, r_k), vds, (G_t, r_G) = qk[h % 2]
                if h + 1 < HD:
                    load_head(h + 1)
                units = [(di, d, r, n0) for di, (window, d) in enumerate(c.DIL)
                         for r in range(d) for n0 in range(0, S // (128 * d), 2)]
                NP = 3

                def sl(d, r, m):
                    a_ = m * 128 * d + r
                    return slice(a_, a_ + 127 * d + 1, d)

                def S_phase(u):
                    di, d, r, n0 = units[u]
                    S_t, r_S = pS[u % NP]
                    w32, r_w32 = W32[u % NP]
                    wb, r_wb = Wb[u % NP]
                    subs = []
                    for qi in range(2):
                        nq = n0 + qi
                        for part in range(2):
                            subs.append((qi, part, max(nq - part, 0), nq))
                    for si, (qi, part, m, nq) in enumerate(subs):
                        Sx.mm(S_t[:, 2 * qi + part, :], k_t[:, sl(d, r, m)], q_t[:, sl(d, r, nq)], True, True,
                              reads=[r_q, r_k], writes=[r_S], signal=(si == len(subs) - 1))
                    Sx.act(w32[:], S_t[:], AF.Exp, reads=[r_S], writes=[r_w32])
                    Sx.tt("dve", wb[:].rearrange("p (a b) n -> p a (b n)", a=2),
                          w32[:].rearrange("p (a b) n -> p a (b n)", a=2),
                          G_t[:, di, :].unsqueeze(1).broadcast_to([128, 2, 256]), ALU.mult,
                          reads=[r_w32, r_G], writes=[r_wb])

                def N_phase(u):
                    di, d, r, n0 = units[u]
                    v_t, r_v = vds[di]
                    N_t, r_N = pN[u % NP]
                    wb, r_wb = Wb[u % NP]
                    nmm = []
                    for which in range(2):
                        for qi in range(2):
                            nq = n0 + qi
                            parts = [pp for pp in range(2) if nq - pp >= 0]
                            for pi_, part in enumerate(parts):
                                nmm.append((which, qi, part, nq - part, pi_ == 0, pi_ == len(parts) - 1))
                    for ni, (which, qi, part, m, first, last) in enumerate(nmm):
                        lhsT = v_t[:, m * d + r, :] if which == 0 else k.ones[:]
                        Sx.mm(N_t[:, 2 * which + qi, :], lhsT, wb[:, 2 * qi + part, :], first, last,
                              reads=[r_v, r_wb, k.r_ones], writes=[r_N], signal=(ni == len(nmm) - 1))
                    a0 = n0 * 128 * d + r
                    acc_view = acc[:, :, a0:a0 + 255 * d + 1:d]
                    n_view = N_t[:].rearrange("p (w q) n -> p w (q n)", w=2)
                    if di == 0:
                        Sx.copy("act", acc_view, n_view, reads=[r_N], writes=[r_acc])
                    else:
                        Sx.tt("dve", acc_view, acc_view, n_view, ALU.add, reads=[r_N, r_acc], writes=[r_acc])

                nu = len(units)
                for step in range(nu + NP - 1):
                    if step < nu:
                        S_phase(step)
                    if step - (NP - 1) >= 0:
                        N_phase(step - (NP - 1))
                for Q in range(c.NT):
                    q0 = Q * T
                    Sx.act(rden[:], acc[:, 1, q0:q0 + T], AF.Ln, reads=[r_acc], writes=[r_rden])
                    Sx.act(rden[:], rden[:], AF.Exp, reads=[r_rden], writes=[r_rden], scale=-1.0)
                    Sx.tt("dve", osb[:], acc[:, 0, q0:q0 + T], rden[:], ALU.mult, reads=[r_acc, r_rden], writes=[r_osb])
                    on_t, r_on = on[oni % 2]
                    oni += 1
                    head_epilogue(Sx, k, l, HS + h, osb[:], r_osb, q0, T, sqh, r_sqh, psn, r_psn, rstd, r_rstd, on_t, r_on)
            Sx.emit()

    def stage_dense(l):
        st, Sx, sb, ps = new_stage(f"c{l}")
        with st:
            k = load_consts(Sx, sb)
            xsrc = xT_in if l == 0 else xr
            xdst = yT_out if l == DEPTH - 1 else xr
            x_t, _ = sb("xt", [128, KC, T], F32)
            yT, _ = sb("yT", [128, KC, T], F32)
            hT, _ = sb("hT", [128, KC, T], BF16)
            actb, _ = sb("actb", [128, FC, T], BF16)
            r_xc = [Res(f"x{i}") for i in range(KC)]
            r_yc = [Res(f"y{i}") for i in range(KC)]
            r_hc = [Res(f"h{i}") for i in range(KC)]
            r_ac = [Res(f"a{i}") for i in range(FC)]
            sqg = [sb(f"sqg{i}", [128, 4, T], BF16) for i in range(2)]
            rstd, r_rstd = sb("rstd", [128, T], F32)
            ptmp, r_ptmp = sb("ptmp", [128, T], F32)
            WK = max(KC, c.KG)
            wsl = [sb(f"w{i}", [128, WK, 512], BF16) for i in range(3)]
            wpp = [sb(f"wpp{i}", [128, c.PC, 512], BF16) for i in range(2)]
            sg = [sb(f"sg{i}", [128, T], F32) for i in range(2)]
            gtmp = [sb(f"gtmp{i}", [128, T], F32) for i in range(2)]
            pb, r_pb = sb("pb", [128, c.PC, T], BF16)
            psn, r_psn = ps("psn", [128, T])
            banks = [ps(f"pb{i}", [128, 512]) for i in range(7)]
            oT_t = actb[:, 0:NH, :]
            bi = [0]
            ei = [0]
            sqi = [0]
            seq = []
            for tt in range(c.NT):
                seq += [(Wb_out[l, b], NH) for b in range(c.NBO)]
                seq += [(Wb_gu[l, b], KC) for b in range(c.NBG)]
                seq += [(Wb_dn[l, cg, kg], c.KG) for cg in range(c.NCG) for kg in range(c.NKG)]
                seq += [(Wb_pg[l, b], KC) for b in range(c.NBO)]
            ws = WStream(Sx, wsl, seq)
            wps = WStream(Sx, wpp, [(Wb_pp[l, b], c.PC) for tt in range(c.NT) for b in range(c.NBO)])

            def next_bank():
                b = banks[bi[0] % 7]
                bi[0] += 1
                return b

            def evac_eng():
                ei[0] += 1
                return "act" if ei[0] % 2 == 0 else "dve"

            pending = []

            def flush_sq():
                while pending:
                    s_t, r_s, g0, g1 = pending.pop(0)
                    for kc in range(g0, g1):
                        Sx.mm(psn[:], k.ones[:], s_t[:, kc - g0, :], start=(kc == 0), stop=(kc == KC - 1),
                              reads=[k.r_ones, r_s], writes=[r_psn], signal=(kc == g1 - 1))

            def sq_group(src, r_src, g, defer=False):
                g0, g1 = 4 * g, min(KC, 4 * g + 4)
                flush_sq()
                s_t, r_s = sqg[sqi[0] % 2]
                sqi[0] += 1
                Sx.act(s_t[:, 0:g1 - g0, :], src[:, g0:g1, :], AF.Square, reads=r_src[g0:g1], writes=[r_s])
                pending.append((s_t, r_s, g0, g1))
                if not defer:
                    flush_sq()

            def rstd_finish():
                flush_sq()
                Sx.act(rstd[:], psn[:], AF.Ln, reads=[r_psn], writes=[r_rstd], scale=1.0 / D, bias=RMS_EPS)
                Sx.act(rstd[:], rstd[:], AF.Exp, reads=[r_rstd], writes=[r_rstd], scale=-0.5)

            NG4 = (KC + 3) // 4

            def post_norm_residual(which, nxt):
                goff = l * c.NG + which * KC
                goff2 = l * c.NG + nxt * KC
                rstd_finish()
                for kc in range(KC):
                    eng = "pool" if kc % 5 == 2 else "dve"
                    scale_chunk(Sx, eng, yT[:, kc, :], yT[:, kc, :], k.gains[:, goff + kc:goff + kc + 1], rstd[:],
                                ptmp[:], r_ptmp, [r_yc[kc], k.r_gains, r_rstd], [r_yc[kc]])
                    Sx.tt(eng, x_t[:, kc, :], x_t[:, kc, :], yT[:, kc, :], ALU.add, reads=[r_xc[kc], r_yc[kc]],
                          writes=[r_xc[kc]])
                for kc in range(KC):
                    Sx.act(hT[:, kc, :], x_t[:, kc, :], AF.Copy, reads=[r_xc[kc], k.r_gains], writes=[r_hc[kc]],
                           scale=k.gains[:, goff2 + kc:goff2 + kc + 1])
                    if kc % 4 == 3 or kc == KC - 1:
                        sq_group(x_t, r_xc, kc // 4, defer=True)
                rstd_finish()

            def load_x_chunk(tt, cc):
                Sx.dma("sp", x_t[:, cc, :], xsrc[cc * 128:(cc + 1) * 128, tt * T:(tt + 1) * T], writes=[r_xc[cc]])

            def load_oT(tt):
                Sx.dma("sp", oT_t, oT[:, tt * T:(tt + 1) * T].rearrange("(kc p) t -> p kc t", p=128),
                       writes=r_ac[0:NH], sres=r_ac[0])

            def load_p(tt):
                Sx.dma("pool", pb[:], pT_in[l][:, tt * T:(tt + 1) * T].rearrange("(kc p) t -> p kc t", p=128),
                       writes=[r_pb])

            load_oT(0)
            for cc in range(KC):
                load_x_chunk(0, cc)
            for tt in range(c.NT):
                t0 = tt * T
                load_p(tt)
                for b in range(c.NBO):
                    w_t, r_w = ws.next()
                    for j in range(4):
                        p_t, r_p = next_bank()
                        for kc in range(NH):
                            Sx.mm(p_t[:], w_t[:, kc, j * 128:(j + 1) * 128], oT_t[:, kc, :], kc == 0, kc == NH - 1,
                                  reads=[r_w, r_ac[kc]], writes=[r_p])
                        Sx.copy(evac_eng(), yT[:, b * 4 + j, :], p_t[:], reads=[r_p], writes=[r_yc[b * 4 + j]])
                    sq_group(yT, r_yc, b, defer=True)
                post_norm_residual(1, 2)
                for b in range(c.NBG):
                    w_t, r_w = ws.next()
                    for j in range(2):
                        f = b * 2 + j
                        pg_t, r_pg = next_bank()
                        pu_t, r_pu = next_bank()
                        for kc in range(KC):
                            Sx.mm(pg_t[:], w_t[:, kc, j * 128:(j + 1) * 128], hT[:, kc, :], kc == 0, kc == KC - 1,
                                  reads=[r_w, r_hc[kc]], writes=[r_pg])
                        for kc in range(KC):
                            Sx.mm(pu_t[:], w_t[:, kc, 256 + j * 128:256 + (j + 1) * 128], hT[:, kc, :], kc == 0,
                                  kc == KC - 1, reads=[r_w, r_hc[kc]], writes=[r_pu])
                        sg_t, r_sg = sg[f % 2]
                        gt_t, r_gt = gtmp[f % 2]
                        Sx.tt("dve", gt_t[:], pg_t[:], rstd[:], ALU.mult, reads=[r_pg, r_rstd], writes=[r_gt])
                        Sx.act(sg_t[:], gt_t[:], AF.Silu, reads=[r_gt], writes=[r_sg])
                        Sx.tt("dve", gt_t[:], pu_t[:], rstd[:], ALU.mult, reads=[r_pu, r_rstd, r_sg], writes=[r_gt])
                        Sx.tt("dve", actb[:, f, :], sg_t[:], gt_t[:], ALU.mult, reads=[r_sg, r_gt], writes=[r_ac[f]])
                for cg in range(c.NCG):
                    ybanks = [next_bank() for _ in range(4)]
                    for kg in range(c.NKG):
                        w_t, r_w = ws.next()
                        for j in range(4):
                            p_t, r_p = ybanks[j]
                            for kk in range(c.KG):
                                first = (kg == 0 and kk == 0)
                                last = (kg == c.NKG - 1 and kk == c.KG - 1)
                                Sx.mm(p_t[:], w_t[:, kk, j * 128:(j + 1) * 128], actb[:, kg * c.KG + kk, :], first, last,
                                      reads=[r_w, r_ac[kg * c.KG + kk]], writes=[r_p], signal=(kk == c.KG - 1))
                    for j in range(4):
                        p_t, r_p = ybanks[j]
                        Sx.copy(evac_eng(), yT[:, cg * 4 + j, :], p_t[:], reads=[r_p], writes=[r_yc[cg * 4 + j]])
                    sq_group(yT, r_yc, cg, defer=True)
                if tt + 1 < c.NT:
                    load_oT(tt + 1)
                post_norm_residual(3, 4)
                for b in range(c.NBO):
                    w_t, r_w = ws.next()
                    wp_t, r_wp = wps.next()
                    for j in range(4):
                        cc = b * 4 + j
                        pg_t, r_pg = next_bank()
                        pu_t, r_pu = next_bank()
                        for kc in range(KC):
                            Sx.mm(pg_t[:], w_t[:, kc, j * 128:(j + 1) * 128], hT[:, kc, :], kc == 0, kc == KC - 1,
                                  reads=[r_w, r_hc[kc]], writes=[r_pg])
                        for kc in range(c.PC):
                            Sx.mm(pu_t[:], wp_t[:, kc, j * 128:(j + 1) * 128], pb[:, kc, :], kc == 0, kc == c.PC - 1,
                                  reads=[r_wp, r_pb], writes=[r_pu])
                        sg_t, r_sg = sg[j % 2]
                        gt_t, r_gt = gtmp[j % 2]
                        Sx.tt("dve", gt_t[:], pg_t[:], rstd[:], ALU.mult, reads=[r_pg, r_rstd], writes=[r_gt])
                        Sx.act(sg_t[:], gt_t[:], AF.Sigmoid, reads=[r_gt], writes=[r_sg])
                        Sx.tt("dve", yT[:, cc, :], sg_t[:], pu_t[:], ALU.mult, reads=[r_sg, r_pu], writes=[r_yc[cc]])
                        Sx.tt("pool" if j == 3 else "dve", x_t[:, cc, :], x_t[:, cc, :], yT[:, cc, :], ALU.add,
                              reads=[r_xc[cc], r_yc[cc]], writes=[r_xc[cc]])
                        Sx.dma("sp", xdst[cc * 128:(cc + 1) * 128, t0:t0 + T], x_t[:, cc, :], reads=[r_xc[cc]])
                        if tt + 1 < c.NT:
                            load_x_chunk(tt + 1, cc)
            Sx.emit()

    gstack = ExitStack()
    pool = SemPool(nc, gstack)
    if want("cvt"):
        stage_convert([0] if c.overlap_cvt else list(range(DEPTH)))
    if want("bias"):
        stage_bias()
    for l in range(DEPTH):
        if want("qkv"):
            stage_qkv(l)
        if want("sb"):
            stage_sb(l)
        if want("dl"):
            stage_dl(l)
        if want("dense"):
            stage_dense(l)
    gstack.close()
    return nc


def pack_gains(cfg, ln_mix_pre, ln_mix_post, ln_ffn_pre, ln_ffn_post, ln_pli, ln_head):
    cols = []
    for l in range(cfg.DEPTH):
        for g in (ln_mix_pre, ln_mix_post, ln_ffn_pre, ln_ffn_post, ln_pli):
            cols.append(np.asarray(g[l], np.float32).reshape(cfg.KC, 128).T)
        cols.append(np.asarray(ln_head[l], np.float32).reshape(cfg.NH, 128).T)
    return np.ascontiguousarray(np.concatenate(cols, axis=1))


def make_in_maps(cfg, x, p, ln_mix_pre, w_in, ln_head, w_out, ln_mix_post, rel_bias,
                 ln_ffn_pre, w_gate_up, w_down, ln_ffn_post, ln_pli, w_pli_gate, w_pli_proj, n_cores):
    cst, oh = make_consts(cfg)
    gains = pack_gains(cfg, ln_mix_pre, ln_mix_post, ln_ffn_pre, ln_ffn_post, ln_pli, ln_head)
    f = lambda a: np.ascontiguousarray(np.asarray(a, np.float32))
    shared = {
        "w_in": f(w_in), "w_out": f(w_out), "w_gate_up": f(w_gate_up), "w_down": f(w_down),
        "w_pli_gate": f(w_pli_gate), "w_pli_proj": f(w_pli_proj), "gains": gains,
        "rel_bias": f(rel_bias), "cst": cst, "oh": oh,
    }
    x = np.asarray(x, np.float32)
    p = np.asarray(p, np.float32)
    maps = []
    for b in range(n_cores):
        m = dict(shared)
        m["xT"] = np.ascontiguousarray(x[b].T)
        m["pT"] = np.ascontiguousarray(p[:, b].transpose(0, 2, 1))
        maps.append(m)
    return maps


_PROGRAM_CACHE = {}


def kernel(x, p, ln_mix_pre, w_in, ln_head, w_out, ln_mix_post, rel_bias,
           ln_ffn_pre, w_gate_up, w_down, ln_ffn_post, ln_pli, w_pli_gate, w_pli_proj):
    cfg = Cfg()
    n = 8
    in_maps = make_in_maps(cfg, x, p, ln_mix_pre, w_in, ln_head, w_out, ln_mix_post, rel_bias,
                           ln_ffn_pre, w_gate_up, w_down, ln_ffn_post, ln_pli, w_pli_gate, w_pli_proj, n)
    nc = build_program(cfg)
    res = run_bass_kernel_spmd(nc, in_maps, core_ids=list(range(n)))
    out = np.stack([np.asarray(r["yT"], np.float32).T for r in res.results], axis=0)
    return np.ascontiguousarray(out)
```

```python
import math
import os
from contextlib import ExitStack

import numpy as np
import concourse.bass as bass
import concourse.mybir as mybir
from concourse.bass_utils import run_bass_kernel_spmd

F32 = mybir.dt.float32
BF16 = mybir.dt.bfloat16
AF = mybir.ActivationFunctionType
ALU = mybir.AluOpType

RMS_EPS = 1e-6
NEG_BIG = -30000.0


class Res:
    __slots__ = ("name", "writer", "readers", "dsem", "dcount", "dbase")

    def __init__(self, name):
        self.name = name
        self.writer = None
        self.readers = []
        self.dsem = None
        self.dcount = 0
        self.dbase = 0


class _Eng:
    def __init__(self, name, sem):
        self.name = name
        self.sem = sem
        self.count = 0
        self.waited = {}
        self.ops = []


class Sched:
    ENGS = ("pe", "act", "dve", "pool", "sp")

    def __init__(self, nc, pool, tag):
        self.nc = nc
        self.pool = pool
        self.tag = tag
        self.eng = {}
        for e in self.ENGS:
            eo = _Eng(e, pool.eng_sems[e])
            eo.count = pool.eng_counts[e]
            self.eng[e] = eo
        self.dma_res = []

    def _deps(self, ename, reads, writes):
        deps = []
        for r in reads:
            if r.writer is not None:
                deps.append(r.writer)
        for w in writes:
            if w.writer is not None:
                deps.append(w.writer)
            deps.extend(w.readers)
        e = self.eng[ename]
        best = {}
        for (sem, val, src) in deps:
            if src == "pe" and ename == "pe":
                continue
            k = id(sem)
            if e.waited.get(k, 0) >= val:
                continue
            if k not in best or best[k][1] < val:
                best[k] = (sem, val)
        for k, (sem, val) in best.items():
            e.waited[k] = val
        return list(best.values())

    def _record(self, tok, reads, writes):
        for r in reads:
            r.readers.append(tok)
        for w in writes:
            w.writer = tok
            w.readers = []

    def op(self, ename, fn, reads=(), writes=(), signal=True):
        e = self.eng[ename]
        waits = self._deps(ename, reads, writes)
        tok = (e.sem, e.count + 1, ename)
        if signal:
            e.count += 1
        e.ops.append((waits, fn, (e.sem, 1) if signal else None))
        self._record(tok, reads, writes)
        return tok

    def dma(self, qname, out_ap, in_ap, reads=(), writes=(), sres=None):
        e = self.eng[qname]
        waits = self._deps(qname, reads, writes)
        if sres is None:
            sres = writes[0] if writes else reads[0]
        if sres.dsem is None:
            sres.dsem, sres.dbase = self.pool.get()
            self.dma_res.append(sres)
        sres.dcount += 1
        tok = (sres.dsem, sres.dbase + 16 * sres.dcount, "dma")

        def fn(eng, out_ap=out_ap, in_ap=in_ap):
            return eng.dma_start(out=out_ap, in_=in_ap)

        e.ops.append((waits, fn, (sres.dsem, 16)))
        self._record(tok, reads, writes)
        return tok

    def mm(self, out, lhsT, rhs, start, stop, reads, writes, signal=None):
        if signal is None:
            signal = stop
        return self.op("pe", lambda e: e.matmul(out, lhsT, rhs, start=start, stop=stop),
                       reads=reads, writes=writes, signal=signal)

    def act(self, out, in_, func, reads, writes, scale=1.0, bias=0.0):
        return self.op("act", lambda e: e.activation(out=out, in_=in_, func=func, bias=bias, scale=scale),
                       reads=reads, writes=writes)

    def tt(self, eng, out, in0, in1, op, reads, writes):
        return self.op(eng, lambda e: e.tensor_tensor(out, in0, in1, op), reads=reads, writes=writes)

    def copy(self, eng, out, in_, reads, writes, scale=None):
        if eng == "act":
            return self.act(out, in_, AF.Copy, reads, writes, scale=(1.0 if scale is None else scale))
        if scale is None:
            return self.op(eng, lambda e: e.tensor_copy(out, in_), reads=reads, writes=writes)
        return self.op(eng, lambda e: e.tensor_scalar_mul(out, in_, scale), reads=reads, writes=writes)

    def emit(self):
        nc = self.nc
        finals = [(r.dsem, r.dbase + 16 * r.dcount) for r in self.dma_res]
        engs = self.eng

        def run(engobj, e):
            for waits, fn, inc in e.ops:
                for sem, val in waits:
                    engobj.wait_ge(sem, val)
                ins = fn(engobj)
                if inc is not None:
                    ins.then_inc(inc[0], inc[1])

        with nc.Block() as block:
            @block.tensor
            def _(t):
                run(t, engs["pe"])

            @block.scalar
            def _(s):
                run(s, engs["act"])

            @block.vector
            def _(v):
                run(v, engs["dve"])

            @block.gpsimd
            def _(g):
                run(g, engs["pool"])

            @block.sync
            def _(s):
                run(s, engs["sp"])
                for sem, val in finals:
                    s.wait_ge(sem, val)

        for e in self.ENGS:
            self.pool.eng_counts[e] = engs[e].count
        for r in self.dma_res:
            self.pool.put(r.dsem, r.dbase + 16 * r.dcount)


class SemPool:
    def __init__(self, nc, stack):
        self.nc = nc
        self.stack = stack
        self.eng_sems = {e: stack.enter_context(nc.semaphore(f"eng_{e}")) for e in Sched.ENGS}
        self.eng_counts = {e: 0 for e in Sched.ENGS}
        self.free = []
        self.n = 0

    def get(self):
        if self.free:
            return self.free.pop()
        sem = self.stack.enter_context(self.nc.semaphore(f"dsem_{self.n}"))
        self.n += 1
        return sem, 0

    def put(self, sem, value):
        self.free.append((sem, value))


class Cfg:
    def __init__(self, D=2048, S=4096, DEPTH=4, HS=8, HD=8, DFF=5632, PLI=256, debug=False,
                 stages=None):
        self.D, self.S, self.DEPTH, self.HS, self.HD, self.DFF, self.PLI = D, S, DEPTH, HS, HD, DFF, PLI
        self.T = 512
        self.KC = D // 128
        self.NH = HS + HD
        self.MIXW = self.NH * 128
        assert self.MIXW == D
        self.INW = 3 * self.MIXW
        self.FC = DFF // 128
        self.NT = S // self.T
        self.NBLK = S // 128
        self.NBI = self.INW // 512
        self.NBO = D // 512
        self.NBG = self.FC // 2
        self.KG = 11
        self.NKG = self.FC // self.KG
        assert self.NKG * self.KG == self.FC
        self.NCG = D // 512
        self.NBS = HS * 128 // 512
        self.NBD = HD * 128 // 512
        self.PC = PLI // 128
        self.NG = 5 * self.KC + self.NH
        self.debug = debug
        self.overlap_cvt = os.environ.get("K_OVERLAP", "sb")
        self.stages = stages
        self.SCALE = 1.0 / math.sqrt(128.0)
        self.DIL = ((128, 1), (512, 4), (2048, 16))


def t5_bucket_np(dist):
    num_buckets, max_distance = 32, 2048
    max_exact = num_buckets // 2
    dist = np.asarray(dist, dtype=np.int32)
    d = np.maximum(dist, 1).astype(np.float32)
    large = max_exact + (np.log(d / np.float32(max_exact)) / np.float32(math.log(max_distance / max_exact))
                         * np.float32(num_buckets - max_exact)).astype(np.int32)
    large = np.minimum(large, num_buckets - 1)
    return np.where(dist < max_exact, dist, large)


def make_consts(cfg):
    jp = np.arange(128)[:, None]
    j = np.arange(128)[None, :]
    tri_neg = np.where(jp >= j, -1.0, 0.0).astype(np.float32)
    masks = np.zeros((128, 4, 512), np.float32)
    t = np.arange(512)[None, :]
    for r in range(4):
        masks[:, r, :] = ((r * 128 + np.arange(128)[:, None]) < t).astype(np.float32)
    J = np.zeros((128, 128), np.float32)
    J[np.arange(128), 127 - np.arange(128)] = 1.0
    cst = np.concatenate([tri_neg, masks.reshape(128, 2048), J], axis=1)
    oh = np.zeros((33, 3, 384), np.float32)
    for di, (window, dil) in enumerate(cfg.DIL):
        n_back = window // dil
        for u in range(384):
            delta = u - 127
            if 0 <= delta <= n_back:
                b = int(t5_bucket_np(np.array([delta * dil]))[0])
                oh[b, di, u] = 1.0
            else:
                oh[32, di, u] = 1.0
    return cst, oh.reshape(33, 3 * 384)


def build_program(cfg):
    nc = bass.Bass("TRN2", target_bir_lowering=False)
    c = cfg
    D, S, T, KC, NH, HS, HD, FC = c.D, c.S, c.T, c.KC, c.NH, c.HS, c.HD, c.FC
    DEPTH = c.DEPTH

    def din(name, shape, dt=F32):
        return nc.dram_tensor(name, list(shape), dt, kind="ExternalInput").ap()

    def dtmp(name, shape, dt):
        kind = "ExternalOutput" if c.debug else "Internal"
        return nc.dram_tensor(name, list(shape), dt, kind=kind).ap()

    xT_in = din("xT", [D, S])
    pT_in = din("pT", [DEPTH, c.PLI, S])
    w_in = din("w_in", [DEPTH, D, c.INW])
    w_out = din("w_out", [DEPTH, c.MIXW, D])
    w_gu = din("w_gate_up", [DEPTH, D, 2 * c.DFF])
    w_dn = din("w_down", [DEPTH, c.DFF, D])
    w_pg = din("w_pli_gate", [DEPTH, D, D])
    w_pp = din("w_pli_proj", [DEPTH, c.PLI, D])
    gains_in = din("gains", [128, DEPTH * c.NG])
    relb_in = din("rel_bias", [32, HD])
    cst_in = din("cst", [128, 128 + 2048 + 128])
    oh_in = din("oh", [33, 3 * 384])
    yT_out = nc.dram_tensor("yT", [D, S], F32, kind="ExternalOutput").ap()

    Wb_in = dtmp("Wb_in", [DEPTH, c.NBI, 128, KC, 512], BF16)
    Wb_out = dtmp("Wb_out", [DEPTH, c.NBO, 128, NH, 512], BF16)
    Wb_gu = dtmp("Wb_gu", [DEPTH, c.NBG, 128, KC, 512], BF16)
    Wb_dn = dtmp("Wb_dn", [DEPTH, c.NCG, c.NKG, 128, c.KG, 512], BF16)
    Wb_pg = dtmp("Wb_pg", [DEPTH, c.NBO, 128, KC, 512], BF16)
    Wb_pp = dtmp("Wb_pp", [DEPTH, c.NBO, 128, c.PC, 512], BF16)
    xr = dtmp("xr", [D, S], F32)
    qT_sb = dtmp("qT_sb", [HS, 128, S], BF16)
    kT_sb = dtmp("kT_sb", [HS, 128, S], BF16)
    v_sb = dtmp("v_sb", [S, HS * 128], BF16)
    qT_dl = dtmp("qT_dl", [HD, 128, S], BF16)
    kT_dl = dtmp("kT_dl", [HD, 128, S], BF16)
    v_dl = dtmp("v_dl", [S, HD * 128], BF16)
    oT = dtmp("oT", [c.MIXW, S], BF16)
    gp = dtmp("gp", [HD, 3, 384], F32)
    Gmat = dtmp("Gmat", [HD, 128, 3, 256], F32)

    uid = [0]

    def want(stage):
        return c.stages is None or stage in c.stages

    def emit_convert(Sx, l, dres, throttle=0, lazy=None):
        dummies = [Res(f"cvthr{i}") for i in range(max(throttle, 1))]
        cnt = [0]

        def cdma(out_ap, in_ap, sres):
            def go(out_ap=out_ap, in_ap=in_ap, sres=sres):
                w = [dummies[cnt[0] % len(dummies)]] if throttle else []
                cnt[0] += 1
                Sx.dma("pool", out_ap, in_ap, writes=w, sres=sres)
            if lazy is None:
                go()
            else:
                lazy.append(go)

        for b in range(c.NBI):
            cdma(Wb_in[l, b],
                   w_in[l][:, b * 512:(b + 1) * 512].rearrange("(kc p) n -> p kc n", p=128), sres=dres)
        for b in range(c.NBO):
            cdma(Wb_out[l, b],
                   w_out[l][:, b * 512:(b + 1) * 512].rearrange("(kc p) n -> p kc n", p=128), sres=dres)
        for b in range(c.NBG):
            cdma(Wb_gu[l, b][:, :, 0:256],
                   w_gu[l][:, b * 256:(b + 1) * 256].rearrange("(kc p) n -> p kc n", p=128), sres=dres)
            cdma(Wb_gu[l, b][:, :, 256:512],
                   w_gu[l][:, c.DFF + b * 256:c.DFF + (b + 1) * 256].rearrange("(kc p) n -> p kc n", p=128),
                   sres=dres)
        for cg in range(c.NCG):
            for kg in range(c.NKG):
                cdma(Wb_dn[l, cg, kg],
                       w_dn[l][kg * c.KG * 128:(kg + 1) * c.KG * 128, cg * 512:(cg + 1) * 512]
                       .rearrange("(kc p) n -> p kc n", p=128), sres=dres)
        for b in range(c.NBO):
            cdma(Wb_pg[l, b],
                   w_pg[l][:, b * 512:(b + 1) * 512].rearrange("(kc p) n -> p kc n", p=128), sres=dres)
            cdma(Wb_pp[l, b],
                   w_pp[l][:, b * 512:(b + 1) * 512].rearrange("(kc p) n -> p kc n", p=128), sres=dres)

    def stage_convert(layers):
        with ExitStack() as st:
            Sx = Sched(nc, pool, f"cv{layers[0]}")
            dres = Res("cvt")
            for l in layers:
                emit_convert(Sx, l, dres)
            Sx.emit()

    class Ctx:
        pass

    def new_stage(tag):
        st = ExitStack()
        Sx = Sched(nc, pool, tag)
        uid[0] += 1
        u = uid[0]

        def sb(name, shape, dt):
            t = st.enter_context(nc.sbuf_tensor(f"{name}_{u}", list(shape), dt))
            return t, Res(name)

        def ps(name, shape, dt=F32):
            t = st.enter_context(nc.psum_tensor(f"{name}_{u}", list(shape), dt))
            return t, Res(name)

        return st, Sx, sb, ps

    def load_consts(Sx, sb):
        k = Ctx()
        k.ones, k.r_ones = sb("ones", [128, 128], BF16)
        k.gains, k.r_gains = sb("gains", [128, DEPTH * c.NG], F32)
        Sx.op("pool", lambda e: e.memset(k.ones[:], 1.0), writes=[k.r_ones])
        Sx.dma("sp", k.gains[:], gains_in, writes=[k.r_gains])
        return k

    def emit_rstd(Sx, k, src3, nchunks, n_feat, sq, r_sq, src_res, psn, r_psn, rstd, r_rstd, width):
        for g0 in range(0, nchunks, 4):
            g1 = min(nchunks, g0 + 4)
            Sx.act(sq[:, g0:g1, 0:width], src3[:, g0:g1, :], AF.Square, reads=[src_res], writes=[r_sq])
        for kc in range(nchunks):
            Sx.mm(psn[:, 0:width], k.ones[:], sq[:, kc, 0:width], start=(kc == 0), stop=(kc == nchunks - 1),
                  reads=[k.r_ones, r_sq], writes=[r_psn])
        Sx.act(rstd[:, 0:width], psn[:, 0:width], AF.Ln, reads=[r_psn], writes=[r_rstd],
               scale=1.0 / n_feat, bias=RMS_EPS)
        Sx.act(rstd[:, 0:width], rstd[:, 0:width], AF.Exp, reads=[r_rstd], writes=[r_rstd], scale=-0.5)

    def scale_chunk(Sx, eng, out, in_, gain_ap, rstd_ap, tmp, r_tmp, reads, writes):
        if eng == "dve":
            Sx.op("dve", lambda e: e.scalar_tensor_tensor(out, in_, gain_ap, rstd_ap, ALU.mult, ALU.mult),
                  reads=reads, writes=writes)
        else:
            Sx.tt("pool", tmp, in_, rstd_ap, ALU.mult, reads=reads, writes=[r_tmp])
            Sx.op("pool", lambda e: e.tensor_scalar_mul(out, tmp, gain_ap), reads=[r_tmp] + list(reads), writes=writes)

    def stage_bias():
        st, Sx, sb, ps = new_stage("gb")
        with st:
            rb, r_rb = sb("rb", [33, HD], F32)
            oh, r_oh = sb("oh", [33, 3 * 384], F32)
            gsb, r_gsb = sb("gsb", [HD, 3 * 384], F32)
            pg = [ps(f"pg{i}", [HD, 512]) for i in range(3)]
            Sx.op("pool", lambda e: e.memset(rb[32:33, :], NEG_BIG), writes=[r_rb])
            Sx.dma("sp", rb[0:32, :], relb_in, writes=[r_rb])
            Sx.dma("sp", oh[:], oh_in, writes=[r_oh])
            for di in range(3):
                Sx.mm(pg[di][0][:, 0:384], rb[:], oh[:, di * 384:(di + 1) * 384], True, True,
                      reads=[r_rb, r_oh], writes=[pg[di][1]])
                Sx.copy("dve", gsb[:, di * 384:(di + 1) * 384], pg[di][0][:, 0:384], reads=[pg[di][1]], writes=[r_gsb])
            r_gp = Res("gp_dram")
            Sx.dma("sp", gp.rearrange("h d u -> h (d u)"), gsb[:], reads=[r_gsb], writes=[r_gp], sres=r_gsb)
            Jm, r_J = sb("Jm", [128, 128], F32)
            Sx.dma("sp", Jm[:], cst_in[:, 128 + 2048:128 + 2048 + 128], writes=[r_J])
            brev = [sb(f"brev{i}", [128, 256], F32) for i in range(2)]
            gout = [sb(f"gout{i}", [128, 3, 256], F32) for i in range(2)]
            pt = [ps(f"pt{i}", [128, 512]) for i in range(2)]
            n_ = 0
            for h in range(HD):
                go_t, r_go = gout[h % 2]
                for di in range(3):
                    b_t, r_b = brev[n_ % 2]
                    p_t, r_p = pt[n_ % 2]
                    n_ += 1
                    src = bass.AP(gp.tensor, (h * 3 + di) * 384, [[1, 128], [1, 256]])
                    Sx.dma("sp", b_t[:], src, reads=[r_gp], writes=[r_b])
                    Sx.mm(p_t[:, 0:256], Jm[:], b_t[:], True, True, reads=[r_J, r_b], writes=[r_p])
                    Sx.act(go_t[:, di, :], p_t[:, 0:256], AF.Exp, reads=[r_p], writes=[r_go])
                Sx.dma("sp", Gmat[h], go_t[:], reads=[r_go])
            Sx.emit()

    class WStream:
        def __init__(self, Sx, slots, seq, queue="sp"):
            self.Sx, self.slots, self.seq, self.queue = Sx, slots, seq, queue
            self.issued = 0
            self.cur = 0

        def _issue_upto(self, j):
            n = len(self.slots)
            while self.issued <= min(j, len(self.seq) - 1):
                i = self.issued
                t, r = self.slots[i % n]
                src, nk = self.seq[i]
                self.Sx.dma(self.queue, t[:, 0:nk, :], src, writes=[r])
                self.issued += 1

        def next(self):
            i = self.cur
            self._issue_upto(i + len(self.slots) - 1)
            self.cur += 1
            return self.slots[i % len(self.slots)]

    def stage_qkv(l):
        st, Sx, sb, ps = new_stage(f"a{l}")
        with st:
            k = load_consts(Sx, sb)
            xsrc = xT_in if l == 0 else xr
            xt = [sb(f"xt{i}", [128, KC, T], F32) for i in range(2)]
            sqs = [sb(f"sq{i}", [128, KC, T], BF16) for i in range(2)]
            hTs = [sb(f"hT{i}", [128, KC, T], BF16) for i in range(2)]
            rstds = [sb(f"rstd{i}", [128, T], F32) for i in range(2)]
            wsl = [sb(f"w{i}", [128, KC, 512], BF16) for i in range(3)]
            ev = [sb(f"ev{i}", [128, 4, 512], BF16) for i in range(4)]
            psn, r_psn = ps("psn", [128, T])
            pacc = [ps(f"pa{i}", [128, 512]) for i in range(6)]
            goff = l * c.NG
            ws = WStream(Sx, wsl, [(Wb_in[l, b], KC) for tt in range(c.NT) for b in range(c.NBI)])
            evi = 0
            pi = 0

            def load_x(tt):
                x_t, r_x = xt[tt % 2]
                Sx.dma("sp", x_t[:], xsrc[:, tt * T:(tt + 1) * T].rearrange("(kc p) t -> p kc t", p=128),
                       writes=[r_x])

            def norm(tt):
                x_t, r_x = xt[tt % 2]
                sq, r_sq = sqs[tt % 2]
                hT, r_hT = hTs[tt % 2]
                rstd, r_rstd = rstds[tt % 2]
                emit_rstd(Sx, k, x_t[:], KC, D, sq, r_sq, r_x, psn, r_psn, rstd, r_rstd, T)
                for kc in range(KC):
                    scale_chunk(Sx, "dve", hT[:, kc, :], x_t[:, kc, :], k.gains[:, goff + kc:goff + kc + 1], rstd[:],
                                None, None, [r_x, k.r_gains, r_rstd], [r_hT])

            if c.overlap_cvt == "qkv" and l + 1 < DEPTH and want("cvt"):
                emit_convert(Sx, l + 1, Res("cvt_next"), throttle=4)
            load_x(0)
            if c.NT > 1:
                load_x(1)
            norm(0)
            for tt in range(c.NT):
                t0 = tt * T
                hT, r_hT = hTs[tt % 2]
                for b in range(c.NBI):
                    w_t, r_w = ws.next()
                    if b == 2 and tt + 1 < c.NT:
                        norm(tt + 1)
                        if tt + 2 < c.NT:
                            load_x(tt + 2)
                    if b < 3 * c.NBS:
                        role, rbk = ("q", "k", "v")[b // c.NBS], b % c.NBS
                        dq, dk, dv = qT_sb, kT_sb, v_sb
                    else:
                        bb = b - 3 * c.NBS
                        role, rbk = ("q", "k", "v")[bb // c.NBD], bb % c.NBD
                        dq, dk, dv = qT_dl, kT_dl, v_dl
                    e_t, r_e = ev[evi % 4]
                    evi += 1
                    for j in range(4):
                        p_t, r_p = pacc[pi % 6]
                        pi += 1
                        for kc in range(KC):
                            if role == "v":
                                Sx.mm(p_t[:], hT[:, kc, j * 128:(j + 1) * 128], w_t[:, kc, :],
                                      kc == 0, kc == KC - 1, reads=[r_hT, r_w], writes=[r_p])
                            else:
                                Sx.mm(p_t[:], w_t[:, kc, j * 128:(j + 1) * 128], hT[:, kc, :],
                                      kc == 0, kc == KC - 1, reads=[r_hT, r_w], writes=[r_p])
                        eng = "act" if j % 2 == 0 else "dve"
                        Sx.copy(eng, e_t[:, j, :], p_t[:], reads=[r_p], writes=[r_e],
                                scale=(c.SCALE if role == "q" else None))
                    if role == "v":
                        Sx.dma("sp", dv[t0:t0 + T, rbk * 512:(rbk + 1) * 512].rearrange("(tb p) n -> p tb n", p=128),
                               e_t[:], reads=[r_e])
                    else:
                        dst = dq if role == "q" else dk
                        Sx.dma("sp", dst[rbk * 4:(rbk + 1) * 4, :, t0:t0 + T].rearrange("h p t -> p h t"),
                               e_t[:], reads=[r_e])
            Sx.emit()

    def head_epilogue(Sx, k, l, hidx, o_ap, r_o, q0, W, sqh, r_sqh, psn, r_psn, rstd, r_rstd, on, r_on):
        emit_rstd(Sx, k, o_ap.unsqueeze(1), 1, 128, sqh, r_sqh, r_o, psn, r_psn, rstd, r_rstd, W)
        gcol = l * c.NG + 5 * KC + hidx
        Sx.op("dve", lambda e: e.scalar_tensor_tensor(on[:, 0:W], o_ap, k.gains[:, gcol:gcol + 1], rstd[:, 0:W],
                                                      ALU.mult, ALU.mult),
              reads=[r_o, k.r_gains, r_rstd], writes=[r_on])
        Sx.dma("sp", oT[hidx * 128:(hidx + 1) * 128, q0:q0 + W], on[:, 0:W], reads=[r_on])

    def stage_sb(l):
        st, Sx, sb, ps = new_stage(f"s{l}")
        with st:
            k = load_consts(Sx, sb)
            negones, r_negones = sb("negones", [128, 128], BF16)
            trineg, r_trineg = sb("trineg", [128, 128], BF16)
            masks, r_masks = sb("masks", [128, 4, 512], BF16)
            Sx.op("pool", lambda e: e.memset(negones[:], -1.0), writes=[r_negones])
            Sx.dma("pool", trineg[:], cst_in[:, 0:128], writes=[r_trineg])
            Sx.dma("pool", masks[:], cst_in[:, 128:128 + 2048].rearrange("p (r t) -> p r t", r=4), writes=[r_masks])
            NB = c.NBLK
            qk = [(sb(f"q{i}", [128, S], BF16), sb(f"k{i}", [128, S], BF16), sb(f"v{i}", [128, NB, 128], BF16))
                  for i in range(2)]
            NR = 5
            e32 = [sb(f"e32_{i}", [128, 512], F32) for i in range(2)]
            spb = [sb(f"sp{i}", [128, 512], BF16) for i in range(NR)]
            aT = [sb(f"aT{i}", [128, 512], BF16) for i in range(NR)]
            Rb = [sb(f"Rb{i}", [128, 512], F32) for i in range(2)]
            Rb16 = [sb(f"Rb16_{i}", [128, 512], BF16) for i in range(2)]
            negones32, r_negones32 = sb("negones32", [128, 128], F32)
            Sx.op("pool", lambda e: e.memset(negones32[:], -1.0), writes=[r_negones32])
            osb, r_osb = sb("osb", [128, 512], F32)
            sqh, r_sqh = sb("sqh", [128, 1, 512], BF16)
            rstd, r_rstd = sb("rstd", [128, 512], F32)
            on = [sb(f"on{i}", [128, 512], BF16) for i in range(2)]
            pA = [ps(f"pA{i}", [128, 512]) for i in range(NR)]
            pO = [ps(f"pO{i}", [128, 512]) for i in range(2)]
            psn, r_psn = ps("psn", [128, 512])
            oni = 0
            gq = 0
            def load_head(h):
                (q_t, r_q), (k_t, r_k), (v_t, r_v) = qk[h % 2]
                Sx.dma("sp", q_t[:], qT_sb[h], writes=[r_q])
                Sx.dma("sp", k_t[:], kT_sb[h], writes=[r_k])
                Sx.dma("sp", v_t[:], v_sb[:, h * 128:(h + 1) * 128].rearrange("(b p) e -> p b e", p=128),
                       writes=[r_v])

            cv_jobs = []
            if c.overlap_cvt == "sb" and l + 1 < DEPTH and want("cvt"):
                emit_convert(Sx, l + 1, Res("cvt_next"), throttle=2, lazy=cv_jobs)
            n_items_total = HS * sum(4 * Q + 4 for Q in range(c.NT))
            cv_every = max(1, n_items_total // (len(cv_jobs) + 1)) if cv_jobs else 0
            cv_step = [0]
            load_head(0)
            for h in range(HS):
                (q_t, r_q), (k_t, r_k), (v_t, r_v) = qk[h % 2]
                if h + 1 < HS:
                    load_head(h + 1)
                items = []
                for Q in range(c.NT):
                    kbs = list(range(4 * Q + 3, -1, -1))
                    for i, kb in enumerate(kbs):
                        items.append((Q, i, kb, len(kbs)))
                G = len(items)

                def Zs(g):
                    Q, i, kb, n = items[g]
                    A, r_A = pA[g % NR]
                    Sx.mm(A[:], k_t[:, kb * 128:(kb + 1) * 128], q_t[:, Q * T:(Q + 1) * T], True, False,
                          reads=[r_q, r_k], writes=[r_A], signal=True)

                def Es_exp(g):
                    Q, i, kb, n = items[g]
                    A, r_A = pA[g % NR]
                    e_t, r_e = e32[g % 2]
                    Sx.act(e_t[:], A[:], AF.Exp, reads=[r_A], writes=[r_e])

                def Es(g):
                    Q, i, kb, n = items[g]
                    e_t, r_e = e32[g % 2]
                    s_t, r_s = spb[g % NR]
                    Sx.act(s_t[:], e_t[:], AF.Ln, reads=[r_e], writes=[r_s], bias=1.0)
                    if kb >= 4 * Q:
                        r = kb - 4 * Q
                        w_ = (r + 1) * 128
                        Sx.tt("pool", s_t[:, 0:w_], s_t[:, 0:w_], masks[:, r, 0:w_], ALU.mult,
                              reads=[r_s, r_masks], writes=[r_s])

                def Cs(g):
                    Q, i, kb, n = items[g]
                    A, r_A = pA[g % NR]
                    s_t, r_s = spb[g % NR]
                    last = (i == 0)
                    Sx.mm(A[:], trineg[:], s_t[:], False, last, reads=[r_trineg, r_s], writes=[r_A], signal=last)
                    if i > 0:
                        R_t, r_Rt = Rb[g % 2]
                        sp_prev, r_sp_prev = spb[(g - 1) % NR]
                        if i == 1:
                            Sx.copy("dve", R_t[:], sp_prev[:], reads=[r_sp_prev], writes=[r_Rt])
                            Sx.mm(A[:], negones[:], sp_prev[:], False, True, reads=[r_negones, r_sp_prev],
                                  writes=[r_A], signal=True)
                        else:
                            R_p, r_Rp = Rb[(g - 1) % 2]
                            R16, r_R16 = Rb16[g % 2]
                            Sx.tt("dve", R_t[:], R_p[:], sp_prev[:], ALU.add, reads=[r_Rp, r_sp_prev], writes=[r_Rt])
                            Sx.copy("dve", R16[:], R_t[:], reads=[r_Rt], writes=[r_R16])
                            Sx.mm(A[:], negones[:], R16[:], False, True, reads=[r_negones, r_R16], writes=[r_A],
                                  signal=True)

                def Xs(g):
                    Q, i, kb, n = items[g]
                    A, r_A = pA[g % NR]
                    a_t, r_a = aT[g % NR]
                    Sx.act(a_t[:], A[:], AF.Exp, reads=[r_A], writes=[r_a])
                    if kb >= 4 * Q:
                        r = kb - 4 * Q
                        w_ = (r + 1) * 128
                        Sx.tt("pool", a_t[:, 0:w_], a_t[:, 0:w_], masks[:, r, 0:w_], ALU.mult,
                              reads=[r_a, r_masks], writes=[r_a])

                def Vs(g):
                    nonlocal oni
                    Q, i, kb, n = items[g]
                    a_t, r_a = aT[g % NR]
                    po_t, r_po = pO[Q % 2]
                    Sx.mm(po_t[:], v_t[:, kb, :], a_t[:], i == 0, i == n - 1, reads=[r_v, r_a], writes=[r_po],
                          signal=(i == n - 1))
                    if i == n - 1:
                        Sx.copy("dve", osb[:], po_t[:], reads=[r_po], writes=[r_osb])
                        on_t, r_on = on[oni % 2]
                        oni += 1
                        head_epilogue(Sx, k, l, h, osb[:], r_osb, Q * T, T, sqh, r_sqh, psn, r_psn, rstd, r_rstd,
                                      on_t, r_on)

                for step in range(G + 3):
                    if step < G:
                        Zs(step)
                        Es_exp(step)
                    if 0 <= step - 2 < G:
                        Cs(step - 2)
                        Xs(step - 2)
                    if step < G:
                        Es(step)
                    if 0 <= step - 3 < G:
                        Vs(step - 3)
                    if cv_jobs:
                        cv_step[0] += 1
                        if cv_step[0] % cv_every == 0:
                            cv_jobs.pop(0)()
            while cv_jobs:
                cv_jobs.pop(0)()
            Sx.emit()

    def stage_dl(l):
        st, Sx, sb, ps = new_stage(f"d{l}")
        with st:
            k = load_consts(Sx, sb)
            NB = c.NBLK
            qk = [(sb(f"q{i}", [128, S], BF16), sb(f"k{i}", [128, S], BF16),
                   [sb(f"v{i}_{di}", [128, NB, 128], BF16) for di in range(3)],
                   sb(f"G{i}", [128, 3, 256], F32))
                  for i in range(2)]
            acc, r_acc = sb("acc", [128, 2, S], F32)
            W32 = [sb(f"W32_{i}", [128, 4, 128], F32) for i in range(3)]
            Wb = [sb(f"Wb{i}", [128, 4, 128], BF16) for i in range(3)]
            osb, r_osb = sb("osb", [128, 512], F32)
            rden, r_rden = sb("rden", [128, 512], F32)
            sqh, r_sqh = sb("sqh", [128, 1, 512], BF16)
            rstd, r_rstd = sb("rstd", [128, 512], F32)
            on = [sb(f"on{i}", [128, 512], BF16) for i in range(2)]
            pS = [ps(f"pS{i}", [128, 4, 128]) for i in range(3)]
            pN = [ps(f"pN{i}", [128, 4, 128]) for i in range(3)]
            psn, r_psn = ps("psn", [128, 512])
            ui = 0
            oni = 0
            def load_head(h):
                (q_t, r_q), (k_t, r_k), vds, (G_t, r_G) = qk[h % 2]
                Sx.dma("sp", q_t[:], qT_dl[h], writes=[r_q])
                Sx.dma("sp", k_t[:], kT_dl[h], writes=[r_k])
                Sx.dma("sp", G_t[:], Gmat[h], writes=[r_G])
                for di, (window, d) in enumerate(c.DIL):
                    v_t, r_v = vds[di]
                    nsub = S // (128 * d)
                    if d == 1:
                        Sx.dma("sp", v_t[:], v_dl[:, h * 128:(h + 1) * 128].rearrange("(n i) e -> i n e", i=128),
                               writes=[r_v])
                    else:
                        for n_ in range(nsub):
                            Sx.dma("sp", v_t[:, n_ * d:(n_ + 1) * d, :],
                                   v_dl[n_ * 128 * d:(n_ + 1) * 128 * d, h * 128:(h + 1) * 128]
                                   .rearrange("(i r) e -> i r e", r=d), writes=[r_v])

            load_head(0)
            for h in range(HD):
                (q_t, r_q), (k_t, r_k), vds, (G_t, r_G) = qk[h % 2]
                if h + 1 < HD:
                    load_head(h + 1)
                units = [(di, d, r, n0) for di, (window, d) in enumerate(c.DIL)
                         for r in range(d) for n0 in range(0, S // (128 * d), 2)]
                NP = 3

                def sl(d, r, m):
                    a_ = m * 128 * d + r
                    return slice(a_, a_ + 127 * d + 1, d)

                def S_phase(u):
                    di, d, r, n0 = units[u]
                    S_t, r_S = pS[u % NP]
                    w32, r_w32 = W32[u % NP]
                    wb, r_wb = Wb[u % NP]
                    subs = []
                    for qi in range(2):
                        nq = n0 + qi
                        for part in range(2):
                            subs.append((qi, part, max(nq - part, 0), nq))
                    for si, (qi, part, m, nq) in enumerate(subs):
                        Sx.mm(S_t[:, 2 * qi + part, :], k_t[:, sl(d, r, m)], q_t[:, sl(d, r, nq)], True, True,
                              reads=[r_q, r_k], writes=[r_S], signal=(si == len(subs) - 1))
                    Sx.act(w32[:], S_t[:], AF.Exp, reads=[r_S], writes=[r_w32])
                    Sx.tt("dve", wb[:].rearrange("p (a b) n -> p a (b n)", a=2),
                          w32[:].rearrange("p (a b) n -> p a (b n)", a=2),
                          G_t[:, di, :].unsqueeze(1).broadcast_to([128, 2, 256]), ALU.mult,
                          reads=[r_w32, r_G], writes=[r_wb])

                def N_phase(u):
                    di, d, r, n0 = units[u]
                    v_t, r_v = vds[di]
                    N_t, r_N = pN[u % NP]
                    wb, r_wb = Wb[u % NP]
                    nmm = []
                    for which in range(2):
                        for qi in range(2):
                            nq = n0 + qi
                            parts = [pp for pp in range(2) if nq - pp >= 0]
                            for pi_, part in enumerate(parts):
                                nmm.append((which, qi, part, nq - part, pi_ == 0, pi_ == len(parts) - 1))
                    for ni, (which, qi, part, m, first, last) in enumerate(nmm):
                        lhsT = v_t[:, m * d + r, :] if which == 0 else k.ones[:]
                        Sx.mm(N_t[:, 2 * which + qi, :], lhsT, wb[:, 2 * qi + part, :], first, last,
                              reads=[r_v, r_wb, k.r_ones], writes=[r_N], signal=(ni == len(nmm) - 1))
                    a0 = n0 * 128 * d + r
                    acc_view = acc[:, :, a0:a0 + 255 * d + 1:d]
                    n_view = N_t[:].rearrange("p (w q) n -> p w (q n)", w=2)
                    if di == 0:
                        Sx.copy("act", acc_view, n_view, reads=[r_N], writes=[r_acc])
                    else:
                        Sx.tt("dve", acc_view, acc_view, n_view, ALU.add, reads=[r_N, r_acc], writes=[r_acc])

                nu = len(units)
                for step in range(nu + NP - 1):
                    if step < nu:
                        S_phase(step)
                    if step - (NP - 1) >= 0:
                        N_phase(step - (NP - 1))
                for Q in range(c.NT):
                    q0 = Q * T
                    Sx.act(rden[:], acc[:, 1, q0:q0 + T], AF.Ln, reads=[r_acc], writes=[r_rden])
                    Sx.act(rden[:], rden[:], AF.Exp, reads=[r_rden], writes=[r_rden], scale=-1.0)
                    Sx.tt("dve", osb[:], acc[:, 0, q0:q0 + T], rden[:], ALU.mult, reads=[r_acc, r_rden], writes=[r_osb])
                    on_t, r_on = on[oni % 2]
                    oni += 1
                    head_epilogue(Sx, k, l, HS + h, osb[:], r_osb, q0, T, sqh, r_sqh, psn, r_psn, rstd, r_rstd, on_t, r_on)
            Sx.emit()

    def stage_dense(l):
        st, Sx, sb, ps = new_stage(f"c{l}")
        with st:
            k = load_consts(Sx, sb)
            xsrc = xT_in if l == 0 else xr
            xdst = yT_out if l == DEPTH - 1 else xr
            x_t, _ = sb("xt", [128, KC, T], F32)
            yT, _ = sb("yT", [128, KC, T], F32)
            hT, _ = sb("hT", [128, KC, T], BF16)
            actb, _ = sb("actb", [128, FC, T], BF16)
            r_xc = [Res(f"x{i}") for i in range(KC)]
            r_yc = [Res(f"y{i}") for i in range(KC)]
            r_hc = [Res(f"h{i}") for i in range(KC)]
            r_ac = [Res(f"a{i}") for i in range(FC)]
            sqg = [sb(f"sqg{i}", [128, 4, T], BF16) for i in range(2)]
            rstd, r_rstd = sb("rstd", [128, T], F32)
            ptmp, r_ptmp = sb("ptmp", [128, T], F32)
            WK = max(KC, c.KG)
            wsl = [sb(f"w{i}", [128, WK, 512], BF16) for i in range(3)]
            wpp = [sb(f"wpp{i}", [128, c.PC, 512], BF16) for i in range(2)]
            sg = [sb(f"sg{i}", [128, T], F32) for i in range(2)]
            gtmp = [sb(f"gtmp{i}", [128, T], F32) for i in range(2)]
            pb, r_pb = sb("pb", [128, c.PC, T], BF16)
            psn, r_psn = ps("psn", [128, T])
            banks = [ps(f"pb{i}", [128, 512]) for i in range(7)]
            oT_t = actb[:, 0:NH, :]
            bi = [0]
            ei = [0]
            sqi = [0]
            seq = []
            for tt in range(c.NT):
                seq += [(Wb_out[l, b], NH) for b in range(c.NBO)]
                seq += [(Wb_gu[l, b], KC) for b in range(c.NBG)]
                seq += [(Wb_dn[l, cg, kg], c.KG) for cg in range(c.NCG) for kg in range(c.NKG)]
                seq += [(Wb_pg[l, b], KC) for b in range(c.NBO)]
            ws = WStream(Sx, wsl, seq)
            wps = WStream(Sx, wpp, [(Wb_pp[l, b], c.PC) for tt in range(c.NT) for b in range(c.NBO)])

            def next_bank():
                b = banks[bi[0] % 7]
                bi[0] += 1
                return b

            def evac_eng():
                ei[0] += 1
                return "act" if ei[0] % 2 == 0 else "dve"

            pending = []

            def flush_sq():
                while pending:
                    s_t, r_s, g0, g1 = pending.pop(0)
                    for kc in range(g0, g1):
                        Sx.mm(psn[:], k.ones[:], s_t[:, kc - g0, :], start=(kc == 0), stop=(kc == KC - 1),
                              reads=[k.r_ones, r_s], writes=[r_psn], signal=(kc == g1 - 1))

            def sq_group(src, r_src, g, defer=False):
                g0, g1 = 4 * g, min(KC, 4 * g + 4)
                flush_sq()
                s_t, r_s = sqg[sqi[0] % 2]
                sqi[0] += 1
                Sx.act(s_t[:, 0:g1 - g0, :], src[:, g0:g1, :], AF.Square, reads=r_src[g0:g1], writes=[r_s])
                pending.append((s_t, r_s, g0, g1))
                if not defer:
                    flush_sq()

            def rstd_finish():
                flush_sq()
                Sx.act(rstd[:], psn[:], AF.Ln, reads=[r_psn], writes=[r_rstd], scale=1.0 / D, bias=RMS_EPS)
                Sx.act(rstd[:], rstd[:], AF.Exp, reads=[r_rstd], writes=[r_rstd], scale=-0.5)

            NG4 = (KC + 3) // 4

            def post_norm_residual(which, nxt):
                goff = l * c.NG + which * KC
                goff2 = l * c.NG + nxt * KC
                rstd_finish()
                for kc in range(KC):
                    eng = "pool" if kc % 5 == 2 else "dve"
                    scale_chunk(Sx, eng, yT[:, kc, :], yT[:, kc, :], k.gains[:, goff + kc:goff + kc + 1], rstd[:],
                                ptmp[:], r_ptmp, [r_yc[kc], k.r_gains, r_rstd], [r_yc[kc]])
                    Sx.tt(eng, x_t[:, kc, :], x_t[:, kc, :], yT[:, kc, :], ALU.add, reads=[r_xc[kc], r_yc[kc]],
                          writes=[r_xc[kc]])
                for kc in range(KC):
                    Sx.act(hT[:, kc, :], x_t[:, kc, :], AF.Copy, reads=[r_xc[kc], k.r_gains], writes=[r_hc[kc]],
                           scale=k.gains[:, goff2 + kc:goff2 + kc + 1])
                    if kc % 4 == 3 or kc == KC - 1:
                        sq_group(x_t, r_xc, kc // 4, defer=True)
                rstd_finish()

            def load_x_chunk(tt, cc):
                Sx.dma("sp", x_t[:, cc, :], xsrc[cc * 128:(cc + 1) * 128, tt * T:(tt + 1) * T], writes=[r_xc[cc]])

            def load_oT(tt):
                Sx.dma("sp", oT_t, oT[:, tt * T:(tt + 1) * T].rearrange("(kc p) t -> p kc t", p=128),
                       writes=r_ac[0:NH], sres=r_ac[0])

            def load_p(tt):
                Sx.dma("pool", pb[:], pT_in[l][:, tt * T:(tt + 1) * T].rearrange("(kc p) t -> p kc t", p=128),
                       writes=[r_pb])

            load_oT(0)
            for cc in range(KC):
                load_x_chunk(0, cc)
            for tt in range(c.NT):
                t0 = tt * T
                load_p(tt)
                for b in range(c.NBO):
                    w_t, r_w = ws.next()
                    for j in range(4):
                        p_t, r_p = next_bank()
                        for kc in range(NH):
                            Sx.mm(p_t[:], w_t[:, kc, j * 128:(j + 1) * 128], oT_t[:, kc, :], kc == 0, kc == NH - 1,
                                  reads=[r_w, r_ac[kc]], writes=[r_p])
                        Sx.copy(evac_eng(), yT[:, b * 4 + j, :], p_t[:], reads=[r_p], writes=[r_yc[b * 4 + j]])
                    sq_group(yT, r_yc, b, defer=True)
                post_norm_residual(1, 2)
                for b in range(c.NBG):
                    w_t, r_w = ws.next()
                    for j in range(2):
                        f = b * 2 + j
                        pg_t, r_pg = next_bank()
                        pu_t, r_pu = next_bank()
                        for kc in range(KC):
                            Sx.mm(pg_t[:], w_t[:, kc, j * 128:(j + 1) * 128], hT[:, kc, :], kc == 0, kc == KC - 1,
                                  reads=[r_w, r_hc[kc]], writes=[r_pg])
                        for kc in range(KC):
                            Sx.mm(pu_t[:], w_t[:, kc, 256 + j * 128:256 + (j + 1) * 128], hT[:, kc, :], kc == 0,
                                  kc == KC - 1, reads=[r_w, r_hc[kc]], writes=[r_pu])
                        sg_t, r_sg = sg[f % 2]
                        gt_t, r_gt = gtmp[f % 2]
                        Sx.tt("dve", gt_t[:], pg_t[:], rstd[:], ALU.mult, reads=[r_pg, r_rstd], writes=[r_gt])
                        Sx.act(sg_t[:], gt_t[:], AF.Silu, reads=[r_gt], writes=[r_sg])
                        Sx.tt("dve", gt_t[:], pu_t[:], rstd[:], ALU.mult, reads=[r_pu, r_rstd, r_sg], writes=[r_gt])
                        Sx.tt("dve", actb[:, f, :], sg_t[:], gt_t[:], ALU.mult, reads=[r_sg, r_gt], writes=[r_ac[f]])
                for cg in range(c.NCG):
                    ybanks = [next_bank() for _ in range(4)]
                    for kg in range(c.NKG):
                        w_t, r_w = ws.next()
                        for j in range(4):
                            p_t, r_p = ybanks[j]
                            for kk in range(c.KG):
                                first = (kg == 0 and kk == 0)
                                last = (kg == c.NKG - 1 and kk == c.KG - 1)
                                Sx.mm(p_t[:], w_t[:, kk, j * 128:(j + 1) * 128], actb[:, kg * c.KG + kk, :], first, last,
                                      reads=[r_w, r_ac[kg * c.KG + kk]], writes=[r_p], signal=(kk == c.KG - 1))
                    for j in range(4):
                        p_t, r_p = ybanks[j]
                        Sx.copy(evac_eng(), yT[:, cg * 4 + j, :], p_t[:], reads=[r_p], writes=[r_yc[cg * 4 + j]])
                    sq_group(yT, r_yc, cg, defer=True)
                if tt + 1 < c.NT:
                    load_oT(tt + 1)
                post_norm_residual(3, 4)
                for b in range(c.NBO):
                    w_t, r_w = ws.next()
                    wp_t, r_wp = wps.next()
                    for j in range(4):
                        cc = b * 4 + j
                        pg_t, r_pg = next_bank()
                        pu_t, r_pu = next_bank()
                        for kc in range(KC):
                            Sx.mm(pg_t[:], w_t[:, kc, j * 128:(j + 1) * 128], hT[:, kc, :], kc == 0, kc == KC - 1,
                                  reads=[r_w, r_hc[kc]], writes=[r_pg])
                        for kc in range(c.PC):
                            Sx.mm(pu_t[:], wp_t[:, kc, j * 128:(j + 1) * 128], pb[:, kc, :], kc == 0, kc == c.PC - 1,
                                  reads=[r_wp, r_pb], writes=[r_pu])
                        sg_t, r_sg = sg[j % 2]
                        gt_t, r_gt = gtmp[j % 2]
                        Sx.tt("dve", gt_t[:], pg_t[:], rstd[:], ALU.mult, reads=[r_pg, r_rstd], writes=[r_gt])
                        Sx.act(sg_t[:], gt_t[:], AF.Sigmoid, reads=[r_gt], writes=[r_sg])
                        Sx.tt("dve", yT[:, cc, :], sg_t[:], pu_t[:], ALU.mult, reads=[r_sg, r_pu], writes=[r_yc[cc]])
                        Sx.tt("pool" if j == 3 else "dve", x_t[:, cc, :], x_t[:, cc, :], yT[:, cc, :], ALU.add,
                              reads=[r_xc[cc], r_yc[cc]], writes=[r_xc[cc]])
                        Sx.dma("sp", xdst[cc * 128:(cc + 1) * 128, t0:t0 + T], x_t[:, cc, :], reads=[r_xc[cc]])
                        if tt + 1 < c.NT:
                            load_x_chunk(tt + 1, cc)
            Sx.emit()

    gstack = ExitStack()
    pool = SemPool(nc, gstack)
    if want("cvt"):
        stage_convert([0] if c.overlap_cvt else list(range(DEPTH)))
    if want("bias"):
        stage_bias()
    for l in range(DEPTH):
        if want("qkv"):
            stage_qkv(l)
        if want("sb"):
            stage_sb(l)
        if want("dl"):
            stage_dl(l)
        if want("dense"):
            stage_dense(l)
    gstack.close()
    return nc


def pack_gains(cfg, ln_mix_pre, ln_mix_post, ln_ffn_pre, ln_ffn_post, ln_pli, ln_head):
    cols = []
    for l in range(cfg.DEPTH):
        for g in (ln_mix_pre, ln_mix_post, ln_ffn_pre, ln_ffn_post, ln_pli):
            cols.append(np.asarray(g[l], np.float32).reshape(cfg.KC, 128).T)
        cols.append(np.asarray(ln_head[l], np.float32).reshape(cfg.NH, 128).T)
    return np.ascontiguousarray(np.concatenate(cols, axis=1))


def make_in_maps(cfg, x, p, ln_mix_pre, w_in, ln_head, w_out, ln_mix_post, rel_bias,
                 ln_ffn_pre, w_gate_up, w_down, ln_ffn_post, ln_pli, w_pli_gate, w_pli_proj, n_cores):
    cst, oh = make_consts(cfg)
    gains = pack_gains(cfg, ln_mix_pre, ln_mix_post, ln_ffn_pre, ln_ffn_post, ln_pli, ln_head)
    f = lambda a: np.ascontiguousarray(np.asarray(a, np.float32))
    shared = {
        "w_in": f(w_in), "w_out": f(w_out), "w_gate_up": f(w_gate_up), "w_down": f(w_down),
        "w_pli_gate": f(w_pli_gate), "w_pli_proj": f(w_pli_proj), "gains": gains,
        "rel_bias": f(rel_bias), "cst": cst, "oh": oh,
    }
    x = np.asarray(x, np.float32)
    p = np.asarray(p, np.float32)
    maps = []
    for b in range(n_cores):
        m = dict(shared)
        m["xT"] = np.ascontiguousarray(x[b].T)
        m["pT"] = np.ascontiguousarray(p[:, b].transpose(0, 2, 1))
        maps.append(m)
    return maps


_PROGRAM_CACHE = {}


def kernel(x, p, ln_mix_pre, w_in, ln_head, w_out, ln_mix_post, rel_bias,
           ln_ffn_pre, w_gate_up, w_down, ln_ffn_post, ln_pli, w_pli_gate, w_pli_proj):
    cfg = Cfg()
    n = 8
    in_maps = make_in_maps(cfg, x, p, ln_mix_pre, w_in, ln_head, w_out, ln_mix_post, rel_bias,
                           ln_ffn_pre, w_gate_up, w_down, ln_ffn_post, ln_pli, w_pli_gate, w_pli_proj, n)
    nc = build_program(cfg)
    res = run_bass_kernel_spmd(nc, in_maps, core_ids=list(range(n)))
    out = np.stack([np.asarray(r["yT"], np.float32).T for r in res.results], axis=0)
    return np.ascontiguousarray(out)
```

```python
import math
import os
from contextlib import ExitStack

import numpy as np
import concourse.bass as bass
import concourse.mybir as mybir
from concourse.bass_utils import run_bass_kernel_spmd

F32 = mybir.dt.float32
BF16 = mybir.dt.bfloat16
AF = mybir.ActivationFunctionType
ALU = mybir.AluOpType

RMS_EPS = 1e-6
NEG_BIG = -30000.0


class Res:
    __slots__ = ("name", "writer", "readers", "dsem", "dcount", "dbase")

    def __init__(self, name):
        self.name = name
        self.writer = None
        self.readers = []
        self.dsem = None
        self.dcount = 0
        self.dbase = 0


class _Eng:
    def __init__(self, name, sem):
        self.name = name
        self.sem = sem
        self.count = 0
        self.waited = {}
        self.ops = []


class Sched:
    ENGS = ("pe", "act", "dve", "pool", "sp")

    def __init__(self, nc, pool, tag):
        self.nc = nc
        self.pool = pool
        self.tag = tag
        self.eng = {}
        for e in self.ENGS:
            eo = _Eng(e, pool.eng_sems[e])
            eo.count = pool.eng_counts[e]
            self.eng[e] = eo
        self.dma_res = []

    def _deps(self, ename, reads, writes):
        deps = []
        for r in reads:
            if r.writer is not None:
                deps.append(r.writer)
        for w in writes:
            if w.writer is not None:
                deps.append(w.writer)
            deps.extend(w.readers)
        e = self.eng[ename]
        best = {}
        for (sem, val, src) in deps:
            if src == "pe" and ename == "pe":
                continue
            k = id(sem)
            if e.waited.get(k, 0) >= val:
                continue
            if k not in best or best[k][1] < val:
                best[k] = (sem, val)
        for k, (sem, val) in best.items():
            e.waited[k] = val
        return list(best.values())

    def _record(self, tok, reads, writes):
        for r in reads:
            r.readers.append(tok)
        for w in writes:
            w.writer = tok
            w.readers = []

    def op(self, ename, fn, reads=(), writes=(), signal=True):
        e = self.eng[ename]
        waits = self._deps(ename, reads, writes)
        tok = (e.sem, e.count + 1, ename)
        if signal:
            e.count += 1
        e.ops.append((waits, fn, (e.sem, 1) if signal else None))
        self._record(tok, reads, writes)
        return tok

    def dma(self, qname, out_ap, in_ap, reads=(), writes=(), sres=None):
        e = self.eng[qname]
        waits = self._deps(qname, reads, writes)
        if sres is None:
            sres = writes[0] if writes else reads[0]
        if sres.dsem is None:
            sres.dsem, sres.dbase = self.pool.get()
            self.dma_res.append(sres)
        sres.dcount += 1
        tok = (sres.dsem, sres.dbase + 16 * sres.dcount, "dma")

        def fn(eng, out_ap=out_ap, in_ap=in_ap):
            return eng.dma_start(out=out_ap, in_=in_ap)

        e.ops.append((waits, fn, (sres.dsem, 16)))
        self._record(tok, reads, writes)
        return tok

    def mm(self, out, lhsT, rhs, start, stop, reads, writes, signal=None):
        if signal is None:
            signal = stop
        return self.op("pe", lambda e: e.matmul(out, lhsT, rhs, start=start, stop=stop),
                       reads=reads, writes=writes, signal=signal)

    def act(self, out, in_, func, reads, writes, scale=1.0, bias=0.0):
        return self.op("act", lambda e: e.activation(out=out, in_=in_, func=func, bias=bias, scale=scale),
                       reads=reads, writes=writes)

    def tt(self, eng, out, in0, in1, op, reads, writes):
        return self.op(eng, lambda e: e.tensor_tensor(out, in0, in1, op), reads=reads, writes=writes)

    def copy(self, eng, out, in_, reads, writes, scale=None):
        if eng == "act":
            return self.act(out, in_, AF.Copy, reads, writes, scale=(1.0 if scale is None else scale))
        if scale is None:
            return self.op(eng, lambda e: e.tensor_copy(out, in_), reads=reads, writes=writes)
        return self.op(eng, lambda e: e.tensor_scalar_mul(out, in_, scale), reads=reads, writes=writes)

    def emit(self):
        nc = self.nc
        finals = [(r.dsem, r.dbase + 16 * r.dcount) for r in self.dma_res]
        engs = self.eng

        def run(engobj, e):
            for waits, fn, inc in e.ops:
                for sem, val in waits:
                    engobj.wait_ge(sem, val)
                ins = fn(engobj)
                if inc is not None:
                    ins.then_inc(inc[0], inc[1])

        with nc.Block() as block:
            @block.tensor
            def _(t):
                run(t, engs["pe"])

            @block.scalar
            def _(s):
                run(s, engs["act"])

            @block.vector
            def _(v):
                run(v, engs["dve"])

            @block.gpsimd
            def _(g):
                run(g, engs["pool"])

            @block.sync
            def _(s):
                run(s, engs["sp"])
                for sem, val in finals:
                    s.wait_ge(sem, val)

        for e in self.ENGS:
            self.pool.eng_counts[e] = engs[e].count
        for r in self.dma_res:
            self.pool.put(r.dsem, r.dbase + 16 * r.dcount)


class SemPool:
    def __init__(self, nc, stack):
        self.nc = nc
        self.stack = stack
        self.eng_sems = {e: stack.enter_context(nc.semaphore(f"eng_{e}")) for e in Sched.ENGS}
        self.eng_counts = {e: 0 for e in Sched.ENGS}
        self.free = []
        self.n = 0

    def get(self):
        if self.free:
            return self.free.pop()
        sem = self.stack.enter_context(self.nc.semaphore(f"dsem_{self.n}"))
        self.n += 1
        return sem, 0

    def put(self, sem, value):
        self.free.append((sem, value))


class Cfg:
    def __init__(self, D=2048, S=4096, DEPTH=4, HS=8, HD=8, DFF=5632, PLI=256, debug=False,
                 stages=None):
        self.D, self.S, self.DEPTH, self.HS, self.HD, self.DFF, self.PLI = D, S, DEPTH, HS, HD, DFF, PLI
        self.T = 512
        self.KC = D // 128
        self.NH = HS + HD
        self.MIXW = self.NH * 128
        assert self.MIXW == D
        self.INW = 3 * self.MIXW
        self.FC = DFF // 128
        self.NT = S // self.T
        self.NBLK = S // 128
        self.NBI = self.INW // 512
        self.NBO = D // 512
        self.NBG = self.FC // 2
        self.KG = 11
        self.NKG = self.FC // self.KG
        assert self.NKG * self.KG == self.FC
        self.NCG = D // 512
        self.NBS = HS * 128 // 512
        self.NBD = HD * 128 // 512
        self.PC = PLI // 128
        self.NG = 5 * self.KC + self.NH
        self.debug = debug
        self.overlap_cvt = os.environ.get("K_OVERLAP", "sb")
        self.stages = stages
        self.SCALE = 1.0 / math.sqrt(128.0)
        self.DIL = ((128, 1), (512, 4), (2048, 16))


def t5_bucket_np(dist):
    num_buckets, max_distance = 32, 2048
    max_exact = num_buckets // 2
    dist = np.asarray(dist, dtype=np.int32)
    d = np.maximum(dist, 1).astype(np.float32)
    large = max_exact + (np.log(d / np.float32(max_exact)) / np.float32(math.log(max_distance / max_exact))
                         * np.float32(num_buckets - max_exact)).astype(np.int32)
    large = np.minimum(large, num_buckets - 1)
    return np.where(dist < max_exact, dist, large)


def make_consts(cfg):
    jp = np.arange(128)[:, None]
    j = np.arange(128)[None, :]
    tri_neg = np.where(jp >= j, -1.0, 0.0).astype(np.float32)
    masks = np.zeros((128, 4, 512), np.float32)
    t = np.arange(512)[None, :]
    for r in range(4):
        masks[:, r, :] = ((r * 128 + np.arange(128)[:, None]) < t).astype(np.float32)
    J = np.zeros((128, 128), np.float32)
    J[np.arange(128), 127 - np.arange(128)] = 1.0
    cst = np.concatenate([tri_neg, masks.reshape(128, 2048), J], axis=1)
    oh = np.zeros((33, 3, 384), np.float32)
    for di, (window, dil) in enumerate(cfg.DIL):
        n_back = window // dil
        for u in range(384):
            delta = u - 127
            if 0 <= delta <= n_back:
                b = int(t5_bucket_np(np.array([delta * dil]))[0])
                oh[b, di, u] = 1.0
            else:
                oh[32, di, u] = 1.0
    return cst, oh.reshape(33, 3 * 384)


def build_program(cfg):
    nc = bass.Bass("TRN2", target_bir_lowering=False)
    c = cfg
    D, S, T, KC, NH, HS, HD, FC = c.D, c.S, c.T, c.KC, c.NH, c.HS, c.HD, c.FC
    DEPTH = c.DEPTH

    def din(name, shape, dt=F32):
        return nc.dram_tensor(name, list(shape), dt, kind="ExternalInput").ap()

    def dtmp(name, shape, dt):
        kind = "ExternalOutput" if c.debug else "Internal"
        return nc.dram_tensor(name, list(shape), dt, kind=kind).ap()

    xT_in = din("xT", [D, S])
    pT_in = din("pT", [DEPTH, c.PLI, S])
    w_in = din("w_in", [DEPTH, D, c.INW])
    w_out = din("w_out", [DEPTH, c.MIXW, D])
    w_gu = din("w_gate_up", [DEPTH, D, 2 * c.DFF])
    w_dn = din("w_down", [DEPTH, c.DFF, D])
    w_pg = din("w_pli_gate", [DEPTH, D, D])
    w_pp = din("w_pli_proj", [DEPTH, c.PLI, D])
    gains_in = din("gains", [128, DEPTH * c.NG])
    relb_in = din("rel_bias", [32, HD])
    cst_in = din("cst", [128, 128 + 2048 + 128])
    oh_in = din("oh", [33, 3 * 384])
    yT_out = nc.dram_tensor("yT", [D, S], F32, kind="ExternalOutput").ap()

    Wb_in = dtmp("Wb_in", [DEPTH, c.NBI, 128, KC, 512], BF16)
    Wb_out = dtmp("Wb_out", [DEPTH, c.NBO, 128, NH, 512], BF16)
    Wb_gu = dtmp("Wb_gu", [DEPTH, c.NBG, 128, KC, 512], BF16)
    Wb_dn = dtmp("Wb_dn", [DEPTH, c.NCG, c.NKG, 128, c.KG, 512], BF16)
    Wb_pg = dtmp("Wb_pg", [DEPTH, c.NBO, 128, KC, 512], BF16)
    Wb_pp = dtmp("Wb_pp", [DEPTH, c.NBO, 128, c.PC, 512], BF16)
    xr = dtmp("xr", [D, S], F32)
    qT_sb = dtmp("qT_sb", [HS, 128, S], BF16)
    kT_sb = dtmp("kT_sb", [HS, 128, S], BF16)
    v_sb = dtmp("v_sb", [S, HS * 128], BF16)
    qT_dl = dtmp("qT_dl", [HD, 128, S], BF16)
    kT_dl = dtmp("kT_dl", [HD, 128, S], BF16)
    v_dl = dtmp("v_dl", [S, HD * 128], BF16)
    oT = dtmp("oT", [c.MIXW, S], BF16)
    gp = dtmp("gp", [HD, 3, 384], F32)
    Gmat = dtmp("Gmat", [HD, 128, 3, 256], F32)

    uid = [0]

    def want(stage):
        return c.stages is None or stage in c.stages

    def emit_convert(Sx, l, dres, throttle=0, lazy=None):
        dummies = [Res(f"cvthr{i}") for i in range(max(throttle, 1))]
        cnt = [0]

        def cdma(out_ap, in_ap, sres):
            def go(out_ap=out_ap, in_ap=in_ap, sres=sres):
                w = [dummies[cnt[0] % len(dummies)]] if throttle else []
                cnt[0] += 1
                Sx.dma("pool", out_ap, in_ap, writes=w, sres=sres)
            if lazy is None:
                go()
            else:
                lazy.append(go)

        for b in range(c.NBI):
            cdma(Wb_in[l, b],
                   w_in[l][:, b * 512:(b + 1) * 512].rearrange("(kc p) n -> p kc n", p=128), sres=dres)
        for b in range(c.NBO):
            cdma(Wb_out[l, b],
                   w_out[l][:, b * 512:(b + 1) * 512].rearrange("(kc p) n -> p kc n", p=128), sres=dres)
        for b in range(c.NBG):
            cdma(Wb_gu[l, b][:, :, 0:256],
                   w_gu[l][:, b * 256:(b + 1) * 256].rearrange("(kc p) n -> p kc n", p=128), sres=dres)
            cdma(Wb_gu[l, b][:, :, 256:512],
                   w_gu[l][:, c.DFF + b * 256:c.DFF + (b + 1) * 256].rearrange("(kc p) n -> p kc n", p=128),
                   sres=dres)
        for cg in range(c.NCG):
            for kg in range(c.NKG):
                cdma(Wb_dn[l, cg, kg],
                       w_dn[l][kg * c.KG * 128:(kg + 1) * c.KG * 128, cg * 512:(cg + 1) * 512]
                       .rearrange("(kc p) n -> p kc n", p=128), sres=dres)
        for b in range(c.NBO):
            cdma(Wb_pg[l, b],
                   w_pg[l][:, b * 512:(b + 1) * 512].rearrange("(kc p) n -> p kc n", p=128), sres=dres)
            cdma(Wb_pp[l, b],
                   w_pp[l][:, b * 512:(b + 1) * 512].rearrange("(kc p) n -> p kc n", p=128), sres=dres)

    def stage_convert(layers):
        with ExitStack() as st:
            Sx = Sched(nc, pool, f"cv{layers[0]}")
            dres = Res("cvt")
            for l in layers:
                emit_convert(Sx, l, dres)
            Sx.emit()

    class Ctx:
        pass

    def new_stage(tag):
        st = ExitStack()
        Sx = Sched(nc, pool, tag)
        uid[0] += 1
        u = uid[0]

        def sb(name, shape, dt):
            t = st.enter_context(nc.sbuf_tensor(f"{name}_{u}", list(shape), dt))
            return t, Res(name)

        def ps(name, shape, dt=F32):
            t = st.enter_context(nc.psum_tensor(f"{name}_{u}", list(shape), dt))
            return t, Res(name)

        return st, Sx, sb, ps

    def load_consts(Sx, sb):
        k = Ctx()
        k.ones, k.r_ones = sb("ones", [128, 128], BF16)
        k.gains, k.r_gains = sb("gains", [128, DEPTH * c.NG], F32)
        Sx.op("pool", lambda e: e.memset(k.ones[:], 1.0), writes=[k.r_ones])
        Sx.dma("sp", k.gains[:], gains_in, writes=[k.r_gains])
        return k

    def emit_rstd(Sx, k, src3, nchunks, n_feat, sq, r_sq, src_res, psn, r_psn, rstd, r_rstd, width):
        for g0 in range(0, nchunks, 4):
            g1 = min(nchunks, g0 + 4)
            Sx.act(sq[:, g0:g1, 0:width], src3[:, g0:g1, :], AF.Square, reads=[src_res], writes=[r_sq])
        for kc in range(nchunks):
            Sx.mm(psn[:, 0:width], k.ones[:], sq[:, kc, 0:width], start=(kc == 0), stop=(kc == nchunks - 1),
                  reads=[k.r_ones, r_sq], writes=[r_psn])
        Sx.act(rstd[:, 0:width], psn[:, 0:width], AF.Ln, reads=[r_psn], writes=[r_rstd],
               scale=1.0 / n_feat, bias=RMS_EPS)
        Sx.act(rstd[:, 0:width], rstd[:, 0:width], AF.Exp, reads=[r_rstd], writes=[r_rstd], scale=-0.5)

    def scale_chunk(Sx, eng, out, in_, gain_ap, rstd_ap, tmp, r_tmp, reads, writes):
        if eng == "dve":
            Sx.op("dve", lambda e: e.scalar_tensor_tensor(out, in_, gain_ap, rstd_ap, ALU.mult, ALU.mult),
                  reads=reads, writes=writes)
        else:
            Sx.tt("pool", tmp, in_, rstd_ap, ALU.mult, reads=reads, writes=[r_tmp])
            Sx.op("pool", lambda e: e.tensor_scalar_mul(out, tmp, gain_ap), reads=[r_tmp] + list(reads), writes=writes)

    def stage_bias():
        st, Sx, sb, ps = new_stage("gb")
        with st:
            rb, r_rb = sb("rb", [33, HD], F32)
            oh, r_oh = sb("oh", [33, 3 * 384], F32)
            gsb, r_gsb = sb("gsb", [HD, 3 * 384], F32)
            pg = [ps(f"pg{i}", [HD, 512]) for i in range(3)]
            Sx.op("pool", lambda e: e.memset(rb[32:33, :], NEG_BIG), writes=[r_rb])
            Sx.dma("sp", rb[0:32, :], relb_in, writes=[r_rb])
            Sx.dma("sp", oh[:], oh_in, writes=[r_oh])
            for di in range(3):
                Sx.mm(pg[di][0][:, 0:384], rb[:], oh[:, di * 384:(di + 1) * 384], True, True,
                      reads=[r_rb, r_oh], writes=[pg[di][1]])
                Sx.copy("dve", gsb[:, di * 384:(di + 1) * 384], pg[di][0][:, 0:384], reads=[pg[di][1]], writes=[r_gsb])
            r_gp = Res("gp_dram")
            Sx.dma("sp", gp.rearrange("h d u -> h (d u)"), gsb[:], reads=[r_gsb], writes=[r_gp], sres=r_gsb)
            Jm, r_J = sb("Jm", [128, 128], F32)
            Sx.dma("sp", Jm[:], cst_in[:, 128 + 2048:128 + 2048 + 128], writes=[r_J])
            brev = [sb(f"brev{i}", [128, 256], F32) for i in range(2)]
            gout = [sb(f"gout{i}", [128, 3, 256], F32) for i in range(2)]
            pt = [ps(f"pt{i}", [128, 512]) for i in range(2)]
            n_ = 0
            for h in range(HD):
                go_t, r_go = gout[h % 2]
                for di in range(3):
                    b_t, r_b = brev[n_ % 2]
                    p_t, r_p = pt[n_ % 2]
                    n_ += 1
                    src = bass.AP(gp.tensor, (h * 3 + di) * 384, [[1, 128], [1, 256]])
                    Sx.dma("sp", b_t[:], src, reads=[r_gp], writes=[r_b])
                    Sx.mm(p_t[:, 0:256], Jm[:], b_t[:], True, True, reads=[r_J, r_b], writes=[r_p])
                    Sx.act(go_t[:, di, :], p_t[:, 0:256], AF.Exp, reads=[r_p], writes=[r_go])
                Sx.dma("sp", Gmat[h], go_t[:], reads=[r_go])
            Sx.emit()

    class WStream:
        def __init__(self, Sx, slots, seq, queue="sp"):
            self.Sx, self.slots, self.seq, self.queue = Sx, slots, seq, queue
            self.issued = 0
            self.cur = 0

        def _issue_upto(self, j):
            n = len(self.slots)
            while self.issued <= min(j, len(self.seq) - 1):
                i = self.issued
                t, r = self.slots[i % n]
                src, nk = self.seq[i]
                self.Sx.dma(self.queue, t[:, 0:nk, :], src, writes=[r])
                self.issued += 1

        def next(self):
            i = self.cur
            self._issue_upto(i + len(self.slots) - 1)
            self.cur += 1
            return self.slots[i % len(self.slots)]

    def stage_qkv(l):
        st, Sx, sb, ps = new_stage(f"a{l}")
        with st:
            k = load_consts(Sx, sb)
            xsrc = xT_in if l == 0 else xr
            xt = [sb(f"xt{i}", [128, KC, T], F32) for i in range(2)]
            sqs = [sb(f"sq{i}", [128, KC, T], BF16) for i in range(2)]
            hTs = [sb(f"hT{i}", [128, KC, T], BF16) for i in range(2)]
            rstds = [sb(f"rstd{i}", [128, T], F32) for i in range(2)]
            wsl = [sb(f"w{i}", [128, KC, 512], BF16) for i in range(3)]
            ev = [sb(f"ev{i}", [128, 4, 512], BF16) for i in range(4)]
            psn, r_psn = ps("psn", [128, T])
            pacc = [ps(f"pa{i}", [128, 512]) for i in range(6)]
            goff = l * c.NG
            ws = WStream(Sx, wsl, [(Wb_in[l, b], KC) for tt in range(c.NT) for b in range(c.NBI)])
            evi = 0
            pi = 0

            def load_x(tt):
                x_t, r_x = xt[tt % 2]
                Sx.dma("sp", x_t[:], xsrc[:, tt * T:(tt + 1) * T].rearrange("(kc p) t -> p kc t", p=128),
                       writes=[r_x])

            def norm(tt):
                x_t, r_x = xt[tt % 2]
                sq, r_sq = sqs[tt % 2]
                hT, r_hT = hTs[tt % 2]
                rstd, r_rstd = rstds[tt % 2]
                emit_rstd(Sx, k, x_t[:], KC, D, sq, r_sq, r_x, psn, r_psn, rstd, r_rstd, T)
                for kc in range(KC):
                    scale_chunk(Sx, "dve", hT[:, kc, :], x_t[:, kc, :], k.gains[:, goff + kc:goff + kc + 1], rstd[:],
                                None, None, [r_x, k.r_gains, r_rstd], [r_hT])

            if c.overlap_cvt == "qkv" and l + 1 < DEPTH and want("cvt"):
                emit_convert(Sx, l + 1, Res("cvt_next"), throttle=4)
            load_x(0)
            if c.NT > 1:
                load_x(1)
            norm(0)
            for tt in range(c.NT):
                t0 = tt * T
                hT, r_hT = hTs[tt % 2]
                for b in range(c.NBI):
                    w_t, r_w = ws.next()
                    if b == 2 and tt + 1 < c.NT:
                        norm(tt + 1)
                        if tt + 2 < c.NT:
                            load_x(tt + 2)
                    if b < 3 * c.NBS:
                        role, rbk = ("q", "k", "v")[b // c.NBS], b % c.NBS
                        dq, dk, dv = qT_sb, kT_sb, v_sb
                    else:
                        bb = b - 3 * c.NBS
                        role, rbk = ("q", "k", "v")[bb // c.NBD], bb % c.NBD
                        dq, dk, dv = qT_dl, kT_dl, v_dl
                    e_t, r_e = ev[evi % 4]
                    evi += 1
                    for j in range(4):
                        p_t, r_p = pacc[pi % 6]
                        pi += 1
                        for kc in range(KC):
                            if role == "v":
                                Sx.mm(p_t[:], hT[:, kc, j * 128:(j + 1) * 128], w_t[:, kc, :],
                                      kc == 0, kc == KC - 1, reads=[r_hT, r_w], writes=[r_p])
                            else:
                                Sx.mm(p_t[:], w_t[:, kc, j * 128:(j + 1) * 128], hT[:, kc, :],
                                      kc == 0, kc == KC - 1, reads=[r_hT, r_w], writes=[r_p])
                        eng = "act" if j % 2 == 0 else "dve"
                        Sx.copy(eng, e_t[:, j, :], p_t[:], reads=[r_p], writes=[r_e],
                                scale=(c.SCALE if role == "q" else None))
                    if role == "v":
                        Sx.dma("sp", dv[t0:t0 + T, rbk * 512:(rbk + 1) * 512].rearrange("(tb p) n -> p tb n", p=128),
                               e_t[:], reads=[r_e])
                    else:
                        dst = dq if role == "q" else dk
                        Sx.dma("sp", dst[rbk * 4:(rbk + 1) * 4, :, t0:t0 + T].rearrange("h p t -> p h t"),
                               e_t[:], reads=[r_e])
            Sx.emit()

    def head_epilogue(Sx, k, l, hidx, o_ap, r_o, q0, W, sqh, r_sqh, psn, r_psn, rstd, r_rstd, on, r_on):
        emit_rstd(Sx, k, o_ap.unsqueeze(1), 1, 128, sqh, r_sqh, r_o, psn, r_psn, rstd, r_rstd, W)
        gcol = l * c.NG + 5 * KC + hidx
        Sx.op("dve", lambda e: e.scalar_tensor_tensor(on[:, 0:W], o_ap, k.gains[:, gcol:gcol + 1], rstd[:, 0:W],
                                                      ALU.mult, ALU.mult),
              reads=[r_o, k.r_gains, r_rstd], writes=[r_on])
        Sx.dma("sp", oT[hidx * 128:(hidx + 1) * 128, q0:q0 + W], on[:, 0:W], reads=[r_on])

    def stage_sb(l):
        st, Sx, sb, ps = new_stage(f"s{l}")
        with st:
            k = load_consts(Sx, sb)
            negones, r_negones = sb("negones", [128, 128], BF16)
            trineg, r_trineg = sb("trineg", [128, 128], BF16)
            masks, r_masks = sb("masks", [128, 4, 512], BF16)
            Sx.op("pool", lambda e: e.memset(negones[:], -1.0), writes=[r_negones])
            Sx.dma("pool", trineg[:], cst_in[:, 0:128], writes=[r_trineg])
            Sx.dma("pool", masks[:], cst_in[:, 128:128 + 2048].rearrange("p (r t) -> p r t", r=4), writes=[r_masks])
            NB = c.NBLK
            qk = [(sb(f"q{i}", [128, S], BF16), sb(f"k{i}", [128, S], BF16), sb(f"v{i}", [128, NB, 128], BF16))
                  for i in range(2)]
            NR = 5
            e32 = [sb(f"e32_{i}", [128, 512], F32) for i in range(2)]
            spb = [sb(f"sp{i}", [128, 512], BF16) for i in range(NR)]
            aT = [sb(f"aT{i}", [128, 512], BF16) for i in range(NR)]
            Rb = [sb(f"Rb{i}", [128, 512], F32) for i in range(2)]
            Rb16 = [sb(f"Rb16_{i}", [128, 512], BF16) for i in range(2)]
            negones32, r_negones32 = sb("negones32", [128, 128], F32)
            Sx.op("pool", lambda e: e.memset(negones32[:], -1.0), writes=[r_negones32])
            osb, r_osb = sb("osb", [128, 512], F32)
            sqh, r_sqh = sb("sqh", [128, 1, 512], BF16)
            rstd, r_rstd = sb("rstd", [128, 512], F32)
            on = [sb(f"on{i}", [128, 512], BF16) for i in range(2)]
            pA = [ps(f"pA{i}", [128, 512]) for i in range(NR)]
            pO = [ps(f"pO{i}", [128, 512]) for i in range(2)]
            psn, r_psn = ps("psn", [128, 512])
            oni = 0
            gq = 0
            def load_head(h):
                (q_t, r_q), (k_t, r_k), (v_t, r_v) = qk[h % 2]
                Sx.dma("sp", q_t[:], qT_sb[h], writes=[r_q])
                Sx.dma("sp", k_t[:], kT_sb[h], writes=[r_k])
                Sx.dma("sp", v_t[:], v_sb[:, h * 128:(h + 1) * 128].rearrange("(b p) e -> p b e", p=128),
                       writes=[r_v])

            cv_jobs = []
            if c.overlap_cvt == "sb" and l + 1 < DEPTH and want("cvt"):
                emit_convert(Sx, l + 1, Res("cvt_next"), throttle=2, lazy=cv_jobs)
            n_items_total = HS * sum(4 * Q + 4 for Q in range(c.NT))
            cv_every = max(1, n_items_total // (len(cv_jobs) + 1)) if cv_jobs else 0
            cv_step = [0]
            load_head(0)
            for h in range(HS):
                (q_t, r_q), (k_t, r_k), (v_t, r_v) = qk[h % 2]
                if h + 1 < HS:
                    load_head(h + 1)
                items = []
                for Q in range(c.NT):
                    kbs = list(range(4 * Q + 3, -1, -1))
                    for i, kb in enumerate(kbs):
                        items.append((Q, i, kb, len(kbs)))
                G = len(items)

                def Zs(g):
                    Q, i, kb, n = items[g]
                    A, r_A = pA[g % NR]
                    Sx.mm(A[:], k_t[:, kb * 128:(kb + 1) * 128], q_t[:, Q * T:(Q + 1) * T], True, False,
                          reads=[r_q, r_k], writes=[r_A], signal=True)

                def Es_exp(g):
                    Q, i, kb, n = items[g]
                    A, r_A = pA[g % NR]
                    e_t, r_e = e32[g % 2]
                    Sx.act(e_t[:], A[:], AF.Exp, reads=[r_A], writes=[r_e])

                def Es(g):
                    Q, i, kb, n = items[g]
                    e_t, r_e = e32[g % 2]
                    s_t, r_s = spb[g % NR]
                    Sx.act(s_t[:], e_t[:], AF.Ln, reads=[r_e], writes=[r_s], bias=1.0)
                    if kb >= 4 * Q:
                        r = kb - 4 * Q
                        w_ = (r + 1) * 128
                        Sx.tt("pool", s_t[:, 0:w_], s_t[:, 0:w_], masks[:, r, 0:w_], ALU.mult,
                              reads=[r_s, r_masks], writes=[r_s])

                def Cs(g):
                    Q, i, kb, n = items[g]
                    A, r_A = pA[g % NR]
                    s_t, r_s = spb[g % NR]
                    last = (i == 0)
                    Sx.mm(A[:], trineg[:], s_t[:], False, last, reads=[r_trineg, r_s], writes=[r_A], signal=last)
                    if i > 0:
                        R_t, r_Rt = Rb[g % 2]
                        sp_prev, r_sp_prev = spb[(g - 1) % NR]
                        if i == 1:
                            Sx.copy("dve", R_t[:], sp_prev[:], reads=[r_sp_prev], writes=[r_Rt])
                            Sx.mm(A[:], negones[:], sp_prev[:], False, True, reads=[r_negones, r_sp_prev],
                                  writes=[r_A], signal=True)
                        else:
                            R_p, r_Rp = Rb[(g - 1) % 2]
                            R16, r_R16 = Rb16[g % 2]
                            Sx.tt("dve", R_t[:], R_p[:], sp_prev[:], ALU.add, reads=[r_Rp, r_sp_prev], writes=[r_Rt])
                            Sx.copy("dve", R16[:], R_t[:], reads=[r_Rt], writes=[r_R16])
                            Sx.mm(A[:], negones[:], R16[:], False, True, reads=[r_negones, r_R16], writes=[r_A],
                                  signal=True)

                def Xs(g):
                    Q, i, kb, n = items[g]
                    A, r_A = pA[g % NR]
                    a_t, r_a = aT[g % NR]
                    Sx.act(a_t[:], A[:], AF.Exp, reads=[r_A], writes=[r_a])
                    if kb >= 4 * Q:
                        r = kb - 4 * Q
                        w_ = (r + 1) * 128
                        Sx.tt("pool", a_t[:, 0:w_], a_t[:, 0:w_], masks[:, r, 0:w_], ALU.mult,
                              reads=[r_a, r_masks], writes=[r_a])

                def Vs(g):
                    nonlocal oni
                    Q, i, kb, n = items[g]
                    a_t, r_a = aT[g % NR]
                    po_t, r_po = pO[Q % 2]
                    Sx.mm(po_t[:], v_t[:, kb, :], a_t[:], i == 0, i == n - 1, reads=[r_v, r_a], writes=[r_po],
                          signal=(i == n - 1))
                    if i == n - 1:
                        Sx.copy("dve", osb[:], po_t[:], reads=[r_po], writes=[r_osb])
                        on_t, r_on = on[oni % 2]
                        oni += 1
                        head_epilogue(Sx, k, l, h, osb[:], r_osb, Q * T, T, sqh, r_sqh, psn, r_psn, rstd, r_rstd,
                                      on_t, r_on)

                for step in range(G + 3):
                    if step < G:
                        Zs(step)
                        Es_exp(step)
                    if 0 <= step - 2 < G:
                        Cs(step - 2)
                        Xs(step - 2)
                    if step < G:
                        Es(step)
                    if 0 <= step - 3 < G:
                        Vs(step - 3)
                    if cv_jobs:
                        cv_step[0] += 1
                        if cv_step[0] % cv_every == 0:
                            cv_jobs.pop(0)()
            while cv_jobs:
                cv_jobs.pop(0)()
            Sx.emit()

    def stage_dl(l):
        st, Sx, sb, ps = new_stage(f"d{l}")
        with st:
            k = load_consts(Sx, sb)
            NB = c.NBLK
            qk = [(sb(f"q{i}", [128, S], BF16), sb(f"k{i}", [128, S], BF16),
                   [sb(f"v{i}_{di}", [128, NB, 128], BF16) for di in range(3)],
                   sb(f"G{i}", [128, 3, 256], F32))
                  for i in range(2)]
            acc, r_acc = sb("acc", [128, 2, S], F32)
            W32 = [sb(f"W32_{i}", [128, 4, 128], F32) for i in range(3)]
            Wb = [sb(f"Wb{i}", [128, 4, 128], BF16) for i in range(3)]
            osb, r_osb = sb("osb", [128, 512], F32)
            rden, r_rden = sb("rden", [128, 512], F32)
            sqh, r_sqh = sb("sqh", [128, 1, 512], BF16)
            rstd, r_rstd = sb("rstd", [128, 512], F32)
            on = [sb(f"on{i}", [128, 512], BF16) for i in range(2)]
            pS = [ps(f"pS{i}", [128, 4, 128]) for i in range(3)]
            pN = [ps(f"pN{i}", [128, 4, 128]) for i in range(3)]
            psn, r_psn = ps("psn", [128, 512])
            ui = 0
            oni = 0
            def load_head(h):
                (q_t, r_q), (k_t, r_k), vds, (G_t, r_G) = qk[h % 2]
                Sx.dma("sp", q_t[:], qT_dl[h], writes=[r_q])
                Sx.dma("sp", k_t[:], kT_dl[h], writes=[r_k])
                Sx.dma("sp", G_t[:], Gmat[h], writes=[r_G])
                for di, (window, d) in enumerate(c.DIL):
                    v_t, r_v = vds[di]
                    nsub = S // (128 * d)
                    if d == 1:
                        Sx.dma("sp", v_t[:], v_dl[:, h * 128:(h + 1) * 128].rearrange("(n i) e -> i n e", i=128),
                               writes=[r_v])
                    else:
                        for n_ in range(nsub):
                            Sx.dma("sp", v_t[:, n_ * d:(n_ + 1) * d, :],
                                   v_dl[n_ * 128 * d:(n_ + 1) * 128 * d, h * 128:(h + 1) * 128]
                                   .rearrange("(i r) e -> i r e", r=d), writes=[r_v])

            load_head(0)
            for h in range(HD):
                (q_t, r_q), (k_t, r_k), vds, (G_t, r_G) = qk[h % 2]
                if h + 1 < HD:
                    load_head(h + 1)
                units = [(di, d, r, n0) for di, (window, d) in enumerate(c.DIL)
                         for r in range(d) for n0 in range(0, S // (128 * d), 2)]
                NP = 3

                def sl(d, r, m):
                    a_ = m * 128 * d + r
                    return slice(a_, a_ + 127 * d + 1, d)

                def S_phase(u):
                    di, d, r, n0 = units[u]
                    S_t, r_S = pS[u % NP]
                    w32, r_w32 = W32[u % NP]
                    wb, r_wb = Wb[u % NP]
                    subs = []
                    for qi in range(2):
                        nq = n0 + qi
                        for part in range(2):
                            subs.append((qi, part, max(nq - part, 0), nq))
                    for si, (qi, part, m, nq) in enumerate(subs):
                        Sx.mm(S_t[:, 2 * qi + part, :], k_t[:, sl(d, r, m)], q_t[:, sl(d, r, nq)], True, True,
                              reads=[r_q, r_k], writes=[r_S], signal=(si == len(subs) - 1))
                    Sx.act(w32[:], S_t[:], AF.Exp, reads=[r_S], writes=[r_w32])
                    Sx.tt("dve", wb[:].rearrange("p (a b) n -> p a (b n)", a=2),
                          w32[:].rearrange("p (a b) n -> p a (b n)", a=2),
                          G_t[:, di, :].unsqueeze(1).broadcast_to([128, 2, 256]), ALU.mult,
                          reads=[r_w32, r_G], writes=[r_wb])

                def N_phase(u):
                    di, d, r, n0 = units[u]
                    v_t, r_v = vds[di]
                    N_t, r_N = pN[u % NP]
                    wb, r_wb = Wb[u % NP]
                    nmm = []
                    for which in range(2):
                        for qi in range(2):
                            nq = n0 + qi
                            parts = [pp for pp in range(2) if nq - pp >= 0]
                            for pi_, part in enumerate(parts):
                                nmm.append((which, qi, part, nq - part, pi_ == 0, pi_ == len(parts) - 1))
                    for ni, (which, qi, part, m, first, last) in enumerate(nmm):
                        lhsT = v_t[:, m * d + r, :] if which == 0 else k.ones[:]
                        Sx.mm(N_t[:, 2 * which + qi, :], lhsT, wb[:, 2 * qi + part, :], first, last,
                              reads=[r_v, r_wb, k.r_ones], writes=[r_N], signal=(ni == len(nmm) - 1))
                    a0 = n0 * 128 * d + r
                    acc_view = acc[:, :, a0:a0 + 255 * d + 1:d]
                    n_view = N_t[:].rearrange("p (w q) n -> p w (q n)", w=2)
                    if di == 0:
                        Sx.copy("act", acc_view, n_view, reads=[r_N], writes=[r_acc])
                    else:
                        Sx.tt("dve", acc_view, acc_view, n_view, ALU.add, reads=[r_N, r_acc], writes=[r_acc])

                nu = len(units)
                for step in range(nu + NP - 1):
                    if step < nu:
                        S_phase(step)
                    if step - (NP - 1) >= 0:
                        N_phase(step - (NP - 1))
                for Q in range(c.NT):
                    q0 = Q * T
                    Sx.act(rden[:], acc[:, 1, q0:q0 + T], AF.Ln, reads=[r_acc], writes=[r_rden])
                    Sx.act(rden[:], rden[:], AF.Exp, reads=[r_rden], writes=[r_rden], scale=-1.0)
                    Sx.tt("dve", osb[:], acc[:, 0, q0:q0 + T], rden[:], ALU.mult, reads=[r_acc, r_rden], writes=[r_osb])
                    on_t, r_on = on[oni % 2]
                    oni += 1
                    head_epilogue(Sx, k, l, HS + h, osb[:], r_osb, q0, T, sqh, r_sqh, psn, r_psn, rstd, r_rstd, on_t, r_on)
            Sx.emit()

    def stage_dense(l):
        st, Sx, sb, ps = new_stage(f"c{l}")
        with st:
            k = load_consts(Sx, sb)
            xsrc = xT_in if l == 0 else xr
            xdst = yT_out if l == DEPTH - 1 else xr
            x_t, _ = sb("xt", [128, KC, T], F32)
            yT, _ = sb("yT", [128, KC, T], F32)
            hT, _ = sb("hT", [128, KC, T], BF16)
            actb, _ = sb("actb", [128, FC, T], BF16)
            r_xc = [Res(f"x{i}") for i in range(KC)]
            r_yc = [Res(f"y{i}") for i in range(KC)]
            r_hc = [Res(f"h{i}") for i in range(KC)]
            r_ac = [Res(f"a{i}") for i in range(FC)]
            sqg = [sb(f"sqg{i}", [128, 4, T], BF16) for i in range(2)]
            rstd, r_rstd = sb("rstd", [128, T], F32)
            ptmp, r_ptmp = sb("ptmp", [128, T], F32)
            WK = max(KC, c.KG)
            wsl = [sb(f"w{i}", [128, WK, 512], BF16) for i in range(3)]
            wpp = [sb(f"wpp{i}", [128, c.PC, 512], BF16) for i in range(2)]
            sg = [sb(f"sg{i}", [128, T], F32) for i in range(2)]
            gtmp = [sb(f"gtmp{i}", [128, T], F32) for i in range(2)]
            pb, r_pb = sb("pb", [128, c.PC, T], BF16)
            psn, r_psn = ps("psn", [128, T])
            banks = [ps(f"pb{i}", [128, 512]) for i in range(7)]
            oT_t = actb[:, 0:NH, :]
            bi = [0]
            ei = [0]
            sqi = [0]
            seq = []
            for tt in range(c.NT):
                seq += [(Wb_out[l, b], NH) for b in range(c.NBO)]
                seq += [(Wb_gu[l, b], KC) for b in range(c.NBG)]
                seq += [(Wb_dn[l, cg, kg], c.KG) for cg in range(c.NCG) for kg in range(c.NKG)]
                seq += [(Wb_pg[l, b], KC) for b in range(c.NBO)]
            ws = WStream(Sx, wsl, seq)
            wps = WStream(Sx, wpp, [(Wb_pp[l, b], c.PC) for tt in range(c.NT) for b in range(c.NBO)])

            def next_bank():
                b = banks[bi[0] % 7]
                bi[0] += 1
                return b

            def evac_eng():
                ei[0] += 1
                return "act" if ei[0] % 2 == 0 else "dve"

            pending = []

            def flush_sq():
                while pending:
                    s_t, r_s, g0, g1 = pending.pop(0)
                    for kc in range(g0, g1):
                        Sx.mm(psn[:], k.ones[:], s_t[:, kc - g0, :], start=(kc == 0), stop=(kc == KC - 1),
                              reads=[k.r_ones, r_s], writes=[r_psn], signal=(kc == g1 - 1))

            def sq_group(src, r_src, g, defer=False):
                g0, g1 = 4 * g, min(KC, 4 * g + 4)
                flush_sq()
                s_t, r_s = sqg[sqi[0] % 2]
                sqi[0] += 1
                Sx.act(s_t[:, 0:g1 - g0, :], src[:, g0:g1, :], AF.Square, reads=r_src[g0:g1], writes=[r_s])
                pending.append((s_t, r_s, g0, g1))
                if not defer:
                    flush_sq()

            def rstd_finish():
                flush_sq()
                Sx.act(rstd[:], psn[:], AF.Ln, reads=[r_psn], writes=[r_rstd], scale=1.0 / D, bias=RMS_EPS)
                Sx.act(rstd[:], rstd[:], AF.Exp, reads=[r_rstd], writes=[r_rstd], scale=-0.5)

            NG4 = (KC + 3) // 4

            def post_norm_residual(which, nxt):
                goff = l * c.NG + which * KC
                goff2 = l * c.NG + nxt * KC
                rstd_finish()
                for kc in range(KC):
                    eng = "dve"
                    scale_chunk(Sx, eng, yT[:, kc, :], yT[:, kc, :], k.gains[:, goff + kc:goff + kc + 1], rstd[:],
                                ptmp[:], r_ptmp, [r_yc[kc], k.r_gains, r_rstd], [r_yc[kc]])
                    Sx.tt(eng, x_t[:, kc, :], x_t[:, kc, :], yT[:, kc, :], ALU.add, reads=[r_xc[kc], r_yc[kc]],
                          writes=[r_xc[kc]])
                for kc in range(KC):
                    Sx.act(hT[:, kc, :], x_t[:, kc, :], AF.Copy, reads=[r_xc[kc], k.r_gains], writes=[r_hc[kc]],
                           scale=k.gains[:, goff2 + kc:goff2 + kc + 1])
                    if kc % 4 == 3 or kc == KC - 1:
                        sq_group(x_t, r_xc, kc // 4, defer=True)
                rstd_finish()

            def load_x_chunk(tt, cc):
                Sx.dma("sp", x_t[:, cc, :], xsrc[cc * 128:(cc + 1) * 128, tt * T:(tt + 1) * T], writes=[r_xc[cc]])

            def load_oT(tt):
                Sx.dma("sp", oT_t, oT[:, tt * T:(tt + 1) * T].rearrange("(kc p) t -> p kc t", p=128),
                       writes=r_ac[0:NH], sres=r_ac[0])

            def load_p(tt):
                Sx.dma("pool", pb[:], pT_in[l][:, tt * T:(tt + 1) * T].rearrange("(kc p) t -> p kc t", p=128),
                       writes=[r_pb])

            load_oT(0)
            for cc in range(KC):
                load_x_chunk(0, cc)
            for tt in range(c.NT):
                t0 = tt * T
                load_p(tt)
                for b in range(c.NBO):
                    w_t, r_w = ws.next()
                    for j in range(4):
                        p_t, r_p = next_bank()
                        for kc in range(NH):
                            Sx.mm(p_t[:], w_t[:, kc, j * 128:(j + 1) * 128], oT_t[:, kc, :], kc == 0, kc == NH - 1,
                                  reads=[r_w, r_ac[kc]], writes=[r_p])
                        Sx.copy(evac_eng(), yT[:, b * 4 + j, :], p_t[:], reads=[r_p], writes=[r_yc[b * 4 + j]])
                    sq_group(yT, r_yc, b, defer=True)
                post_norm_residual(1, 2)
                for b in range(c.NBG):
                    w_t, r_w = ws.next()
                    for j in range(2):
                        f = b * 2 + j
                        pg_t, r_pg = next_bank()
                        pu_t, r_pu = next_bank()
                        for kc in range(KC):
                            Sx.mm(pg_t[:], w_t[:, kc, j * 128:(j + 1) * 128], hT[:, kc, :], kc == 0, kc == KC - 1,
                                  reads=[r_w, r_hc[kc]], writes=[r_pg])
                        for kc in range(KC):
                            Sx.mm(pu_t[:], w_t[:, kc, 256 + j * 128:256 + (j + 1) * 128], hT[:, kc, :], kc == 0,
                                  kc == KC - 1, reads=[r_w, r_hc[kc]], writes=[r_pu])
                        sg_t, r_sg = sg[f % 2]
                        gt_t, r_gt = gtmp[f % 2]
                        Sx.tt("dve", gt_t[:], pg_t[:], rstd[:], ALU.mult, reads=[r_pg, r_rstd], writes=[r_gt])
                        Sx.act(sg_t[:], gt_t[:], AF.Silu, reads=[r_gt], writes=[r_sg])
                        Sx.tt("dve", gt_t[:], pu_t[:], rstd[:], ALU.mult, reads=[r_pu, r_rstd, r_sg], writes=[r_gt])
                        Sx.tt("dve", actb[:, f, :], sg_t[:], gt_t[:], ALU.mult, reads=[r_sg, r_gt], writes=[r_ac[f]])
                for cg in range(c.NCG):
                    ybanks = [next_bank() for _ in range(4)]
                    for kg in range(c.NKG):
                        w_t, r_w = ws.next()
                        for j in range(4):
                            p_t, r_p = ybanks[j]
                            for kk in range(c.KG):
                                first = (kg == 0 and kk == 0)
                                last = (kg == c.NKG - 1 and kk == c.KG - 1)
                                Sx.mm(p_t[:], w_t[:, kk, j * 128:(j + 1) * 128], actb[:, kg * c.KG + kk, :], first, last,
                                      reads=[r_w, r_ac[kg * c.KG + kk]], writes=[r_p], signal=(kk == c.KG - 1))
                    for j in range(4):
                        p_t, r_p = ybanks[j]
                        Sx.copy(evac_eng(), yT[:, cg * 4 + j, :], p_t[:], reads=[r_p], writes=[r_yc[cg * 4 + j]])
                    sq_group(yT, r_yc, cg, defer=True)
                if tt + 1 < c.NT:
                    load_oT(tt + 1)
                post_norm_residual(3, 4)
                for b in range(c.NBO):
                    w_t, r_w = ws.next()
                    wp_t, r_wp = wps.next()
                    for j in range(4):
                        cc = b * 4 + j
                        pg_t, r_pg = next_bank()
                        pu_t, r_pu = next_bank()
                        for kc in range(KC):
                            Sx.mm(pg_t[:], w_t[:, kc, j * 128:(j + 1) * 128], hT[:, kc, :], kc == 0, kc == KC - 1,
                                  reads=[r_w, r_hc[kc]], writes=[r_pg])
                        for kc in range(c.PC):
                            Sx.mm(pu_t[:], wp_t[:, kc, j * 128:(j + 1) * 128], pb[:, kc, :], kc == 0, kc == c.PC - 1,
                                  reads=[r_wp, r_pb], writes=[r_pu])
                        sg_t, r_sg = sg[j % 2]
                        gt_t, r_gt = gtmp[j % 2]
                        Sx.tt("dve", gt_t[:], pg_t[:], rstd[:], ALU.mult, reads=[r_pg, r_rstd], writes=[r_gt])
                        Sx.act(sg_t[:], gt_t[:], AF.Sigmoid, reads=[r_gt], writes=[r_sg])
                        Sx.tt("dve", yT[:, cc, :], sg_t[:], pu_t[:], ALU.mult, reads=[r_sg, r_pu], writes=[r_yc[cc]])
                        Sx.tt("dve", x_t[:, cc, :], x_t[:, cc, :], yT[:, cc, :], ALU.add,
                              reads=[r_xc[cc], r_yc[cc]], writes=[r_xc[cc]])
                        Sx.dma("sp", xdst[cc * 128:(cc + 1) * 128, t0:t0 + T], x_t[:, cc, :], reads=[r_xc[cc]])
                        if tt + 1 < c.NT:
                            load_x_chunk(tt + 1, cc)
            Sx.emit()

    gstack = ExitStack()
    pool = SemPool(nc, gstack)
    if want("cvt"):
        stage_convert([0] if c.overlap_cvt else list(range(DEPTH)))
    if want("bias"):
        stage_bias()
    for l in range(DEPTH):
        if want("qkv"):
            stage_qkv(l)
        if want("sb"):
            stage_sb(l)
        if want("dl"):
            stage_dl(l)
        if want("dense"):
            stage_dense(l)
    gstack.close()
    return nc


def pack_gains(cfg, ln_mix_pre, ln_mix_post, ln_ffn_pre, ln_ffn_post, ln_pli, ln_head):
    cols = []
    for l in range(cfg.DEPTH):
        for g in (ln_mix_pre, ln_mix_post, ln_ffn_pre, ln_ffn_post, ln_pli):
            cols.append(np.asarray(g[l], np.float32).reshape(cfg.KC, 128).T)
        cols.append(np.asarray(ln_head[l], np.float32).reshape(cfg.NH, 128).T)
    return np.ascontiguousarray(np.concatenate(cols, axis=1))


def make_in_maps(cfg, x, p, ln_mix_pre, w_in, ln_head, w_out, ln_mix_post, rel_bias,
                 ln_ffn_pre, w_gate_up, w_down, ln_ffn_post, ln_pli, w_pli_gate, w_pli_proj, n_cores):
    cst, oh = make_consts(cfg)
    gains = pack_gains(cfg, ln_mix_pre, ln_mix_post, ln_ffn_pre, ln_ffn_post, ln_pli, ln_head)
    f = lambda a: np.ascontiguousarray(np.asarray(a, np.float32))
    shared = {
        "w_in": f(w_in), "w_out": f(w_out), "w_gate_up": f(w_gate_up), "w_down": f(w_down),
        "w_pli_gate": f(w_pli_gate), "w_pli_proj": f(w_pli_proj), "gains": gains,
        "rel_bias": f(rel_bias), "cst": cst, "oh": oh,
    }
    x = np.asarray(x, np.float32)
    p = np.asarray(p, np.float32)
    maps = []
    for b in range(n_cores):
        m = dict(shared)
        m["xT"] = np.ascontiguousarray(x[b].T)
        m["pT"] = np.ascontiguousarray(p[:, b].transpose(0, 2, 1))
        maps.append(m)
    return maps


_PROGRAM_CACHE = {}


def kernel(x, p, ln_mix_pre, w_in, ln_head, w_out, ln_mix_post, rel_bias,
           ln_ffn_pre, w_gate_up, w_down, ln_ffn_post, ln_pli, w_pli_gate, w_pli_proj):
    cfg = Cfg()
    n = 8
    in_maps = make_in_maps(cfg, x, p, ln_mix_pre, w_in, ln_head, w_out, ln_mix_post, rel_bias,
                           ln_ffn_pre, w_gate_up, w_down, ln_ffn_post, ln_pli, w_pli_gate, w_pli_proj, n)
    nc = build_program(cfg)
    res = run_bass_kernel_spmd(nc, in_maps, core_ids=list(range(n)))
    out = np.stack([np.asarray(r["yT"], np.float32).T for r in res.results], axis=0)
    return np.ascontiguousarray(out)
```
